# Optimizing a Trainium2 kernel written in Bass

```python
import math
import jax, jax.numpy as jnp
from jax import lax
import numpy as np

D_MODEL = 1024
BATCH = 8
SEQ = 2048
DEPTH = 2
DEC_BATCH = 128
DEC_SEQ = 8
PAST_LEN = 16384
PAGE_SIZE = 128

N_MIXERS = 4
BRANCH = D_MODEL // N_MIXERS
N_HEADS = 4
HEAD_DIM = BRANCH // N_HEADS
MIX_W = N_MIXERS * BRANCH
GLA_KEY_DIM = HEAD_DIM // 2
GLA_KEY_WIDTH = N_HEADS * GLA_KEY_DIM
GLA_GATE_RANK = 16
GLA_TAU = 16.0
RWKV_DECAY_RANK = 32
RWKV_A_RANK = 32
RWKV_V_RANK = 32
N_MEM = 256
X_HEADS = 4
X_HEAD_DIM = 64
X_INNER = X_HEADS * X_HEAD_DIM
CHUNK = 16
ROPE_BASE = 10000.0
RMS_EPS = 1e-6
RET_GN_EPS = 1e-5
RWKV_GN_EPS = 64e-5
LB_FLOOR = 1e-20
RET_W = 4 * BRANCH
HGRN_W = 4 * BRANCH
GLA_W = 2 * GLA_KEY_WIDTH + BRANCH + GLA_GATE_RANK + BRANCH
RWKV_W = 3 * BRANCH + RWKV_DECAY_RANK + RWKV_A_RANK + BRANCH
D_IN = RET_W + HGRN_W + GLA_W + RWKV_W
STATE_NAMES = ('ret', 'hgrn', 'gla', 'rwkv', 'rwkv_shift')

kernel_name = 'hybrid_quad_recurrent_decoder_step'


def rmsnorm(x, g):
    xf = x.astype(jnp.float32)
    y = xf * lax.rsqrt(jnp.mean(xf * xf, axis=-1, keepdims=True) + RMS_EPS)
    return (y * g).astype(x.dtype)


def split_heads(t, d):
    return t.reshape(t.shape[:-1] + (t.shape[-1] // d, d))


def merge_heads(t):
    return t.reshape(t.shape[:-2] + (t.shape[-2] * t.shape[-1],))


def head_group_norm(o, eps):
    of = o.astype(jnp.float32)
    mu = jnp.mean(of, axis=-1, keepdims=True)
    c = of - mu
    return (c * lax.rsqrt(jnp.mean(c * c, axis=-1, keepdims=True) + eps)).astype(o.dtype)


def head_rms(o, g):
    of = o.astype(jnp.float32)
    return (of * lax.rsqrt(jnp.mean(of * of, axis=-1, keepdims=True) + RMS_EPS) * g).astype(o.dtype)


def rotary(x, pos):
    half = x.shape[-1] // 2
    inv = ROPE_BASE ** (-jnp.arange(half, dtype=jnp.float32) / half)
    ang = pos.astype(jnp.float32)[:, None] * inv[None, :]
    cos = jnp.cos(ang)[None, :, None, :]
    sin = jnp.sin(ang)[None, :, None, :]
    x1 = x[..., :half].astype(jnp.float32)
    x2 = x[..., half:].astype(jnp.float32)
    return jnp.concatenate([x1 * cos - x2 * sin, x1 * sin + x2 * cos], axis=-1).astype(x.dtype)


def chunked_gated_linear(q, k, v, log_g, s0):
    B, L, H, _ = q.shape
    dv = v.shape[-1]
    c = math.gcd(L, CHUNK)
    n = L // c

    def to_chunks(t):
        return t.reshape(B, n, c, H, t.shape[-1]).transpose(1, 0, 3, 2, 4)

    causal = jnp.tril(jnp.ones((c, c), dtype=bool))[:, :, None]

    def step(S, inp):
        qi, ki, vi, gi = inp
        qf = qi.astype(jnp.float32)
        kf = ki.astype(jnp.float32)
        vf = vi.astype(jnp.float32)
        b = jnp.cumsum(gi.astype(jnp.float32), axis=2)
        diff = b[:, :, :, None, :] - b[:, :, None, :, :]
        decay = jnp.where(causal, jnp.exp(jnp.minimum(diff, 0.0)), 0.0)
        scores = jnp.sum(qf[:, :, :, None, :] * kf[:, :, None, :, :] * decay, axis=-1)
        o = (jnp.einsum('bhjm,bhmv->bhjv', scores, vf)
             + jnp.einsum('bhjk,bhkv->bhjv', qf * jnp.exp(b), S))
        b_last = b[:, :, -1:, :]
        S_new = (jnp.exp(b_last[:, :, 0, :])[..., None] * S
                 + jnp.einsum('bhmk,bhmv->bhkv', kf * jnp.exp(b_last - b), vf))
        return S_new, o

    S, o = lax.scan(step, s0.astype(jnp.float32),
                    (to_chunks(q), to_chunks(k), to_chunks(v), to_chunks(log_g)))
    o = o.transpose(1, 0, 3, 2, 4).reshape(B, L, H, dv)
    return o.astype(v.dtype), S.astype(s0.dtype)


def rwkv7_scan(r, w, k, v, kk, a, s0):
    def step(S, inp):
        rt, wt, kt, vt, kkt, at = inp
        sa = jnp.einsum('bhvk,bhk->bhv', S, kkt)
        S = (S * wt[:, :, None, :] - sa[..., None] * (kkt * at)[:, :, None, :]
             + vt[..., None] * kt[:, :, None, :])
        return S, jnp.einsum('bhvk,bhk->bhv', S, rt)

    xs = tuple(t.transpose(1, 0, 2, 3) for t in (r, w, k, v, kk, a))
    S, o = lax.scan(step, s0, xs)
    return o.transpose(1, 0, 2, 3), S


def rwkv_branch(p, shift, l, v_first, s0, P):
    B, L, _ = p.shape
    prev = jnp.concatenate([shift[:, None, :].astype(p.dtype), p[:, :-1]], axis=1)
    ps = p + (prev - p) * P['rwkv_mu'][l]
    offs = [BRANCH, BRANCH + RWKV_DECAY_RANK, 2 * BRANCH + RWKV_DECAY_RANK,
            3 * BRANCH + RWKV_DECAY_RANK, 3 * BRANCH + RWKV_DECAY_RANK + RWKV_A_RANK]
    r, wl, k, v, al, g = jnp.split(ps, offs, axis=-1)
    w_log = -jax.nn.softplus(-(P['rwkv_w0'][l] + jnp.tanh(wl) @ P['rwkv_w2'][l]).astype(jnp.float32)) - 0.5
    decay = jnp.exp(-jnp.exp(w_log))
    a = jax.nn.sigmoid((P['rwkv_a0'][l] + al @ P['rwkv_a2'][l]).astype(jnp.float32))
    if l == 0:
        v_first = v
    else:
        vmix = jax.nn.sigmoid(P['rwkv_v0'][l - 1] + (v @ P['rwkv_v1'][l - 1]) @ P['rwkv_v2'][l - 1])
        v = v + (v_first - v) * vmix
    kk = split_heads((k * P['rwkv_k_k'][l]).astype(jnp.float32), HEAD_DIM)
    kk = kk / jnp.maximum(jnp.sqrt(jnp.sum(kk * kk, axis=-1, keepdims=True)), 1e-12)
    k = k.astype(jnp.float32) * (1.0 + (a - 1.0) * P['rwkv_k_a'][l])
    rh = split_heads(r.astype(jnp.float32), HEAD_DIM)
    kh = split_heads(k, HEAD_DIM)
    vh = split_heads(v.astype(jnp.float32), HEAD_DIM)
    o, S = rwkv7_scan(rh, split_heads(decay, HEAD_DIM), kh, vh, kk, split_heads(a, HEAD_DIM),
                      s0.astype(jnp.float32))
    o = (head_group_norm(o, RWKV_GN_EPS) * split_heads(P['rwkv_gn_g'][l], HEAD_DIM)
         + split_heads(P['rwkv_gn_b'][l], HEAD_DIM))
    o = o + jnp.sum(rh * kh * P['rwkv_r_k'][l], axis=-1, keepdims=True) * vh
    o = merge_heads(o).astype(p.dtype) * jax.nn.silu(g)
    return o, S.astype(s0.dtype), p[:, -1], v_first


def mixer(h, l, pos, st, v_first, P):
    B, L, _ = h.shape
    proj = h @ P['w_in'][l]
    p_ret, p_hgrn, p_gla, p_rwkv = jnp.split(
        proj, [RET_W, RET_W + HGRN_W, RET_W + HGRN_W + GLA_W], axis=-1)

    q, k, v, g = jnp.split(p_ret, 4, axis=-1)
    q = rotary(split_heads(q, HEAD_DIM), pos)
    k = rotary(split_heads(k, HEAD_DIM), pos) * (HEAD_DIM ** -0.5)
    log_gamma = jnp.log1p(-jnp.exp2(-5.0 - jnp.arange(N_HEADS, dtype=jnp.float32)))
    lg = jnp.broadcast_to(log_gamma[:, None], (B, L, N_HEADS, 1))
    o, s_ret = chunked_gated_linear(q, k, split_heads(v, HEAD_DIM), lg, st['ret'])
    o_ret = merge_heads(head_group_norm(o, RET_GN_EPS)) * jax.nn.silu(g)

    q, f, i, g = jnp.split(p_hgrn, 4, axis=-1)
    sm = jax.nn.softmax(P['hgrn_lb_logits'].astype(jnp.float32), axis=0)
    lb = (jnp.cumsum(sm, axis=0) - sm[0])[l]
    ff = f.astype(jnp.float32)
    log_f = jnp.logaddexp(jnp.log(jnp.maximum(lb, LB_FLOOR)), jnp.log1p(-lb) + jax.nn.log_sigmoid(ff))
    k_h = (1.0 - lb) * jax.nn.sigmoid(-ff)
    o, s_hgrn = chunked_gated_linear(split_heads(jax.nn.silu(q), HEAD_DIM), split_heads(k_h, HEAD_DIM),
                                     split_heads(i, HEAD_DIM), split_heads(log_f, HEAD_DIM), st['hgrn'])
    o_hgrn = merge_heads(head_rms(o, P['hgrn_norm'][l])) * jax.nn.silu(g)

    q, k, v, gl, g = jnp.split(p_gla, [GLA_KEY_WIDTH, 2 * GLA_KEY_WIDTH, 2 * GLA_KEY_WIDTH + BRANCH,
                                       2 * GLA_KEY_WIDTH + BRANCH + GLA_GATE_RANK], axis=-1)
    log_a = jax.nn.log_sigmoid((gl @ P['gla_w_gate2'][l] + P['gla_b_gate'][l]).astype(jnp.float32)) / GLA_TAU
    o, s_gla = chunked_gated_linear(split_heads(q, GLA_KEY_DIM) * (GLA_KEY_DIM ** -0.5),
                                    split_heads(k, GLA_KEY_DIM), split_heads(v, HEAD_DIM),
                                    split_heads(log_a, GLA_KEY_DIM), st['gla'])
    o_gla = merge_heads(head_rms(o, P['gla_norm'][l])) * jax.nn.silu(g)

    o_rwkv, s_rwkv, new_shift, v_first = rwkv_branch(p_rwkv, st['rwkv_shift'], l, v_first, st['rwkv'], P)

    out = jnp.concatenate([o_ret, o_hgrn, o_gla, o_rwkv], axis=-1) @ P['w_out'][l]
    states = {'ret': s_ret, 'hgrn': s_hgrn, 'gla': s_gla, 'rwkv': s_rwkv,
              'rwkv_shift': new_shift.astype(st['rwkv_shift'].dtype)}
    return out, states, v_first


def cross_attend(hn, wq, wo, mk, mv):
    q = split_heads(hn @ wq, X_HEAD_DIM)
    s = jnp.einsum('blhd,bmhd->bhlm', q, mk).astype(jnp.float32) * (X_HEAD_DIM ** -0.5)
    p = jax.nn.softmax(s, axis=-1).astype(mv.dtype)
    o = jnp.einsum('bhlm,bmhd->blhd', p, mv)
    return merge_heads(o) @ wo


def run_trunk(x, pos0, mem_k, mem_v, init, P):
    L = x.shape[1]
    pos = pos0 + jnp.arange(L, dtype=jnp.int32)
    new = {n: [] for n in STATE_NAMES}
    v_first = None
    for l in range(DEPTH):
        out, st_l, v_first = mixer(rmsnorm(x, P['norm_mix'][l]), l, pos,
                                   {n: init[n][l] for n in STATE_NAMES}, v_first, P)
        x = x + out
        x = x + cross_attend(rmsnorm(x, P['norm_x'][l]), P['wq_x'][l], P['wo_x'][l], mem_k[l], mem_v[l])
        for n in STATE_NAMES:
            new[n].append(st_l[n])
    return rmsnorm(x, P['norm_f']), {n: jnp.stack(new[n], axis=0) for n in STATE_NAMES}


def setup_inputs(seed: int = 0) -> dict:
    key = jax.random.key(seed)
    ks = iter(jax.random.split(key, 48))

    def nrm(shape, scale=1.0):
        return scale * jax.random.normal(next(ks), shape, jnp.float32)

    def unif(shape, lo, hi):
        return jax.random.uniform(next(ks), shape, jnp.float32, lo, hi)

    L1 = DEPTH - 1
    return {
        'x_prompt': nrm((BATCH, SEQ, D_MODEL)),
        'x_sample': nrm((DEC_BATCH, DEC_SEQ, D_MODEL)),
        'mem_prompt': nrm((BATCH, N_MEM, D_MODEL)),
        'state_ret': nrm((DEPTH, DEC_BATCH, N_HEADS, HEAD_DIM, HEAD_DIM), 0.5),
        'state_hgrn': nrm((DEPTH, DEC_BATCH, N_HEADS, HEAD_DIM, HEAD_DIM), 0.5),
        'state_gla': nrm((DEPTH, DEC_BATCH, N_HEADS, GLA_KEY_DIM, HEAD_DIM), 0.5),
        'state_rwkv': nrm((DEPTH, DEC_BATCH, N_HEADS, HEAD_DIM, HEAD_DIM), 0.3),
        'state_rwkv_shift': nrm((DEPTH, DEC_BATCH, RWKV_W)),
        'cache_mem_k': nrm((DEPTH, DEC_BATCH, N_MEM, X_HEADS, X_HEAD_DIM)),
        'cache_mem_v': nrm((DEPTH, DEC_BATCH, N_MEM, X_HEADS, X_HEAD_DIM)),
        'norm_mix': 1.0 + nrm((DEPTH, D_MODEL), 0.02),
        'w_in': nrm((DEPTH, D_MODEL, D_IN), D_MODEL ** -0.5),
        'hgrn_lb_logits': nrm((DEPTH, BRANCH), 0.1),
        'hgrn_norm': 1.0 + nrm((DEPTH, HEAD_DIM), 0.02),
        'gla_w_gate2': nrm((DEPTH, GLA_GATE_RANK, GLA_KEY_WIDTH), GLA_GATE_RANK ** -0.5),
        'gla_b_gate': nrm((DEPTH, GLA_KEY_WIDTH), 0.01),
        'gla_norm': 1.0 + nrm((DEPTH, HEAD_DIM), 0.02),
        'rwkv_mu': unif((DEPTH, RWKV_W), 0.0, 1.0),
        'rwkv_w0': unif((DEPTH, BRANCH), -6.5, -1.5),
        'rwkv_w2': nrm((DEPTH, RWKV_DECAY_RANK, BRANCH), 0.1 * RWKV_DECAY_RANK ** -0.5),
        'rwkv_a0': nrm((DEPTH, BRANCH), 0.01),
        'rwkv_a2': nrm((DEPTH, RWKV_A_RANK, BRANCH), RWKV_A_RANK ** -0.5),
        'rwkv_v0': 1.0 + nrm((L1, BRANCH), 0.01),
        'rwkv_v1': nrm((L1, BRANCH, RWKV_V_RANK), BRANCH ** -0.5),
        'rwkv_v2': nrm((L1, RWKV_V_RANK, BRANCH), RWKV_V_RANK ** -0.5),
        'rwkv_k_k': 0.85 + nrm((DEPTH, BRANCH), 0.02),
        'rwkv_k_a': 1.0 + nrm((DEPTH, BRANCH), 0.02),
        'rwkv_r_k': nrm((DEPTH, N_HEADS, HEAD_DIM), 0.1),
        'rwkv_gn_g': 1.0 + nrm((DEPTH, BRANCH), 0.02),
        'rwkv_gn_b': nrm((DEPTH, BRANCH), 0.01),
        'w_out': nrm((DEPTH, MIX_W, D_MODEL), MIX_W ** -0.5),
        'norm_x': 1.0 + nrm((DEPTH, D_MODEL), 0.02),
        'wq_x': nrm((DEPTH, D_MODEL, X_INNER), D_MODEL ** -0.5),
        'wo_x': nrm((DEPTH, X_INNER, D_MODEL), X_INNER ** -0.5),
        'norm_mem': 1.0 + nrm((DEPTH, D_MODEL), 0.02),
        'wk_x': nrm((DEPTH, D_MODEL, X_INNER), D_MODEL ** -0.5),
        'wv_x': nrm((DEPTH, D_MODEL, X_INNER), D_MODEL ** -0.5),
        'norm_f': 1.0 + nrm((D_MODEL,), 0.02),
    }


def reference(x_prompt, x_sample, mem_prompt, state_ret, state_hgrn, state_gla, state_rwkv,
              state_rwkv_shift, cache_mem_k, cache_mem_v, norm_mix, w_in, hgrn_lb_logits, hgrn_norm,
              gla_w_gate2, gla_b_gate, gla_norm, rwkv_mu, rwkv_w0, rwkv_w2, rwkv_a0, rwkv_a2,
              rwkv_v0, rwkv_v1, rwkv_v2, rwkv_k_k, rwkv_k_a, rwkv_r_k, rwkv_gn_g, rwkv_gn_b,
              w_out, norm_x, wq_x, wo_x, norm_mem, wk_x, wv_x, norm_f):
    P = {'norm_mix': norm_mix, 'w_in': w_in, 'hgrn_lb_logits': hgrn_lb_logits, 'hgrn_norm': hgrn_norm,
         'gla_w_gate2': gla_w_gate2, 'gla_b_gate': gla_b_gate, 'gla_norm': gla_norm,
         'rwkv_mu': rwkv_mu, 'rwkv_w0': rwkv_w0, 'rwkv_w2': rwkv_w2, 'rwkv_a0': rwkv_a0,
         'rwkv_a2': rwkv_a2, 'rwkv_v0': rwkv_v0, 'rwkv_v1': rwkv_v1, 'rwkv_v2': rwkv_v2,
         'rwkv_k_k': rwkv_k_k, 'rwkv_k_a': rwkv_k_a, 'rwkv_r_k': rwkv_r_k,
         'rwkv_gn_g': rwkv_gn_g, 'rwkv_gn_b': rwkv_gn_b, 'w_out': w_out,
         'norm_x': norm_x, 'wq_x': wq_x, 'wo_x': wo_x, 'norm_f': norm_f}

    B = x_prompt.shape[0]
    dt = x_prompt.dtype
    mk_list, mv_list = [], []
    for l in range(DEPTH):
        m = rmsnorm(mem_prompt, norm_mem[l])
        mk_list.append(split_heads(m @ wk_x[l], X_HEAD_DIM))
        mv_list.append(split_heads(m @ wv_x[l], X_HEAD_DIM))
    mem_k_p = jnp.stack(mk_list, axis=0)
    mem_v_p = jnp.stack(mv_list, axis=0)
    zero_init = {
        'ret': jnp.zeros((DEPTH, B, N_HEADS, HEAD_DIM, HEAD_DIM), dt),
        'hgrn': jnp.zeros((DEPTH, B, N_HEADS, HEAD_DIM, HEAD_DIM), dt),
        'gla': jnp.zeros((DEPTH, B, N_HEADS, GLA_KEY_DIM, HEAD_DIM), dt),
        'rwkv': jnp.zeros((DEPTH, B, N_HEADS, HEAD_DIM, HEAD_DIM), dt),
        'rwkv_shift': jnp.zeros((DEPTH, B, RWKV_W), dt),
    }
    y_prompt, sp = run_trunk(x_prompt, 0, mem_k_p, mem_v_p, zero_init, P)

    cached = {'ret': state_ret, 'hgrn': state_hgrn, 'gla': state_gla, 'rwkv': state_rwkv,
              'rwkv_shift': state_rwkv_shift}
    y_sample, ss = run_trunk(x_sample, PAST_LEN, cache_mem_k, cache_mem_v, cached, P)

    return (y_prompt, y_sample, sp['ret'], ss['ret'], sp['hgrn'], ss['hgrn'], sp['gla'], ss['gla'],
            sp['rwkv'], ss['rwkv'], sp['rwkv_shift'], ss['rwkv_shift'], mem_k_p, mem_v_p)
```

```python
import contextlib
import sys
import numpy as np
import concourse.bass as bass
import concourse.mybir as mybir
from concourse.bass_utils import run_bass_kernel_spmd

F32 = mybir.dt.float32
BF16 = mybir.dt.bfloat16
AF = mybir.ActivationFunctionType
ALU = mybir.AluOpType
AX = mybir.AxisListType

COMPUTE = ('pe', 'act', 'dve', 'pool')
import os
NO_SELF_SYNC = tuple(x for x in os.environ.get('NO_SELF_SYNC', '').split(',') if x)
N_DMA_SEMS = 40
EPOCH = 3000

D = 1024
DIN = 3920
NT = 17
R0, H0, G0, W0 = 0, 1024, 2048, 2832
PAST = 16384
RW_GROUPS = [(0, 128), (128, 128), (256, 32), (288, 128), (416, 128), (544, 128), (672, 128), (800, 32),
             (832, 128), (960, 128)]
GR_R0, GR_R1, GR_WL, GR_K0, GR_K1, GR_V0, GR_V1, GR_AL, GR_G0, GR_G1 = range(10)


class Sched:
    def __init__(self, nc):
        self.nc = nc
        self.engs = ('pe', 'act', 'dve', 'pool', 'sp')
        self.prog = {e: [] for e in self.engs}
        self.cnt = {e: 0 for e in COMPUTE}
        self.last_w = {}
        self.readers = {}
        self.waited = {e: {} for e in self.engs}
        self.dma_cum = [0] * N_DMA_SEMS
        self.dma_next = 0
        self.semnames = set()
        self.limit = None
        self.nrec = 0
        self.cap = None
        self.parsel = {}
        self.par = [0]

    def _deps(self, reads, writes):
        deps = []
        for k in reads:
            if k in self.last_w:
                deps.append(self.last_w[k])
            if k.startswith('pb'):
                deps.extend(self.readers.get(k, ()))
        for k in writes:
            if k in self.last_w:
                deps.append(self.last_w[k])
            deps.extend(self.readers.get(k, ()))
        return deps

    def _waits_for(self, eng, deps):
        best = {}
        for (s, v) in deps:
            if eng == 'pe' and s.startswith('pe_'):
                continue
            if NO_SELF_SYNC and s.startswith(eng + '_') and eng in NO_SELF_SYNC:
                continue
            if v > best.get(s, 0):
                best[s] = v
        out = []
        w = self.waited[eng]
        for s, v in best.items():
            if w.get(s, 0) < v:
                w[s] = v
                out.append((s, v))
        return out

    def _commit(self, dep, reads, writes):
        for k in writes:
            self.last_w[k] = dep
            self.readers[k] = []
        for k in reads:
            if k in writes:
                continue
            self.readers.setdefault(k, []).append(dep)

    def _tag(self):
        f = sys._getframe(2)
        tags = []
        while f is not None and len(tags) < 3:
            if f.f_code.co_name not in ('op', 'dma', 'mm', 'tr', 'act', 'tt', 'ts', 'stt', 'cp', 'red', 'recip', 'load_cast', '<lambda>'):
                tags.append('%s:%d' % (f.f_code.co_name, f.f_lineno))
            f = f.f_back
        return '|'.join(tags)

    def _pk(self, keys):
        ps = self.parsel
        return tuple((k + '#%d' % ps[k][0]) if k in ps else k for k in keys)

    def begin_capture(self):
        if not hasattr(self, 'capstack'):
            self.capstack = []
        self.capstack.append(self.cap)
        self.cap = []

    def end_capture(self):
        c = self.cap
        self.cap = self.capstack.pop()
        return c

    def replay(self, items):
        if self.cap is not None:
            self.cap.extend(items)
            return
        for it in items:
            if it[0] == 'op':
                self.op(it[1], it[2], it[3], it[4], _tag=it[5], _raw=True)
            else:
                self.dma(it[1], it[2], it[3], eng=it[4], _raw=True)

    def barrier(self):
        if os.environ.get('NOBAR', '0') == '1':
            return
        for e in self.engs:
            self.final_wait_all(e)

    def op(self, eng, fn, reads=(), writes=(), _tag=None, _raw=False):
        if not _raw:
            reads = self._pk(reads); writes = self._pk(writes)
            _tag = self._tag()
            if self.cap is not None:
                self.cap.append(('op', eng, fn, reads, writes, _tag))
                return
        self.nrec += 1
        if self.limit is not None and self.nrec > self.limit:
            return
        reads = tuple(reads); writes = tuple(writes)
        tag = _tag
        fn0 = fn
        fn = lambda e: fn0(e).annotate(tag)
        waits = self._waits_for(eng, self._deps(reads, writes))
        c = self.cnt[eng]
        self.cnt[eng] += 1
        sname = '%s_%d' % (eng, c // EPOCH)
        self.semnames.add(sname)
        dep = (sname, c % EPOCH + 1)
        self.prog[eng].append((waits, fn, (sname, 1)))
        self._commit(dep, reads, writes)

    def dma(self, fns, reads=(), writes=(), eng='sp', _raw=False):
        if not _raw:
            reads = self._pk(reads); writes = self._pk(writes)
            if self.cap is not None:
                self.cap.append(('dma', fns, reads, writes, eng))
                return
        self.nrec += 1
        if self.limit is not None and self.nrec > self.limit:
            return
        reads = tuple(reads); writes = tuple(writes)
        if not isinstance(fns, (list, tuple)):
            fns = [fns]
        si = self.dma_next
        self.dma_next = (self.dma_next + 1) % N_DMA_SEMS
        sname = 'dma%d' % si
        self.semnames.add(sname)
        deps = self._deps(reads, writes)
        if self.dma_cum[si] > 0:
            deps.append((sname, self.dma_cum[si]))
        waits = self._waits_for(eng, deps)
        first = True
        for fn in fns:
            self.dma_cum[si] += 16
            self.prog[eng].append((waits if first else [], fn, (sname, 16)))
            first = False
        dep = (sname, self.dma_cum[si])
        self._commit(dep, reads, writes)

    def final_wait_all(self, eng='sp'):
        deps = list(self.last_w.values())
        for lst in self.readers.values():
            deps.extend(lst)
        waits = self._waits_for(eng, deps)
        self.prog[eng].append((waits, None, None))

    def emit(self):
        nc = self.nc
        sems = {}
        with contextlib.ExitStack() as st:
            for n in sorted(self.semnames):
                sems[n] = st.enter_context(nc.semaphore('s_' + n))
            block = st.enter_context(nc.Block())

            def run(engname):
                def body(engobj):
                    for waits, fn, inc in self.prog[engname]:
                        for s, v in waits:
                            engobj.wait_ge(sems[s], v)
                        if fn is None:
                            continue
                        ins = fn(engobj)
                        ins.then_inc(sems[inc[0]], inc[1])
                return body

            block.tensor(run('pe'))
            block.scalar(run('act'))
            block.vector(run('dve'))
            block.gpsimd(run('pool'))
            block.sync(run('sp'))


CST = {}


def _cst_layout():
    off = 0
    lay = {}
    for name, w in [('ident', 128), ('UiP', 128), ('UiH', 128), ('UiS', 128), ('UsP', 128), ('UsS', 128),
                    ('UsTP', 128), ('UsTS', 128), ('cos', NT * 32), ('sin', NT * 32), ('gqP', 8), ('gqS', 8),
                    ('geP', 2), ('geS', 2), ('bindH', 4), ('bindS', 16), ('headind', 2), ('bones', 128)]:
        lay[name] = (off, w)
        off += w
    return lay, off


CL, NCST = _cst_layout()


def make_consts():
    c = np.zeros((128, NCST), np.float32)

    def put(name, arr):
        o, w = CL[name]
        c[:, o:o + w] = np.asarray(arr, np.float32).reshape(128, w)

    j = np.arange(128)[:, None]
    i = np.arange(128)[None, :]
    put('ident', (j == i))
    for nm, bl in [('P', 128), ('H', 32), ('S', 8)]:
        same = (j // bl) == (i // bl)
        put('Ui' + nm, same & (j <= i))
        if nm != 'H':
            put('Us' + nm, same & (j < i))
            put('UsT' + nm, (same & (j < i)).T)
    half = 32
    inv = 10000.0 ** (-np.arange(half, dtype=np.float32) / half)
    cos = np.zeros((128, NT, 32), np.float32)
    sin = np.zeros((128, NT, 32), np.float32)
    for t in range(NT):
        if t < 16:
            pos = (t * 128 + np.arange(128)).astype(np.float32)
        else:
            pos = (PAST + np.arange(128) % 8).astype(np.float32)
        ang = pos[:, None] * inv[None, :]
        cos[:, t] = np.cos(ang)
        sin[:, t] = np.sin(ang)
    put('cos', cos)
    put('sin', sin)
    gam = 1.0 - 2.0 ** (-5.0 - np.arange(4, dtype=np.float64))
    for nm, bl in [('P', 128), ('S', 8)]:
        pib = (np.arange(128) % bl)[:, None].astype(np.float64)
        gq = np.concatenate([gam[None, :] ** (pib + 1), (gam[None, :] ** (-(pib + 1))) * (64 ** -0.5)], axis=1)
        put('gq' + nm, gq)
        ge = np.zeros((128, 2))
        for p in range(128):
            for hp in range(2):
                ge[p, hp] = gam[2 * hp + p // 64] ** bl
        put('ge' + nm, ge)
    put('bindH', (np.arange(128)[:, None] // 32) == np.arange(4)[None, :])
    put('bindS', (np.arange(128)[:, None] // 8) == np.arange(16)[None, :])
    put('headind', (np.arange(128)[:, None] // 64) == np.arange(2)[None, :])
    put('bones', (j // 64) == (i // 64))
    bH = ((np.arange(128)[None, :] // 32) == np.arange(4)[:, None]).astype(np.float32).reshape(-1)
    bS = ((np.arange(128)[None, :] // 8) == np.arange(16)[:, None]).astype(np.float32).reshape(-1)
    return c, bH, bS


def build_nc(TILES=tuple(range(NT)), NLAYERS=2, LIMIT=None, MIXSEL=None):
    LASTP = max([t for t in TILES if t < 16] + [-1])
    nc = bass.Bass("TRN2", target_bir_lowering=False)
    S = Sched(nc)
    S.limit = LIMIT

    def din(name, shape):
        return nc.dram_tensor(name, list(shape), F32, kind="ExternalInput").ap()

    def dout(name, shape):
        return nc.dram_tensor(name, list(shape), F32, kind="ExternalOutput").ap()

    x_p = din("x_p", [2048, D]); x_s = din("x_s", [128, D]); mem_p = din("mem_p", [256, D])
    st_ret = din("st_ret", [2, 16, 4, 64, 64]); st_hgrn = din("st_hgrn", [2, 16, 4, 64, 64])
    st_gla = din("st_gla", [2, 16, 4, 32, 64]); st_rwkv = din("st_rwkv", [2, 16, 4, 64, 64])
    st_shift = din("st_shift", [2, 16, 1088])
    cmk = din("cmk", [2, 16, 256, 256]); cmv = din("cmv", [2, 16, 256, 256])
    norm_mix = din("norm_mix", [2, D]); w_in = din("w_in", [2, D, DIN])
    lb_logits = din("hgrn_lb_logits", [2, 256]); hgrn_norm = din("hgrn_norm", [2, 64])
    gla_wg2 = din("gla_w_gate2", [2, 16, 128]); gla_bg = din("gla_b_gate", [2, 128]); gla_norm = din("gla_norm", [2, 64])
    rw_mu = din("rwkv_mu", [2, 1088]); rw_w0 = din("rwkv_w0", [2, 256]); rw_w2 = din("rwkv_w2", [2, 32, 256])
    rw_a0 = din("rwkv_a0", [2, 256]); rw_a2 = din("rwkv_a2", [2, 32, 256])
    rw_v0 = din("rwkv_v0", [1, 256]); rw_v1 = din("rwkv_v1", [1, 256, 32]); rw_v2 = din("rwkv_v2", [1, 32, 256])
    rw_kk = din("rwkv_k_k", [2, 256]); rw_ka = din("rwkv_k_a", [2, 256]); rw_rk = din("rwkv_r_k", [2, 256])
    rw_gg = din("rwkv_gn_g", [2, 256]); rw_gb = din("rwkv_gn_b", [2, 256])
    w_out = din("w_out", [2, D, D]); norm_x = din("norm_x", [2, D]); wq_x = din("wq_x", [2, D, 256])
    wo_x = din("wo_x", [2, 256, D]); norm_mem = din("norm_mem", [2, D]); wk_x = din("wk_x", [2, D, 256])
    wv_x = din("wv_x", [2, D, 256]); norm_f = din("norm_f", [D])
    cst_d = din("cst", [128, NCST]); bindH_d = din("bindH_row", [4 * 128]); bindS_d = din("bindS_row", [16 * 128])

    y_p = dout("y_p", [2048, D]); y_s = dout("y_s", [128, D])
    o_ret_p = dout("ret_p", [2, 4, 64, 64]); o_ret_s = dout("ret_s", [2, 16, 4, 64, 64])
    o_hgrn_p = dout("hgrn_p", [2, 4, 64, 64]); o_hgrn_s = dout("hgrn_s", [2, 16, 4, 64, 64])
    o_gla_p = dout("gla_p", [2, 4, 32, 64]); o_gla_s = dout("gla_s", [2, 16, 4, 32, 64])
    o_rwkv_p = dout("rwkv_p", [2, 4, 64, 64]); o_rwkv_s = dout("rwkv_s", [2, 16, 4, 64, 64])
    o_shift_p = dout("shift_p", [2, 1088]); o_shift_s = dout("shift_s", [2, 16, 1088])
    o_mk = dout("mk_p", [2, 256, 256]); o_mv = dout("mv_p", [2, 256, 256])
    xs_d = nc.dram_tensor("xs_scratch", [NT * 128, D], F32, kind="Internal").ap()
    hts_d = nc.dram_tensor("hts_scratch", [NT * 128, D], BF16, kind="Internal").ap()
    xacc_d = nc.dram_tensor("xacc_scratch", [NT * 128, D], F32, kind="Internal").ap()
    vfs_d = nc.dram_tensor("vfs_scratch", [NT * 128, 256], F32, kind="Internal").ap()

    def sb(name, shape, dt=F32):
        return nc.alloc_sbuf_tensor("sb_" + name, list(shape), dt)

    win = sb("win", [128, 8, 1088], BF16)
    WB = [0]
    wout = sb("wout", [128, 8, D], BF16)
    wq = sb("wq", [128, 8, 256], BF16)
    wo = sb("wo", [128, 2, D], BF16)
    xt = sb("xt", [128, 2, D])
    xa = sb("xa", [128, 1, D])
    cst = sb("cst", [128, NCST])
    identb = sb("identb", [128, 128], BF16)
    mask4 = {T: sb("mask4" + T, [128, 4, 128], BF16) for T in 'PHS'}
    mAB = {T: sb("mAB" + T, [128, 256], BF16) for T in 'PS'}
    mAK = {T: sb("mAK" + T, [128, 256], BF16) for T in 'PS'}
    mABt = {T: sb("mABt" + T, [128, 128], BF16) for T in 'PS'}
    bindFM = {'H': sb("bindFMH", [128, 4, 128], BF16), 'S': sb("bindFMS", [128, 16, 128], BF16)}
    headindb = sb("headindb", [128, 2], BF16)
    bonesb = sb("bonesb", [128, 128], BF16)
    nrow = sb("nrow", [128, 2, D])
    NPB = 64 + 64 + 128 + 256 * 6
    prmb = sb("prmb", [128, NPB])
    PB = {}
    o = 0
    for nm, w in [('hnorm', 64), ('gnorm', 64), ('glab', 128), ('lb', 256), ('oml', 256), ('w0', 256), ('v0', 256),
                  ('gng', 256), ('gnb', 256)]:
        PB[nm] = (o, w); o += w
    prmp = sb("prmp", [128, 40])
    PP = {'na0': 30, 'lb': 0, 'oml': 2, 'noml': 4, 'a0': 6, 'kk': 8, 'ka': 10, 'omka': 12, 'rk': 14, 'mu': 16, 'l0': 26, 'l1': 28}
    wg2 = sb("wg2", [16, 128], BF16); w2 = sb("w2", [32, 256], BF16); a2 = sb("a2", [32, 256], BF16)
    v1 = sb("v1", [128, 2, 32], BF16); v2 = sb("v2", [32, 256], BF16)
    hnb = sb("hnb", [128, D], BF16)
    hT = sb("hT", [128, 2, 8, 128], BF16)
    HI = [0]
    st4 = sb("st4", [128, 8]); st4_main = st4; st4b = sb("st4b", [128, 8])
    qk32 = sb("qk32", [128, 512]); rt1 = sb("rt1", [128, 8, 32]); rt2 = sb("rt2", [128, 8, 32])
    qkTM = sb("qkTM", [128, 512], BF16)
    qT = sb("qT", [128, 2, 128], BF16); kT = sb("kT", [128, 2, 128], BF16)
    kTM = sb("kTM", [128, 256], BF16); bTM = sb("bTM", [128, 256], BF16)
    vb = sb("vb", [128, 256], BF16)
    gsil = sb("gsil", [128, 256])
    sTm = sb("sTm", [128, 4, 128], BF16)
    Qblk = sb("Qblk", [128, 1, 16, 128], BF16)
    SAblk = Qblk[:].rearrange("p a b c -> p (a b c)").rearrange("p (h s d) -> p h s d", h=2, s=16)
    Vblk = sb("Vblk", [128, 2, 16, 64], BF16)
    of32 = sb("of32", [128, 256]); osq = sb("osq", [128, 256]); ofin = sb("ofin", [128, D], BF16)
    oT = sb("oT", [128, 8, 128], BF16)
    ffm = sb("ffm", [128, 4, 128])
    ffm2 = sb("ffm2", [128, 4, 128])
    lfTM = sb("lfTM", [128, 256])
    Eb = sb("Eb", [128, 2, 128]); Einv = sb("Einv", [128, 2, 128]); Ecx = sb("Ecx", [128, 2, 128])
    Eend = sb("Eend", [128, 2, 16])
    glT = sb("glT", [32, 128], BF16); alT = sb("alT", [32, 128], BF16)
    _SfP = sb("SfP", [128, 2, 5, 64]); _SbP = sb("SbP", [128, 2, 4, 128], BF16); SbP_ = _SbP
    SfP = {m: _SfP for m in ('ret', 'hgrn', 'gla', 'rwkv')}
    SbP = {m: _SbP for m in ('ret', 'hgrn', 'gla', 'rwkv')}
    SfS = sb("SfS", [128, 2, 16, 64]); SbS = sb("SbS", [128, 2, 16, 128], BF16); SnS = SfS
    qTm = sb("qTm", [128, 4, 128], BF16); arTm = sb("arTm", [128, 2, 2, 2, 128], BF16); qxTm = sb("qxTm", [128, 2, 2, 128], BF16)
    stmp = sb("stmp", [128, 64])
    Sbdn = sb("Sbdn", [128, 8, 128]); stg = sb("stg", [128, 8, 64]); IIf = sb("IIf", [128, 64])
    pbuf = [sb("pbuf%d" % i, [128, 10, 129]) for i in range(2)]
    psf = sb("psf", [128, 10, 128])
    wstage = psf[:].rearrange("p a b -> p (a b)")
    shT = sb("shT", [128, 10, 16])
    aT = sb("aT", [128, 2, 128]); kkT = sb("kkT", [128, 2, 128]); k2T = sb("k2T", [128, 2, 128])
    arT = sb("arT", [128, 2, 2, 128], BF16)
    btT = sb("btT", [128, 2, 128], BF16); ktT = sb("ktT", [128, 2, 128], BF16)
    rkb = sb("rkb", [128, 2, 128], BF16)
    Aab = sb("Aab", [128, 4, 256], BF16); Aak = sb("Aak", [128, 4, 256], BF16)
    Mi = [sb("Mi%d" % i, [128, 4, 128], BF16) for i in range(2)]
    MiT = [sb("MiT%d" % i, [128, 4, 128], BF16) for i in range(2)]
    Pm = [sb("Pm%d" % i, [128, 4, 128], BF16) for i in range(2)]
    XT = sb("XT", [128, 256], BF16); SAT = sb("SAT", [128, 256], BF16)
    vTM32 = sb("vTM32", [128, 256]); vmix = sb("vmix", [128, 256]); l1T = sb("l1T", [32, 128], BF16)
    bsum = sb("bsum", [128, 4]); lwn = sb("lwn", [128, 256]); twl = sb("twl", [32, 128], BF16)
    vTb = sb("vTb", [128, 2, 128], BF16)
    mkT = sb("mkT", [128, 2, 2, 256], BF16)
    mvb = sb("mvb", [128, 2, 2, 256], BF16)
    memb = sb("memb", [128, 2, 256], BF16)
    mkTs = sb("mkTs", [128, 2, 256], BF16)
    mvs = sb("mvs", [128, 2, 256], BF16)
    pT = sb("pT", [128, 2, 4, 128], BF16)
    oxT = sb("oxT", [128, 2, 128], BF16)
    rsb = sb("rsb", [128, 4, 128])
    mstage = of32
    onesb = sb("onesb", [128, 128], BF16)
    class ParBuf:
        def __init__(self, bufs, sel=None):
            self.bufs = bufs
            self.sel = S.par if sel is None else sel

        def __getitem__(self, idx):
            return self.bufs[self.sel[0]][idx]
    ovl_bf = xt[:].rearrange("p a b -> p (a b)").bitcast(BF16)
    ovl_f = xt[:].rearrange("p a b -> p (a b)")
    _o = [0]

    def carve_bf(shape):
        n = int(np.prod(shape[1:]))
        v = ovl_bf[:, _o[0]:_o[0] + n]
        _o[0] += n
        if len(shape) == 3:
            v = v.rearrange("p (a b) -> p a b", b=shape[2])
        elif len(shape) == 4:
            v = v.rearrange("p (a b c) -> p a b c", b=shape[2], c=shape[3])
        return v

    def carve_f(shape):
        assert _o[0] % 2 == 0
        n = int(np.prod(shape[1:]))
        v = ovl_f[:, _o[0] // 2:_o[0] // 2 + n]
        _o[0] += 2 * n
        if len(shape) == 3:
            v = v.rearrange("p (a b) -> p a b", b=shape[2])
        return v
    pbo = pbuf[0][:].rearrange("p a b -> p (a b)")
    _o2 = [0]

    def carve2(shape, dt):
        n = int(np.prod(shape[1:]))
        if dt == BF16:
            v = pbo.bitcast(BF16)[:, _o2[0]:_o2[0] + n]; _o2[0] += n
        else:
            assert _o2[0] % 2 == 0
            v = pbo[:, _o2[0] // 2:_o2[0] // 2 + n]; _o2[0] += 2 * n
        if len(shape) == 3:
            v = v.rearrange("p (a b) -> p a b", b=shape[2])
        return v
    qT_c = carve2([128, 2, 128], BF16); kT_c = carve2([128, 2, 128], BF16); kTM_c = carve2([128, 256], BF16); vb_c = carve2([128, 256], BF16); gsil_c = carve2([128, 256], F32)
    qT = ParBuf([qT[:], carve_bf([128, 2, 128])]); kT = ParBuf([kT[:], carve_bf([128, 2, 128])])
    kTM = ParBuf([kTM[:], carve_bf([128, 256])]); bTM = ParBuf([bTM[:], carve_bf([128, 256])])
    vb = ParBuf([vb[:], carve_bf([128, 256])])
    arT = ParBuf([arT[:], carve_bf([128, 2, 2, 128])]); btT = ParBuf([btT[:], carve_bf([128, 2, 128])]); ktT = ParBuf([ktT[:], carve_bf([128, 2, 128])])
    PO = [0]; PG = [0]; PX = [0]
    pb1f = pbuf[1][:].rearrange("p a b -> p (a b)")
    pT = ParBuf([pT[:], pb1f.bitcast(BF16)[:, 0:1024].rearrange("p (a b c) -> p a b c", a=2, b=4)], PX)
    rsb = ParBuf([rsb[:], pb1f[:, 512:1024].rearrange("p (a b) -> p a b", b=128)], PX)
    sfo = SfS[:].rearrange("p a b c -> p (a b c)")
    gsil = ParBuf([gsil[:], sfo[:, 0:256], sfo[:, 256:512]], PG); vTM32 = ParBuf([vTM32[:], carve_f([128, 256])])
    of32 = ParBuf([of32[:], sfo[:, 512:768]], PO)
    bon = ParBuf([sfo[:, 768:1024], sfo[:, 1024:1280]], PO)
    PIPED = [False]
    Eend = ParBuf([Eend[:], carve_f([128, 2, 16])]); bsum = ParBuf([bsum[:], carve_f([128, 4])])
    qT.bufs.append(qT_c); kT.bufs.append(kT_c); kTM.bufs.append(kTM_c); vb.bufs.append(vb_c)
    S.parsel = {k: S.par for k in ('qT', 'kT', 'kTM', 'bTM', 'vb', 'arT', 'btT', 'ktT', 'vTM32', 'Eend', 'bsum')}
    S.parsel['pT'] = PX; S.parsel['rsb'] = PX
    S.parsel['gsil'] = PG; S.parsel['of32'] = PO; S.parsel['bon'] = PO
    print("sbuf bytes remaining", nc.sbuf_bytes_remaining, "overlay used (bf16 elems)", _o[0])

    pb = [nc.alloc_psum_tensor("pb%d" % i, [128, 512], F32) for i in range(8)]
    pk = ['pb%d' % i for i in range(8)]

    def C(name):
        o_, w_ = CL[name]
        return cst[:, o_:o_ + w_]

    def PBr(name):
        o_, w_ = PB[name]
        return prmb[:, o_:o_ + w_]

    def mm(out, lhsT, rhs, start, stop, reads, writes, tp=None):
        kw = {}
        if tp is not None:
            kw['tile_position'] = tp
        S.op('pe', lambda e: e.matmul(out, lhsT=lhsT, rhs=rhs, start=start, stop=stop, **kw), reads, writes)

    def tr(out, in_, ident, reads, writes):
        S.op('pe', lambda e: e.transpose(out=out, in_=in_, identity=ident), reads, writes)

    def act(out, in_, func, reads, writes, scale=1.0, bias=0.0, accum=None):
        kw = {}
        if accum is not None:
            kw['accum_out'] = accum
        S.op('act', lambda e: e.activation(out=out, in_=in_, func=func, scale=scale, bias=bias, **kw), reads, writes)

    P2D = os.environ.get('POOL2DVE', '0') == '1'

    def tt(eng, out, in0, in1, op, reads, writes):
        if P2D and eng == 'pool':
            eng = 'dve'
        S.op(eng, lambda e: e.tensor_tensor(out=out, in0=in0, in1=in1, op=op), reads, writes)

    def ts(eng, out, in0, s1, s2, op0, op1, reads, writes):
        S.op(eng, lambda e: e.tensor_scalar(out=out, in0=in0, scalar1=s1, scalar2=s2, op0=op0, op1=op1), reads, writes)

    def stt(out, in0, scalar, in1, op0, op1, reads, writes):
        S.op('dve', lambda e: e.scalar_tensor_tensor(out=out, in0=in0, scalar=scalar, in1=in1, op0=op0, op1=op1), reads, writes)

    def sigm(out, in_, reads, wkey, scale=1.0, nbias=0.0, eng2='dve'):
        act(out, in_, AF.Exp, reads, [wkey], scale=-scale, bias=nbias)
        act(out, out, AF.Ln, [wkey], [wkey], bias=1.0)
        act(out, out, AF.Exp, [wkey], [wkey], scale=-1.0)

    def rsq(out, in_, reads, wkey, scale, eps):
        act(out, in_, AF.Ln, reads, [wkey], scale=scale, bias=eps)
        act(out, out, AF.Exp, [wkey], [wkey], scale=-0.5)

    def cp(eng, out, in_, reads, writes):
        if eng == 'act':
            act(out, in_, AF.Copy, reads, writes)
        else:
            S.op(eng, lambda e: e.tensor_copy(out=out, in_=in_), reads, writes)

    def red(out, in_, reads, writes):
        S.op('dve', lambda e: e.tensor_reduce(out=out, in_=in_, axis=AX.X, op=ALU.add), reads, writes)

    def recip(out, in_, reads, writes):
        S.op('dve', lambda e: e.reciprocal(out=out, in_=in_), reads, writes)

    def dma(out, in_, reads, writes, eng='sp'):
        S.dma(lambda e: e.dma_start(out=out, in_=in_, allow_slow_non_contiguous=True), reads, writes, eng=eng)

    pbb = [p[:].bitcast(BF16) for p in pb]

    def load_cast(dst, src, dkey, alt=None):
        shp = dst.shape
        p = shp[0]
        if alt == 1:
            stg_, skey_, eng_ = SfS[:].rearrange("p a b c -> p (a b c)"), 'SfS', 'act'
        elif alt == 0:
            stg_, skey_, eng_ = wstage, 'psf', 'dve'
        else:
            stg_, skey_, eng_ = wstage, 'psf', 'pool'
        if len(shp) == 2:
            stv = stg_[0:p, 0:shp[1]]
        else:
            stv = stg_[0:p, 0:shp[1] * shp[2]].rearrange("p (a b) -> p a b", b=shp[2])
        dma(stv, src, [], [skey_])
        cp(eng_, dst, stv, [skey_], [dkey])

    S.op('pool', lambda e: e.memset(Sbdn[:].rearrange('p a b -> p (a b)'), 0.0), [], ['Sbdn'])
    S.op('pool', lambda e: e.memset(qxTm[:].rearrange('p a b c -> p (a b c)'), 0.0), [], ['qxTm'])
    S.op('pool', lambda e: e.memset(shT[:].rearrange('p a b -> p (a b)'), 0.0), [], ['shT'])
    S.op('pool', lambda e: e.memset(prmp[:], 0.0), [], ['prmp'])
    for i_ in range(2):
        S.op('pool', lambda e, i_=i_: e.memset(pbuf[i_][:].rearrange('p a b -> p (a b)'), 0.0), [], ['pbuf%d' % i_])
    for i_ in range(8):
        S.op('dve', lambda e, i_=i_: e.memset(pb[i_][:], 0.0), [], [pk[i_]])
    dma(cst[:], cst_d, [], ['cst'])
    load_cast(bindFM['H'][:].rearrange("p a b -> p (a b)"), bindH_d.partition_broadcast(128), 'bindFMH')
    for hf in range(2):
        load_cast(bindFM['S'][:].rearrange("p a b -> p (a b)")[:, hf * 1024:(hf + 1) * 1024], bindS_d[hf * 1024:(hf + 1) * 1024].partition_broadcast(128), 'bindFMS')
    for l in range(2):
        for kq in range(2):
            load_cast(wout[:, 4 * kq:4 * kq + 4, (2 * l) * 256:(2 * l + 1) * 256], wk_x[l, kq * 512:(kq + 1) * 512, :].rearrange("(kc p) n -> p kc n", p=128), 'wout')
            load_cast(wout[:, 4 * kq:4 * kq + 4, (2 * l + 1) * 256:(2 * l + 2) * 256], wv_x[l, kq * 512:(kq + 1) * 512, :].rearrange("(kc p) n -> p kc n", p=128), 'wout')
    cp('dve', identb[:], C('ident'), ['cst'], ['identb'])
    for hh_ in range(2):
        cp('dve', IIf[hh_ * 64:(hh_ + 1) * 64, :], C('ident')[hh_ * 64:(hh_ + 1) * 64, hh_ * 64:(hh_ + 1) * 64], ['cst'], ['IIf'])
    cp('dve', headindb[:], C('headind'), ['cst'], ['headindb'])
    cp('dve', bonesb[:], C('bones'), ['cst'], ['bonesb'])
    S.op('dve', lambda e: e.memset(onesb[:], 1.0), [], ['onesb'])
    for T in 'PHS':
        for h in range(4):
            cp('dve', mask4[T][:, h, :], C('Ui' + T), ['cst'], ['mask4' + T])
    for T in 'PS':
        act(mAB[T][:, 0:128], C('Us' + T), AF.Copy, ['cst'], ['mAB' + T], scale=-1.0)
        act(mAB[T][:, 128:256], C('Ui' + T), AF.Copy, ['cst'], ['mAB' + T], scale=-1.0)
        act(mAK[T][:, 0:128], C('Us' + T), AF.Copy, ['cst'], ['mAK' + T])
        act(mAK[T][:, 128:256], C('Ui' + T), AF.Copy, ['cst'], ['mAK' + T])
        act(mABt[T][:], C('UsT' + T), AF.Copy, ['cst'], ['mABt' + T], scale=-1.0)

    def load_layer_weights(l):
        for kq in range(2):
            load_cast(wq[:, 4 * kq:4 * kq + 4, :], wq_x[l, kq * 512:(kq + 1) * 512, :].rearrange("(kc p) n -> p kc n", p=128), 'wq', alt=0)
            load_cast(wo[:, kq, :], wo_x[l, kq * 128:(kq + 1) * 128, :], 'wo', alt=1)

    def load_wout(l):
        for kc in range(8):
            load_cast(wout[:, kc, :], w_out[l, kc * 128:(kc + 1) * 128, :], 'wout', alt=kc % 2)


    def norm_T(src, src_key, grow, grow_key, eps=1e-6, bank=4, st=None):
        st4 = st4_main if st is None else st
        stkey = 'st4' if st is None else 'st4b'
        S.op('dve', lambda e: e.scalar_tensor_tensor(out=hnb[:], in0=src, scalar=1.0, in1=src, op0=ALU.mult, op1=ALU.mult, accum_out=st4[:, 0:1]), [src_key], ['hnb', stkey])
        rsq(st4[:, 2:3], st4[:, 0:1], [stkey], stkey, 1.0 / D, eps)
        stt(hnb[:], src, st4[:, 2:3], grow, ALU.mult, ALU.mult, [src_key, stkey, grow_key, 'hnb'], ['hnb'])
        for c in range(8):
            tr(pbb[bank][:, c * 128:(c + 1) * 128], hnb[:, c * 128:(c + 1) * 128], identb[:], ['hnb', 'identb'], [pk[bank]])
        cp('act', hT[:, HI[0], :, :].rearrange("p a b -> p (a b)"), pbb[bank][:, 0:1024], [pk[bank]], ['hT%d' % HI[0]])

    for l in range(2):
        dma(nrow[:, 0, :], norm_mem[l].partition_broadcast(128), [], ['nrow0'])
        for mc in range(2):
            memt = psf[:, 0:8, :].rearrange("p a b -> p (a b)")
            dma(memt, mem_p[mc * 128:(mc + 1) * 128, :], [], ['psf'])
            norm_T(memt, 'psf', nrow[:, 0, :], 'nrow0')
            for which in range(2):
                col0 = (2 * l + which) * 256
                for kc in range(8):
                    mm(pb[2][:, 0:256], hT[:, HI[0], kc, :], wout[:, kc, col0:col0 + 256], kc == 0, kc == 7, ['hT%d' % HI[0], 'wout'], [pk[2]])
                cp('act', mstage[:], pb[2][:, 0:256], [pk[2]], ['of32'])
                if which == 1:
                    cp('dve', mvb[:, l, mc, :], pb[2][:, 0:256], [pk[2]], ['mvb'])
                dma((o_mk if which == 0 else o_mv)[l, mc * 128:(mc + 1) * 128, :], mstage[:], ['of32'], [])
            for hp in range(2):
                col0 = (2 * l) * 256 + hp * 128
                for kc in range(8):
                    mm(pb[3][:, hp * 128:(hp + 1) * 128], wout[:, kc, col0:col0 + 128], hT[:, HI[0], kc, :], kc == 0, kc == 7, ['hT%d' % HI[0], 'wout'], [pk[3]])
            cp('dve', mkT[:, l, :, mc * 128:(mc + 1) * 128], pb[3][:, 0:256].rearrange("p (a b) -> p a b", a=2), [pk[3]], ['mkT'])
    load_wout(0)

    def load_params(l):
        for nm, src in [('hnorm', hgrn_norm[l]), ('gnorm', gla_norm[l]), ('glab', gla_bg[l]), ('w0', rw_w0[l]),
                        ('v0', rw_v0[0]), ('gng', rw_gg[l]), ('gnb', rw_gb[l])]:
            dma(PBr(nm), src.partition_broadcast(128), [], ['prmb'])
        dma(qk32[:, 0:256], lb_logits[0].partition_broadcast(128), [], ['qk32'])
        dma(qk32[:, 256:512], lb_logits[1].partition_broadcast(128), [], ['qk32'])
        tt('dve', qk32[:, 0:256], qk32[:, 256:512], qk32[:, 0:256], ALU.subtract, ['qk32'], ['qk32'])
        sigm(PBr('lb'), qk32[:, 0:256], ['qk32'], 'prmb')
        ts('dve', PBr('lb'), PBr('lb'), float(l), None, ALU.mult, ALU.bypass, ['prmb'], ['prmb'])
        ts('dve', PBr('oml'), PBr('lb'), -1.0, 1.0, ALU.mult, ALU.add, ['prmb'], ['prmb'])
        def pp(col, src256):
            dma(prmp[:, col:col + 2], src256.rearrange("(g p) -> p g", p=128), [], ['prmp'])
        pp(PP['l0'], lb_logits[0]); pp(PP['l1'], lb_logits[1])
        pp(PP['a0'], rw_a0[l]); pp(PP['kk'], rw_kk[l]); pp(PP['ka'], rw_ka[l]); pp(PP['rk'], rw_rk[l])
        for gi, (c0, w) in enumerate(RW_GROUPS):
            dma(prmp[0:w, PP['mu'] + gi:PP['mu'] + gi + 1], rw_mu[l, c0:c0 + w].rearrange("(p o) -> p o", o=1), [], ['prmp'])
        tt('dve', prmp[:, PP['lb']:PP['lb'] + 2], prmp[:, PP['l1']:PP['l1'] + 2], prmp[:, PP['l0']:PP['l0'] + 2], ALU.subtract, ['prmp'], ['prmp'])
        sigm(prmp[:, PP['lb']:PP['lb'] + 2], prmp[:, PP['lb']:PP['lb'] + 2], ['prmp'], 'prmp')
        ts('dve', prmp[:, PP['lb']:PP['lb'] + 2], prmp[:, PP['lb']:PP['lb'] + 2], float(l), None, ALU.mult, ALU.bypass, ['prmp'], ['prmp'])
        ts('dve', prmp[:, PP['oml']:PP['oml'] + 2], prmp[:, PP['lb']:PP['lb'] + 2], -1.0, 1.0, ALU.mult, ALU.add, ['prmp'], ['prmp'])
        ts('dve', prmp[:, PP['noml']:PP['noml'] + 2], prmp[:, PP['oml']:PP['oml'] + 2], -1.0, None, ALU.mult, ALU.bypass, ['prmp'], ['prmp'])
        ts('dve', prmp[:, PP['omka']:PP['omka'] + 2], prmp[:, PP['ka']:PP['ka'] + 2], -1.0, 1.0, ALU.mult, ALU.add, ['prmp'], ['prmp'])
        ts('dve', prmp[:, PP['na0']:PP['na0'] + 2], prmp[:, PP['a0']:PP['a0'] + 2], -1.0, None, ALU.mult, ALU.bypass, ['prmp'], ['prmp'])
        load_cast(wg2[:], gla_wg2[l], 'wg2')
        load_cast(w2[:], rw_w2[l], 'w2')
        load_cast(a2[:], rw_a2[l], 'a2')
        load_cast(v1[:], rw_v1[0].rearrange("(g p) r -> p g r", p=128), 'v1')
        load_cast(v2[:], rw_v2[0], 'v2')

    def sbd_view(sample, G):
        if sample:
            return SbS[:] if G == 2 else SbS[:].rearrange("p g s w -> p (g s w)").rearrange("p (g s w) -> p g s w", g=1, s=16)
        return SbP_[:] if G == 2 else SbP_[:].rearrange("p g s w -> p (g s w)").rearrange("p (g s w) -> p g s w", g=1, s=4)

    def cast_state(sample, G, dk, g, s0, s1, srcSf, skey, sbkey):
        hpg = 4 // G
        Sbd = sbd_view(sample, G)
        for hh in range(hpg):
            po = hh * dk
            cp('act', Sbd[po:po + dk, g, s0:s1, hh * 64:(hh + 1) * 64], srcSf[po:po + dk, g, s0:s1, :], [skey], [sbkey])

    def lin_core(l, t, m, G, dk, ttype, gstate):
        nblk = {'P': 1, 'H': 4, 'S': 16}[ttype]
        sample = (ttype == 'S')
        hpg = 4 // G
        W = hpg * 64
        qTm_v = qTm[:].rearrange("p (g h) t -> p g h t", g=G)
        Sbd = sbd_view(sample, G)
        for g in range(G):
            for hh in range(hpg):
                po = hh * dk
                cp('act' if hh % 2 == 0 else 'dve', qTm_v[po:po + dk, g, hh, :], qT[po:po + dk, g, :], ['qT'], ['qTm'])
        for h in range(4):
            g, hh = h // hpg, h % hpg
            mm(pb[5][:, h * 128:(h + 1) * 128], kT[:, g, :], qTm_v[:, g, hh, :], True, True, ['kT', 'qTm'], [pk[5]])
        tt('dve', sTm[:].rearrange("p a b -> p (a b)"), pb[5][:, 0:512], mask4[ttype][:].rearrange("p a b -> p (a b)"), ALU.mult, [pk[5], 'mask4' + ttype], ['sTm'])
        if sample:
            Sf, skey, sbkey = SfS, 'SfS', 'SbS'
        else:
            Sf, skey, sbkey = SfP[m], 'SfP', 'SbP'

        def Eap(g, s):
            if m == 'ret':
                return C('ge' + ('S' if sample else 'P'))[:, g:g + 1]
            return Eend[:, g, s:s + 1]
        chunks = [(c0, min(nblk, c0 + 8)) for c0 in range(0, nblk, 8)]
        ubank = [7, 5]
        for g in range(G):
            for pr in range(hpg // 2):
                h0 = g * hpg + 2 * pr
                if nblk > 1:
                    bind = C('bind' + ttype)
                    tt('pool', Vblk[:, :, 0:nblk, :], vb[:, h0 * 64:(h0 + 2) * 64].rearrange("p (h d) -> p h d", d=64).unsqueeze(2).to_broadcast([128, 2, nblk, 64]),
                       bind.unsqueeze(1).unsqueeze(3).to_broadcast([128, 2, nblk, 64]), ALU.mult, ['vb', 'cst'], ['Vblk'])
                for ci, (c0, c1) in enumerate(chunks):
                    ncol = (c1 - c0) * 64
                    for j in range(2):
                        h = h0 + j
                        po = (h % hpg) * dk
                        tp = (0, po) if po == 96 else None
                        rhs = Vblk[:, j, c0:c1, :].rearrange("p a b -> p (a b)") if nblk > 1 else vb[:, h * 64:(h + 1) * 64]
                        bk = ubank[ci]
                        mm(pb[bk][po:po + dk, 0:ncol], kTM[:, g * 128 + po:g * 128 + po + dk], rhs, True, True, ['kTM', 'Vblk' if nblk > 1 else 'vb'], [pk[bk]], tp=tp)
            for ci, (c0, c1) in enumerate(chunks):
                ncol = (c1 - c0) * 64
                bk = ubank[ci]
                if sample:
                    tt('dve', Sf[:, g, c0:c1, :], pb[bk][:, 0:ncol].rearrange("p (a b) -> p a b", b=64), Sf[:, g, c0:c1, :], ALU.add, [pk[bk], skey], [skey])
                    if m == 'ret':
                        ts('dve', Sf[:, g, c0:c1, :], Sf[:, g, c0:c1, :], Eap(g, 0), None, ALU.mult, ALU.bypass, [skey, 'cst'], [skey])
                    else:
                        tt('pool', Sf[:, g, c0:c1, :], Sf[:, g, c0:c1, :], Eend[:, g, c0:c1].unsqueeze(2).to_broadcast([128, c1 - c0, 64]), ALU.mult, [skey, 'Eend'], [skey])
                else:
                    for s in range(nblk):
                        tt('dve', stmp[:], pb[bk][:, s * 64:(s + 1) * 64], Sf[:, g, s, :], ALU.add, [pk[bk], skey], ['stmp'])
                        ts('dve', Sf[:, g, s + 1, :], stmp[:], Eap(g, s), None, ALU.mult, ALU.bypass, ['stmp', 'Eend', 'cst'], [skey])
                    if nblk > 1:
                        cast_state(False, G, dk, g, 1, nblk, Sf, skey, sbkey)
        for g in range(G):
            if nblk > 1:
                tt('pool', Qblk[:, 0, 0:nblk, :], qT[:, g, :].unsqueeze(1).to_broadcast([128, nblk, 128]), bindFM[ttype][:, 0:nblk, :], ALU.mult, ['qT', 'bindFM' + ttype], ['Qblk'])
            for s in range(nblk):
                lhs = Qblk[:, 0, s, :] if nblk > 1 else qT[:, g, :]
                mm(pb[6][:, g * W:(g + 1) * W], lhs, Sbd[:, g, s, 0:W], s == 0, False, ['Qblk' if nblk > 1 else 'qT', sbkey], [pk[6]])
            for hh in range(hpg):
                h = g * hpg + hh
                mm(pb[6][:, h * 64:(h + 1) * 64], sTm[:, h, :], vb[:, h * 64:(h + 1) * 64], False, hh == hpg - 1, ['sTm', 'vb'], [pk[6]])
        if not sample:
            for g in range(G):
                cp('pool', Sf[:, g, 0, :], Sf[:, g, nblk, :], [skey], [skey])
                cast_state(False, G, dk, g, 0, 1, Sf, skey, sbkey)

    def state_dram(m, l, which):
        src = {'ret': (st_ret, o_ret_p, o_ret_s), 'hgrn': (st_hgrn, o_hgrn_p, o_hgrn_s), 'gla': (st_gla, o_gla_p, o_gla_s),
               'rwkv': (st_rwkv, o_rwkv_p, o_rwkv_s)}[m]
        return src[which][l]

    def load_sample_state(m, l):
        d = state_dram(m, l, 0)
        if m == 'gla':
            dma(SfS[:, 0, :, :], d.rearrange("b h k v -> (h k) b v"), [], ['SfS'])
            cast_state(True, 1, 32, 0, 0, 16, SfS, 'SfS', 'SbS')
        else:
            for hp in range(2):
                dma(SfS[:, hp, :, :], d[:, 2 * hp:2 * hp + 2].rearrange("b h k v -> (h k) b v"), [], ['SfS'])
            for g in range(2):
                cast_state(True, 2, 64, g, 0, 16, SfS, 'SfS', 'SbS')

    def store_sample_state(m, l):
        d = state_dram(m, l, 2)
        if m == 'gla':
            dma(d.rearrange("b h k v -> (h k) b v"), SnS[:, 0, :, :], ['SfS'], [])
        else:
            for hp in range(2):
                dma(d[:, 2 * hp:2 * hp + 2].rearrange("b h k v -> (h k) b v"), SnS[:, hp, :, :], ['SfS'], [])

    def store_prompt_state(m, l):
        d = state_dram(m, l, 1)
        if m == 'gla':
            dma(d.rearrange("h k v -> (h k) v"), SfP[m][:, 0, 0, :], ['SfP'], [])
        else:
            for hp in range(2):
                dma(d[2 * hp:2 * hp + 2].rearrange("h k v -> (h k) v"), SfP[m][:, hp, 0, :], ['SfP'], [])

    def zero_prompt_states():
        S.op('dve', lambda e: e.memset(_SfP[:].rearrange("p a b c -> p (a b c)"), 0.0), [], ['SfP'])
        S.op('dve', lambda e: e.memset(_SbP[:].rearrange("p a b c -> p (a b c)"), 0.0), [], ['SbP'])
        S.op('pool', lambda e: e.memset(SbS[:].rearrange("p a b c -> p (a b c)"), 0.0), [], ['SbS'])
        S.op('pool', lambda e: e.memset(qTm[:].rearrange("p a b -> p (a b)"), 0.0), [], ['qTm'])
        S.op('pool', lambda e: e.memset(arTm[:].rearrange("p a b c d -> p (a b c d)"), 0.0), [], ['arTm'])

    def make_gate(gate_ap, gate_keys):
        sigm(gsil[:], gate_ap, gate_keys, 'gsil', eng2='pool')
        tt('dve', gsil[:], gsil[:], gate_ap, ALU.mult, ['gsil'] + list(gate_keys), ['gsil'])

    def o_evac():
        cp('act', of32[:], pb[6][:, 0:256], [pk[6]], ['of32'])

    def post_norm(kind, gate_ap, gate_keys, col0, gain_name=None, eps=1e-6, extra=None):
        o3 = of32[:].rearrange("p (h d) -> p h d", d=64)
        if kind == 'gn':
            red(st4[:, 0:4], o3, ['of32'], ['st4'])
            ts('dve', st4[:, 0:4], st4[:, 0:4], 1.0 / 64, None, ALU.mult, ALU.bypass, ['st4'], ['st4'])
            tt('dve', o3, o3, st4[:, 0:4].unsqueeze(2).to_broadcast([128, 4, 64]), ALU.subtract, ['of32', 'st4'], ['of32'])
        tt('pool', osq[:], of32[:], of32[:], ALU.mult, ['of32'], ['osq'])
        red(st4[:, 4:8], osq[:].rearrange("p (h d) -> p h d", d=64), ['osq'], ['st4'])
        rsq(st4[:, 4:8], st4[:, 4:8], ['st4'], 'st4', 1.0 / 64, eps)
        tt('dve', o3, o3, st4[:, 4:8].unsqueeze(2).to_broadcast([128, 4, 64]), ALU.mult, ['of32', 'st4'], ['of32'])
        if gain_name is not None:
            tt('pool', o3, o3, PBr(gain_name).unsqueeze(1).to_broadcast([128, 4, 64]), ALU.mult, ['of32', 'prmb'], ['of32'])
        if extra is not None:
            extra()
        if gate_ap is not None:
            make_gate(gate_ap, gate_keys)
        tt('dve', ofin[:, col0:col0 + 256], of32[:], gsil[:], ALU.mult, ['of32', 'gsil'], ['ofin'])

    def tile_type(t):
        return 'S' if t == 16 else 'P'

    def proj_TM(col_ranges, bank):
        o_ = 0
        for (c0, w) in col_ranges:
            for kc in range(8):
                mm(pb[bank][:, o_:o_ + w], hT[:, HI[0], kc, :], win[:, kc, c0 - WB[0]:c0 - WB[0] + w], kc == 0, kc == 7, ['hT%d' % HI[0], 'win'], [pk[bank]])
            o_ += w

    def proj_FM(groups, bank):
        for i, (c0, w) in enumerate(groups):
            for kc in range(8):
                mm(pb[bank][0:w, i * 128:(i + 1) * 128], win[:, kc, c0 - WB[0]:c0 - WB[0] + w], hT[:, HI[0], kc, :], kc == 0, kc == 7, ['hT%d' % HI[0], 'win'], [pk[bank]])

    def ret_fe(l, t):
        T = tile_type(t)
        proj_TM([(R0, 512)], 2)
        proj_TM([(R0 + 512, 512)], 3)
        cp('act', qk32[:], pb[2][:, 0:512], [pk[2]], ['qk32'])
        cp('act', vb[:], pb[3][:, 0:256], [pk[3]], ['vb'])
        S.begin_capture()
        x3 = qk32[:].rearrange("p (h d) -> p h d", d=64)
        cosb = C('cos')[:, t * 32:(t + 1) * 32].unsqueeze(1).to_broadcast([128, 8, 32])
        sinb = C('sin')[:, t * 32:(t + 1) * 32].unsqueeze(1).to_broadcast([128, 8, 32])
        tt('dve', rt1[:], x3[:, :, 0:32], cosb, ALU.mult, ['qk32', 'cst'], ['rt1'])
        tt('pool', rt2[:], x3[:, :, 32:64], sinb, ALU.mult, ['qk32', 'cst'], ['rt2'])
        tt('dve', rt1[:], rt1[:], rt2[:], ALU.subtract, ['rt1', 'rt2'], ['rt1'])
        tt('pool', rt2[:], x3[:, :, 0:32], sinb, ALU.mult, ['qk32', 'cst', 'rt2'], ['rt2'])
        tt('dve', x3[:, :, 32:64], x3[:, :, 32:64], cosb, ALU.mult, ['qk32', 'cst'], ['qk32'])
        tt('dve', x3[:, :, 32:64], x3[:, :, 32:64], rt2[:], ALU.add, ['qk32', 'rt2'], ['qk32'])
        cp('pool', x3[:, :, 0:32], rt1[:], ['rt1'], ['qk32'])
        gq = C('gq' + T).unsqueeze(2).to_broadcast([128, 8, 64])
        tt('dve', qkTM[:].rearrange("p (h d) -> p h d", d=64), x3, gq, ALU.mult, ['qk32', 'cst'], ['qkTM'])
        for i in range(4):
            tr(pbb[2][:, i * 128:(i + 1) * 128], qkTM[:, i * 128:(i + 1) * 128], identb[:], ['qkTM', 'identb'], [pk[2]])
        cp('act', qT[:].rearrange("p a b -> p (a b)"), pbb[2][:, 0:256], [pk[2]], ['qT'])
        cp('act', kT[:].rearrange("p a b -> p (a b)"), pbb[2][:, 256:512], [pk[2]], ['kT'])
        cp('pool', kTM[:], qkTM[:, 256:512], ['qkTM'], ['kTM'])
        sA = S.end_capture()
        S.begin_capture()
        make_gate(pb[3][:, 256:512], [pk[3]])
        sC = S.end_capture()
        S.replay(merge2(sA, sC))

    def ret_be(l, t):
        T = tile_type(t)
        if T == 'S':
            load_sample_state('ret', l)
        lin_core(l, t, 'ret', 2, 64, T, None)
        o_evac()

    def ret_be2(l, t):
        T = tile_type(t)
        post_norm('gn', None, [], 0, None, eps=1e-5)
        if T == 'S':
            store_sample_state('ret', l)

    def merge2(a, b):
        out = []
        ia = ib = 0
        na, nb = len(a), len(b)
        while ia < na or ib < nb:
            if ib >= nb or (ia < na and ia * nb <= ib * na):
                out.append(a[ia]); ia += 1
            else:
                out.append(b[ib]); ib += 1
        return out

    def mergeN(lists):
        out = lists[0]
        for x in lists[1:]:
            out = merge2(out, x)
        return out

    def hgrn_fe(l, t):
        T = 'S' if t == 16 else 'H'
        nblk = 16 if t == 16 else 4
        proj_FM([(H0, 128), (H0 + 128, 128), (H0 + 256, 128), (H0 + 384, 128)], 0)
        proj_TM([(H0 + 256, 512)], 2)
        proj_TM([(H0 + 768, 256)], 3)
        S.begin_capture()
        act(lfTM[:], pb[2][:, 0:256], AF.Exp, [pk[2]], ['lfTM'], scale=-1.0)
        tt('dve', qk32[:, 0:256], lfTM[:], PBr('lb'), ALU.mult, ['lfTM', 'prmb'], ['qk32'])
        act(qk32[:, 0:256], qk32[:, 0:256], AF.Ln, ['qk32'], ['qk32'], bias=1.0)
        act(lfTM[:], lfTM[:], AF.Ln, ['lfTM'], ['lfTM'], bias=1.0)
        tt('dve', lfTM[:], qk32[:, 0:256], lfTM[:], ALU.subtract, ['lfTM', 'qk32'], ['lfTM'])
        cp('act', vb[:], pb[2][:, 256:512], [pk[2]], ['vb'])
        for g in range(2):
            mm(pb[1][:, g * 128:(g + 1) * 128], lfTM[:, g * 128:(g + 1) * 128], C('Ui' + T), True, True, ['lfTM', 'cst'], [pk[1]])
        act(Eb[:].rearrange("p a b -> p (a b)"), pb[1][:, 0:256], AF.Exp, [pk[1]], ['Eb'])
        act(Einv[:].rearrange("p a b -> p (a b)"), pb[1][:, 0:256], AF.Exp, [pk[1]], ['Einv'], scale=-1.0)
        blen = 128 // nblk
        cp('pool', Eend[:, :, 0:nblk], Eb[:, :, blen - 1:128:blen], ['Eb'], ['Eend'])
        sA = S.end_capture()
        S.begin_capture()
        ffl = ffm[:].rearrange("p a b -> p (a b)")
        act(ffl, pb[0][:, 0:512], AF.Exp, [pk[0]], ['ffm'], scale=-1.0)
        act(ffm2[:].rearrange("p a b -> p (a b)"), ffl, AF.Ln, ['ffm'], ['ffm2'], bias=1.0)
        act(ffm2[:].rearrange("p a b -> p (a b)"), ffm2[:].rearrange("p a b -> p (a b)"), AF.Exp, ['ffm2'], ['ffm2'], scale=-1.0)
        tt('dve', ffm[:, 0:2, :], ffm2[:, 0:2, :], pb[0][:, 0:256].rearrange("p (a b) -> p a b", b=128), ALU.mult, ['ffm2', pk[0]], ['ffm'])
        tt('pool', ffm[:, 2:4, :], ffm[:, 2:4, :], ffm2[:, 2:4, :], ALU.mult, ['ffm', 'ffm2'], ['ffm'])
        for g in range(2):
            ts('dve', ffm[:, 2 + g, :], ffm[:, 2 + g, :], prmp[:, PP['oml'] + g:PP['oml'] + g + 1], None, ALU.mult, ALU.bypass, ['ffm', 'prmp'], ['ffm'])
        sB = S.end_capture()
        S.begin_capture()
        make_gate(pb[3][:, 0:256], [pk[3]])
        sC = S.end_capture()
        S.replay(mergeN([sA, sB, sC]))
        tt('dve', qT[:], ffm[:, 0:2, :], Eb[:], ALU.mult, ['ffm', 'Eb'], ['qT'])
        tt('pool', kT[:], ffm[:, 2:4, :], Einv[:], ALU.mult, ['ffm', 'Einv'], ['kT'])
        for g in range(2):
            tr(pbb[2][:, g * 128:(g + 1) * 128], kT[:, g, :], identb[:], ['kT', 'identb'], [pk[2]])
        cp('act', kTM[:], pbb[2][:, 0:256], [pk[2]], ['kTM'])

    def hgrn_be(l, t):
        T = 'S' if t == 16 else 'H'
        if T == 'S':
            load_sample_state('hgrn', l)
        lin_core(l, t, 'hgrn', 2, 64, T, None)
        o_evac()

    def hgrn_be2(l, t):
        T = 'S' if t == 16 else 'H'
        post_norm('rms', None, [], 256, 'hnorm')
        if T == 'S':
            store_sample_state('hgrn', l)

    def gla_fe(l, t):
        T = tile_type(t)
        nblk = 16 if T == 'S' else 1
        proj_FM([(G0, 128), (G0 + 128, 128), (G0 + 512, 16)], 0)
        proj_TM([(G0 + 256, 256), (G0 + 528, 256)], 2)
        cp('act', glT[0:16, :], pb[0][0:16, 256:384], [pk[0]], ['glT'])
        cp('act', vb[:], pb[2][:, 0:256], [pk[2]], ['vb'])
        S.begin_capture()
        mm(pb[3][:, 0:128], glT[0:16, :], wg2[:], True, True, ['glT', 'wg2'], [pk[3]])
        tt('dve', lfTM[:, 0:128], pb[3][:, 0:128], PBr('glab'), ALU.add, [pk[3], 'prmb'], ['lfTM'])
        act(lfTM[:, 0:128], lfTM[:, 0:128], AF.Exp, ['lfTM'], ['lfTM'], scale=-1.0)
        act(lfTM[:, 0:128], lfTM[:, 0:128], AF.Ln, ['lfTM'], ['lfTM'], bias=1.0)
        mm(pb[1][:, 0:128], lfTM[:, 0:128], C('Ui' + T), True, True, ['lfTM', 'cst'], [pk[1]])
        act(Eb[:, 0, :], pb[1][:, 0:128], AF.Exp, [pk[1]], ['Eb'], scale=-1.0 / 16)
        act(Einv[:, 0, :], pb[1][:, 0:128], AF.Exp, [pk[1]], ['Einv'], scale=1.0 / 16)
        blen = 128 // nblk
        cp('pool', Eend[:, 0, 0:nblk], Eb[:, 0, blen - 1:128:blen], ['Eb'], ['Eend'])
        stt(qT[:, 0, :], pb[0][:, 0:128], float(32 ** -0.5), Eb[:, 0, :], ALU.mult, ALU.mult, [pk[0], 'Eb'], ['qT'])
        tt('dve', kT[:, 0, :], pb[0][:, 128:256], Einv[:, 0, :], ALU.mult, [pk[0], 'Einv'], ['kT'])
        tr(pbb[3][:, 0:128], kT[:, 0, :], identb[:], ['kT', 'identb'], [pk[3]])
        cp('act', kTM[:, 0:128], pbb[3][:, 0:128], [pk[3]], ['kTM'])
        sA = S.end_capture()
        S.begin_capture()
        make_gate(pb[2][:, 256:512], [pk[2]])
        sC = S.end_capture()
        S.replay(merge2(sA, sC))

    def gla_be(l, t):
        T = tile_type(t)
        if T == 'S':
            load_sample_state('gla', l)
        lin_core(l, t, 'gla', 1, 32, T, None)
        o_evac()

    def gla_be2(l, t):
        T = tile_type(t)
        post_norm('rms', None, [], 512, 'gnorm')
        if T == 'S':
            store_sample_state('gla', l)

    def rwkv_fe(l, t):
        T = tile_type(t)
        sample = (T == 'S')
        nblk = 16 if sample else 1
        blen = 128 // nblk
        nlev = 3 if sample else 7
        pbc = pbuf[t % 2]; pkey = 'pbuf%d' % (t % 2)
        pbn = pbuf[(t + 1) % 2]; pnkey = 'pbuf%d' % ((t + 1) % 2)
        m = 'rwkv'
        for b0 in range(0, 10, 4):
            grp = RW_GROUPS[b0:b0 + 4]
            bank = (b0 // 4) % 2
            proj_FM([(W0 + c0, w) for (c0, w) in grp], bank)
            n = len(grp)
            cp('act', pbc[:, b0:b0 + n, 1:129], pb[bank][:, 0:n * 128].rearrange("p (a b) -> p a b", b=128), [pk[bank]], [pkey])
        if sample:
            fns = []
            for gi, (c0, w) in enumerate(RW_GROUPS):
                fns.append(lambda e, gi=gi, c0=c0, w=w: e.dma_start(out=shT[0:w, gi, :], in_=st_shift[l][:, c0:c0 + w].rearrange("b f -> f b"), allow_slow_non_contiguous=True))
            S.dma(fns, [], ['shT'])
        elif t == 0:
            S.op('pool', lambda e: e.memset(pbc[:, :, 0:1], 0.0), [pkey], [pkey])
        if sample or t == LASTP:
            ncol = 16 if sample else 1
            for (c0, w, bank) in [(0, 512, 2), (512, 512, 3), (1024, 64, 1)]:
                for kc in range(8):
                    lhs = hT[:, HI[0], kc, 7:128:8] if sample else hT[:, HI[0], kc, 127:128]
                    mm(pb[bank][0:ncol, 0:w], lhs, win[:, kc, c0:c0 + w], kc == 0, kc == 7, ['hT%d' % HI[0], 'win'], [pk[bank]])
                cp('act', qk32[0:ncol, 0:w], pb[bank][0:ncol, 0:w], [pk[bank]], ['qk32'])
                if sample:
                    dma(o_shift_s[l][:, c0:c0 + w], qk32[0:16, 0:w], ['qk32'], [])
                else:
                    dma(o_shift_p[l:l + 1, c0:c0 + w], qk32[0:1, 0:w], ['qk32'], [])
        tt('dve', psf[:], pbc[:, :, 0:128], pbc[:, :, 1:129], ALU.subtract, [pkey], ['psf'])
        if sample:
            tt('pool', psf[:, :, 0:128:8], shT[:], pbc[:, :, 1:129:8], ALU.subtract, [pkey, 'shT', 'psf'], ['psf'])
        tt('dve', psf[:], psf[:], prmp[:, PP['mu']:PP['mu'] + 10].unsqueeze(2).to_broadcast([128, 10, 128]), ALU.mult, ['psf', 'prmp'], ['psf'])
        tt('dve', psf[:], psf[:], pbc[:, :, 1:129], ALU.add, ['psf', pkey], ['psf'])
        if not sample and t < LASTP:
            cp('pool', pbn[:, :, 0:1], pbc[:, :, 128:129], [pkey, pnkey], [pnkey])
        S.begin_capture()
        act(ffm2[0:32, 0, :], psf[0:32, GR_WL, :], AF.Exp, ['psf'], ['ffm2'], scale=2.0)
        act(ffm2[0:32, 0, :], ffm2[0:32, 0, :], AF.Ln, ['ffm2'], ['ffm2'], bias=1.0)
        act(ffm2[0:32, 0, :], ffm2[0:32, 0, :], AF.Exp, ['ffm2'], ['ffm2'], scale=-1.0)
        ts('dve', twl[:], ffm2[0:32, 0, :], -2.0, 1.0, ALU.mult, ALU.add, ['ffm2'], ['twl'])
        mm(pb[3][:, 0:256], twl[:], w2[:], True, True, ['twl', 'w2'], [pk[3]])
        tt('dve', lwn[:], pb[3][:, 0:256], PBr('w0'), ALU.add, [pk[3], 'prmb'], ['lwn'])
        act(lwn[:], lwn[:], AF.Exp, ['lwn'], ['lwn'], scale=-1.0)
        act(lwn[:], lwn[:], AF.Ln, ['lwn'], ['lwn'], bias=1.0)
        act(lwn[:], lwn[:], AF.Exp, ['lwn'], ['lwn'], scale=-1.0, bias=-0.5)
        for g in range(2):
            mm(pb[1][:, g * 128:(g + 1) * 128], lwn[:, g * 128:(g + 1) * 128], C('Ui' + T), True, True, ['lwn', 'cst'], [pk[1]])
        for g in range(2):
            mm(pb[1][:, 256 + g * 128:256 + (g + 1) * 128], lwn[:, g * 128:(g + 1) * 128], C('Us' + T), True, True, ['lwn', 'cst'], [pk[1]])
        act(Eb[:].rearrange("p a b -> p (a b)"), pb[1][:, 0:256], AF.Exp, [pk[1]], ['Eb'], scale=-1.0)
        act(Einv[:].rearrange("p a b -> p (a b)"), pb[1][:, 0:256], AF.Exp, [pk[1]], ['Einv'])
        act(Ecx[:].rearrange("p a b -> p (a b)"), pb[1][:, 256:512], AF.Exp, [pk[1]], ['Ecx'], scale=-1.0)
        cp('pool', Eend[:, :, 0:nblk], Eb[:, :, blen - 1:128:blen], ['Eb'], ['Eend'])
        seg_a = S.end_capture()
        S.begin_capture()
        cp('act', alT[:], psf[0:32, GR_AL, :], ['psf'], ['alT'])
        for g in range(2):
            mm(pb[0][:, g * 128:(g + 1) * 128], a2[:, g * 128:(g + 1) * 128], alT[:], True, True, ['a2', 'alT'], [pk[0]])
        for g in range(2):
            S.op('act', lambda e, g=g: e.activation(out=aT[:, g, :], in_=pb[0][:, g * 128:(g + 1) * 128], func=AF.Exp, bias=prmp[:, PP['na0'] + g:PP['na0'] + g + 1], scale=-1.0),
                 [pk[0], 'prmp'], ['aT'])
        act(aT[:], aT[:], AF.Ln, ['aT'], ['aT'], bias=1.0)
        act(aT[:], aT[:], AF.Exp, ['aT'], ['aT'], scale=-1.0)
        for g in range(2):
            ts('dve', kkT[:, g, :], psf[:, GR_K0 + g, :], prmp[:, PP['kk'] + g:PP['kk'] + g + 1], None, ALU.mult, ALU.bypass, ['psf', 'prmp'], ['kkT'])
        tt('pool', ffm[:, 0:2, :], kkT[:], kkT[:], ALU.mult, ['kkT'], ['ffm'])
        cp('act', rkb[:], ffm[:, 0:2, :], ['ffm'], ['rkb'])
        for g in range(2):
            mm(pb[0][:, 256 + g * 128:256 + (g + 1) * 128], bonesb[:], rkb[:, g, :], True, True, ['bonesb', 'rkb'], [pk[0]])
        ts('dve', ffm[:, 2:4, :], pb[0][:, 256:512].rearrange("p (a b) -> p a b", b=128), 1e-19, None, ALU.max, ALU.bypass, [pk[0]], ['ffm'])
        act(ffm[:, 2:4, :], ffm[:, 2:4, :], AF.Ln, ['ffm'], ['ffm'])
        act(ffm[:, 2:4, :], ffm[:, 2:4, :], AF.Exp, ['ffm'], ['ffm'], scale=-0.5)
        tt('dve', kkT[:], kkT[:], ffm[:, 2:4, :], ALU.mult, ['kkT', 'ffm'], ['kkT'])
        for g in range(2):
            ts('dve', k2T[:, g, :], aT[:, g, :], prmp[:, PP['ka'] + g:PP['ka'] + g + 1], prmp[:, PP['omka'] + g:PP['omka'] + g + 1], ALU.mult, ALU.add, ['aT', 'prmp'], ['k2T'])
        tt('pool', k2T[:], k2T[:], psf[:, GR_K0:GR_K0 + 2, :], ALU.mult, ['k2T', 'psf'], ['k2T'])
        seg_b = S.end_capture()
        S.replay(merge2(seg_a, seg_b))
        S.begin_capture()
        tt('dve', arT[:, :, 0, :], kkT[:], Ecx[:], ALU.mult, ['kkT', 'Ecx'], ['arT'])
        tt('pool', arT[:, :, 1, :], psf[:, GR_R0:GR_R0 + 2, :], Eb[:], ALU.mult, ['psf', 'Eb'], ['arT'])
        tt('dve', ffm2[:, 0:2, :], kkT[:], aT[:], ALU.mult, ['kkT', 'aT'], ['ffm2'])
        tt('dve', btT[:], ffm2[:, 0:2, :], Einv[:], ALU.mult, ['ffm2', 'Einv'], ['btT'])
        tt('pool', ktT[:], k2T[:], Einv[:], ALU.mult, ['k2T', 'Einv'], ['ktT'])
        tt('pool', ffm2[:, 2:4, :], psf[:, GR_R0:GR_R0 + 2, :], k2T[:], ALU.mult, ['psf', 'k2T'], ['ffm2'])
        for g in range(2):
            ts('dve', rkb[:, g, :], ffm2[:, 2 + g, :], prmp[:, PP['rk'] + g:PP['rk'] + g + 1], None, ALU.mult, ALU.bypass, ['ffm2', 'prmp', 'rkb'], ['rkb'])
        sS = S.end_capture()
        S.begin_capture()
        for g in range(2):
            S.op('pe', lambda e, g=g: e.transpose(out=pb[3][:, 256 + g * 128:256 + (g + 1) * 128], in_=psf[:, GR_V0 + g, :], identity=C('ident')), ['psf', 'cst'], [pk[3]])
        cp('act', vTM32[:], pb[3][:, 256:512], [pk[3]], ['vTM32'])
        if l == 0:
            dma(vfs_d[t * 128:(t + 1) * 128, :], vTM32[:], ['vTM32'], ['vfs%d' % t])
        else:
            cp('act', vTb[:], psf[:, GR_V0:GR_V0 + 2, :], ['psf'], ['vTb'])
            for g in range(2):
                mm(pb[3][0:32, 0:128], v1[:, g, :], vTb[:, g, :], g == 0, g == 1, ['v1', 'vTb'], [pk[3]])
            cp('act', l1T[:], pb[3][0:32, 0:128], [pk[3]], ['l1T'])
            mm(pb[3][:, 0:256], l1T[:], v2[:], True, True, ['l1T', 'v2'], [pk[3]])
            tt('dve', vmix[:], pb[3][:, 0:256], PBr('v0'), ALU.add, [pk[3], 'prmb'], ['vmix'])
            sigm(vmix[:], vmix[:], ['vmix'], 'vmix')
            dma(lfTM[:], vfs_d[t * 128:(t + 1) * 128, :], ['vfs%d' % t], ['lfTM'])
            tt('pool', lfTM[:], lfTM[:], vTM32[:], ALU.subtract, ['lfTM', 'vTM32'], ['lfTM'])
            tt('pool', lfTM[:], lfTM[:], vmix[:], ALU.mult, ['lfTM', 'vmix'], ['lfTM'])
            tt('pool', vTM32[:], vTM32[:], lfTM[:], ALU.add, ['vTM32', 'lfTM'], ['vTM32'])
        cp('act', vb[:], vTM32[:], ['vTM32'], ['vb'])
        sV = S.end_capture()
        S.replay(merge2(sS, sV))
        S.begin_capture()
        for g in range(2):
            mm(pb[3][:, 256 + 2 * g:256 + 2 * g + 2], rkb[:, g, :], headindb[:], True, True, ['rkb', 'headindb'], [pk[3]])
        cp('act', bsum[:], pb[3][:, 256:260], [pk[3]], ['bsum'])
        for g in range(2):
            tr(pbb[2][:, g * 128:(g + 1) * 128], ktT[:, g, :], identb[:], ['ktT', 'identb'], [pk[2]])
            tr(pbb[2][:, 256 + g * 128:256 + (g + 1) * 128], btT[:, g, :], identb[:], ['btT', 'identb'], [pk[2]])
        cp('act', kTM[:], pbb[2][:, 0:256], [pk[2]], ['kTM'])
        act(bTM[:], pbb[2][:, 256:512], AF.Copy, [pk[2]], ['bTM'], scale=-1.0)
        sX = S.end_capture()
        S.begin_capture()
        sigm(ffm2[:, 2:4, :], psf[:, GR_G0:GR_G0 + 2, :], ['psf'], 'ffm2', eng2='pool')
        tt('pool', sTm[:, 0:2, :], ffm2[:, 2:4, :], psf[:, GR_G0:GR_G0 + 2, :], ALU.mult, ['ffm2', 'psf'], ['sTm'])
        for g in range(2):
            tr(pbb[2][:, 512 + g * 128:512 + (g + 1) * 128], sTm[:, g, :], identb[:], ['sTm', 'identb'], [pk[2]])
        cp('act', gsil[:], pbb[2][:, 512:768], [pk[2]], ['gsil'])
        sG = S.end_capture()
        S.replay(merge2(sX, sG))

    def rwkv_be(l, t):
        T = tile_type(t)
        sample = (T == 'S')
        nblk = 16 if sample else 1
        blen = 128 // nblk
        nlev = 3 if sample else 7
        m = 'rwkv'
        for g in range(2):
            for hh in range(2):
                po = hh * 64
                cp('act' if hh == 0 else 'dve', arTm[po:po + 64, g, hh, :, :], arT[po:po + 64, g, :, :], ['arT'], ['arTm'])
        for (lhsT_, lkey, dst, dkey, msk, mkey) in [(btT, 'btT', Aab, 'Aab', mAB[T], 'mAB' + T), (ktT, 'ktT', Aak, 'Aak', mAK[T], 'mAK' + T)]:
            for g in range(2):
                for hh in range(2):
                    mm(pb[5][:, hh * 256:(hh + 1) * 256], lhsT_[:, g, :], arTm[:, g, hh, :, :].rearrange("p a b -> p (a b)"), True, True, [lkey, 'arTm'], [pk[5]])
                for hh in range(2):
                    tt('dve', dst[:, 2 * g + hh, :], pb[5][:, hh * 256:(hh + 1) * 256], msk[:], ALU.mult, [pk[5], mkey], [dkey])
        for h in range(4):
            g, hh = h // 2, h % 2
            mm(pb[5][:, h * 128:(h + 1) * 128], arTm[:, g, hh, 0, :], btT[:, g, :], True, True, ['btT', 'arTm'], [pk[5]])
        for h in range(4):
            tt('dve', MiT[0][:, h, :], pb[5][:, h * 128:(h + 1) * 128], mABt[T][:], ALU.mult, [pk[5], 'mABt' + T], ['MiT0'])
        for h in range(4):
            tt('pool', Pm[0][:, h, :], Aab[:, h, 0:128], identb[:], ALU.add, ['Aab', 'identb'], ['Pm0'])
        curP = 0
        for lev in range(1, nlev):
            nxt = lev % 2
            prv = (lev - 1) % 2
            last = (lev == nlev - 1)
            for h in range(4):
                Mprev = Aab[:, h, 0:128] if lev == 1 else Mi[prv][:, h, :]
                mkey = 'Aab' if lev == 1 else 'Mi%d' % prv
                mm(pb[5][:, h * 128:(h + 1) * 128], Mprev, MiT[prv][:, h, :], True, True, [mkey, 'MiT%d' % prv], [pk[5]])
            cp('act', MiT[nxt][:].rearrange("p a b -> p (a b)"), pb[5][:, 0:512], [pk[5]], ['MiT%d' % nxt])
            if not last:
                for h in range(4):
                    Mprev = Aab[:, h, 0:128] if lev == 1 else Mi[prv][:, h, :]
                    mkey = 'Aab' if lev == 1 else 'Mi%d' % prv
                    mm(pb[7][:, h * 128:(h + 1) * 128], MiT[prv][:, h, :], Mprev, True, True, [mkey, 'MiT%d' % prv], [pk[7]])
                cp('act', Mi[nxt][:].rearrange("p a b -> p (a b)"), pb[7][:, 0:512], [pk[7]], ['Mi%d' % nxt])
            for h in range(4):
                mm(pb[6][:, h * 128:(h + 1) * 128], MiT[nxt][:, h, :], Pm[curP][:, h, :], True, True, ['MiT%d' % nxt, 'Pm%d' % curP], [pk[6]])
            tt('dve', Pm[1 - curP][:].rearrange("p a b -> p (a b)"), pb[6][:, 0:512], Pm[curP][:].rearrange("p a b -> p (a b)"), ALU.add, [pk[6], 'Pm%d' % curP], ['Pm%d' % (1 - curP)])
            curP = 1 - curP
        Tm = Pm[curP]; tkey = 'Pm%d' % curP
        if sample:
            d = st_rwkv[l]
            for hp in range(2):
                for half in range(2):
                    b0 = half * 8
                    fns = []
                    for hh in range(2):
                        fns.append(lambda e, hp=hp, hh=hh, b0=b0: e.dma_start(out=Sbdn[hh * 64:(hh + 1) * 64, :, hh * 64:(hh + 1) * 64],
                                                                             in_=d[b0:b0 + 8, 2 * hp + hh].rearrange("b v k -> v b k"), allow_slow_non_contiguous=True))
                    S.dma(fns, [], ['Sbdn'])
                    for bb in range(8):
                        mm(pb[7][:, bb * 64:(bb + 1) * 64], Sbdn[:, bb, :], IIf[:], True, True, ['Sbdn', 'IIf'], [pk[7]])
                    cp('dve', SfS[:, hp, b0:b0 + 8, :], pb[7][:, 0:512].rearrange("p (a b) -> p a b", b=64), [pk[7]], ['SfS'])
            for g in range(2):
                cast_state(True, 2, 64, g, 0, 16, SfS, 'SfS', 'SbS')
            sbkey, Sf, skey = 'SbS', SfS, 'SfS'
        else:
            sbkey, Sf, skey = 'SbP', SfP[m], 'SfP'
        Sbd = sbd_view(sample, 2)
        for g in range(2):
            if sample:
                tt('pool', Qblk[:, 0, 0:nblk, :], arT[:, g, 0, :].unsqueeze(1).to_broadcast([128, nblk, 128]), bindFM['S'][:], ALU.mult, ['arT', 'bindFMS'], ['Qblk'])
            for s in range(nblk):
                lhs = Qblk[:, 0, s, :] if sample else arT[:, g, 0, :]
                mm(pb[6][:, g * 128:(g + 1) * 128], lhs, Sbd[:, g, s, :], s == 0, False, ['Qblk' if sample else 'arT', sbkey], [pk[6]])
            for hh in range(2):
                h = 2 * g + hh
                mm(pb[6][:, h * 64:(h + 1) * 64], Aak[:, h, 0:128], vb[:, h * 64:(h + 1) * 64], False, hh == 1, ['Aak', 'vb'], [pk[6]])
        cp('act', XT[:], pb[6][:, 0:256], [pk[6]], ['XT'])
        for h in range(4):
            mm(pb[6][:, 256 + h * 64:256 + (h + 1) * 64], Tm[:, h, :], XT[:, h * 64:(h + 1) * 64], True, True, [tkey, 'XT'], [pk[6]])
        cp('act', SAT[:], pb[6][:, 256:512], [pk[6]], ['SAT'])
        for g in range(2):
            if sample:
                tt('pool', Qblk[:, 0, 0:nblk, :], arT[:, g, 1, :].unsqueeze(1).to_broadcast([128, nblk, 128]), bindFM['S'][:], ALU.mult, ['arT', 'bindFMS', 'Qblk'], ['Qblk'])
            for s in range(nblk):
                lhs = Qblk[:, 0, s, :] if sample else arT[:, g, 1, :]
                mm(pb[6][:, g * 128:(g + 1) * 128], lhs, Sbd[:, g, s, :], s == 0, False, ['Qblk' if sample else 'arT', sbkey], [pk[6]])
            for hh in range(2):
                h = 2 * g + hh
                mm(pb[6][:, h * 64:(h + 1) * 64], Aab[:, h, 128:256], SAT[:, h * 64:(h + 1) * 64], False, False, ['Aab', 'SAT'], [pk[6]])
                mm(pb[6][:, h * 64:(h + 1) * 64], Aak[:, h, 128:256], vb[:, h * 64:(h + 1) * 64], False, hh == 1, ['Aak', 'vb'], [pk[6]])
        chunks = [(c0, min(nblk, c0 + 8)) for c0 in range(0, nblk, 8)]
        ubank = [7, 5]
        for g in range(2):
            if sample:
                bind = C('bindS')
                tt('pool', Vblk[:], vb[:, g * 128:(g + 1) * 128].rearrange("p (h d) -> p h d", d=64).unsqueeze(2).to_broadcast([128, 2, 16, 64]),
                   bind.unsqueeze(1).unsqueeze(3).to_broadcast([128, 2, 16, 64]), ALU.mult, ['vb', 'cst'], ['Vblk'])
                tt('pool', SAblk, SAT[:, g * 128:(g + 1) * 128].rearrange("p (h d) -> p h d", d=64).unsqueeze(2).to_broadcast([128, 2, 16, 64]),
                   bind.unsqueeze(1).unsqueeze(3).to_broadcast([128, 2, 16, 64]), ALU.mult, ['SAT', 'cst', 'Qblk'], ['Qblk'])
            for ci, (c0, c1) in enumerate(chunks):
                ncol = (c1 - c0) * 64
                bk = ubank[ci]
                for hh in range(2):
                    h = 2 * g + hh
                    po = hh * 64
                    rv = Vblk[:, hh, c0:c1, :].rearrange("p a b -> p (a b)") if sample else vb[:, h * 64:(h + 1) * 64]
                    rs_ = SAblk[:, hh, c0:c1, :].rearrange("p a b -> p (a b)") if sample else SAT[:, h * 64:(h + 1) * 64]
                    mm(pb[bk][po:po + 64, 0:ncol], kTM[:, g * 128 + po:g * 128 + po + 64], rv, True, False, ['kTM', 'vb', 'Vblk'], [pk[bk]])
                    mm(pb[bk][po:po + 64, 0:ncol], bTM[:, g * 128 + po:g * 128 + po + 64], rs_, False, True, ['bTM', 'SAT', 'Qblk'], [pk[bk]])
                if sample:
                    tt('dve', Sf[:, g, c0:c1, :], pb[bk][:, 0:ncol].rearrange("p (a b) -> p a b", b=64), Sf[:, g, c0:c1, :], ALU.add, [pk[bk], skey], [skey])
                    tt('pool', Sf[:, g, c0:c1, :], Sf[:, g, c0:c1, :], Eend[:, g, c0:c1].unsqueeze(2).to_broadcast([128, c1 - c0, 64]), ALU.mult, [skey, 'Eend'], [skey])
                else:
                    tt('dve', stmp[:], pb[bk][:, 0:64], Sf[:, g, 0, :], ALU.add, [pk[bk], skey], ['stmp'])
                    ts('dve', Sf[:, g, 0, :], stmp[:], Eend[:, g, 0:1], None, ALU.mult, ALU.bypass, ['stmp', 'Eend'], [skey])
                    cast_state(False, 2, 64, g, 0, 1, Sf, skey, sbkey)
        if sample:
            do = o_rwkv_s[l]
            for hp in range(2):
                for half in range(2):
                    b0 = half * 8
                    for hh in range(2):
                        cp('pool', Sbdn[hh * 64:(hh + 1) * 64, :, hh * 64:(hh + 1) * 64], SfS[hh * 64:(hh + 1) * 64, hp, b0:b0 + 8, :], ['SfS'], ['Sbdn'])
                    for bb in range(8):
                        mm(pb[7][:, bb * 64:(bb + 1) * 64], Sbdn[:, bb, :], IIf[:], True, True, ['Sbdn', 'IIf'], [pk[7]])
                    cp('act', stg[:], pb[7][:, 0:512].rearrange("p (a b) -> p a b", b=64), [pk[7]], ['stg'])
                    fns = []
                    for hh in range(2):
                        fns.append(lambda e, hp=hp, hh=hh, b0=b0: e.dma_start(out=do[b0:b0 + 8, 2 * hp + hh].rearrange("b v k -> v b k"),
                                                                             in_=stg[hh * 64:(hh + 1) * 64, :, :], allow_slow_non_contiguous=True))
                    S.dma(fns, ['stg'], [])
        o_evac()
        if PIPED[0]:
            tt('pool', bon[:].rearrange("p (h d) -> p h d", d=64), vTM32[:].rearrange("p (h d) -> p h d", d=64), bsum[:].unsqueeze(2).to_broadcast([128, 4, 64]), ALU.mult, ['vTM32', 'bsum'], ['bon'])

    def rwkv_be2(l, t):
        o3 = of32[:].rearrange("p (h d) -> p h d", d=64)
        red(st4[:, 0:4], o3, ['of32'], ['st4'])
        ts('dve', st4[:, 0:4], st4[:, 0:4], 1.0 / 64, None, ALU.mult, ALU.bypass, ['st4'], ['st4'])
        tt('dve', o3, o3, st4[:, 0:4].unsqueeze(2).to_broadcast([128, 4, 64]), ALU.subtract, ['of32', 'st4'], ['of32'])
        tt('pool', osq[:], of32[:], of32[:], ALU.mult, ['of32'], ['osq'])
        red(st4[:, 4:8], osq[:].rearrange("p (h d) -> p h d", d=64), ['osq'], ['st4'])
        rsq(st4[:, 4:8], st4[:, 4:8], ['st4'], 'st4', 1.0 / 64, 64e-5)
        tt('dve', o3, o3, st4[:, 4:8].unsqueeze(2).to_broadcast([128, 4, 64]), ALU.mult, ['of32', 'st4'], ['of32'])
        tt('pool', of32[:], of32[:], PBr('gng'), ALU.mult, ['of32', 'prmb'], ['of32'])
        tt('pool', of32[:], of32[:], PBr('gnb'), ALU.add, ['of32', 'prmb'], ['of32'])
        if PIPED[0]:
            tt('pool', of32[:], of32[:], bon[:], ALU.add, ['of32', 'bon'], ['of32'])
        else:
            tt('pool', osq[:].rearrange("p (h d) -> p h d", d=64), vTM32[:].rearrange("p (h d) -> p h d", d=64), bsum[:].unsqueeze(2).to_broadcast([128, 4, 64]), ALU.mult, ['vTM32', 'bsum'], ['osq'])
            tt('pool', of32[:], of32[:], osq[:], ALU.add, ['of32', 'osq'], ['of32'])
        tt('dve', ofin[:, 768:1024], of32[:], gsil[:], ALU.mult, ['of32', 'gsil'], ['ofin'])

    def store_rwkv_prompt_state(l):
        d = o_rwkv_p[l]
        fns = []
        for h in range(4):
            hp, hh = h // 2, h % 2
            fns.append(lambda e, h=h, hp=hp, hh=hh: e.dma_start(out=d[h].rearrange("v k -> k v"), in_=SfP['rwkv'][hh * 64:(hh + 1) * 64, hp, 0, :], allow_slow_non_contiguous=True))
        S.dma(fns, ['SfP'], [])

    def out_proj(t, mi):
        ak = 'xa0'
        for c in range(2):
            tr(pbb[4][:, c * 128:(c + 1) * 128], ofin[:, mi * 256 + c * 128:mi * 256 + (c + 1) * 128], identb[:], ['ofin', 'identb'], [pk[4]])
        cp('act', oT[:, 0:2, :].rearrange("p a b -> p (a b)"), pbb[4][:, 0:256], [pk[4]], ['oT'])
        dma(xa[:, 0, :], xacc_d[t * 128:(t + 1) * 128, :], ['xacc%d' % t], [ak])
        for half in range(2):
            for c in range(2):
                mm(pb[4][:, 0:512], oT[:, c, :], wout[:, 2 * mi + c, half * 512:(half + 1) * 512], c == 0, c == 1, ['oT', 'wout'], [pk[4]])
            tt('dve', xa[:, 0, half * 512:(half + 1) * 512], pb[4][:, 0:512], xa[:, 0, half * 512:(half + 1) * 512], ALU.add, [pk[4], ak], [ak])
        dma(xacc_d[t * 128:(t + 1) * 128, :], xa[:, 0, :], [ak], ['xacc%d' % t])

    def cross_fe(l, t):
        T = tile_type(t)
        xk = 'xt%d' % (t % 2)
        norm_T(xt[:, t % 2, :], xk, nrow[:, 0, :], 'nrow0')
        for hp in range(2):
            for kc in range(8):
                mm(pb[0][:, hp * 128:(hp + 1) * 128], wq[:, kc, hp * 128:(hp + 1) * 128], hT[:, HI[0], kc, :], kc == 0, kc == 7, ['wq', 'hT%d' % HI[0]], [pk[0]])
        for hp in range(2):
            for hh in range(2):
                po = hh * 64
                act(qxTm[po:po + 64, hp, hh, :], pb[0][po:po + 64, hp * 128:(hp + 1) * 128], AF.Copy, [pk[0]], ['qxTm'], scale=0.125)
        if T == 'P':
            for mc in range(2):
                for h in range(4):
                    hp, hh = h // 2, h % 2
                    mm(pb[1 + mc][:, h * 128:(h + 1) * 128], mkT[:, l, hp, mc * 128:(mc + 1) * 128], qxTm[:, hp, hh, :], True, True, ['mkT', 'qxTm'], [pk[1 + mc]])
        else:
            for b in range(16):
                load_cast(memb[:], cmk[l, b].rearrange("(mc p) f -> p mc f", p=128), 'memb', alt=b % 2)
                for mc in range(2):
                    for hp in range(2):
                        tr(pbb[4][:, (mc * 2 + hp) * 128:(mc * 2 + hp + 1) * 128], memb[:, mc, hp * 128:(hp + 1) * 128], identb[:], ['memb', 'identb'], [pk[4]])
                cp('act', mkTs[:].rearrange("p hp (mc m) -> p mc hp m", mc=2), pbb[4][:, 0:512].rearrange("p (mc hp m) -> p mc hp m", mc=2, hp=2), [pk[4]], ['mkTs'])
                for mc in range(2):
                    for h in range(4):
                        hp, hh = h // 2, h % 2
                        mm(pb[1 + mc][:, h * 128 + 8 * b:h * 128 + 8 * b + 8], mkTs[:, hp, mc * 128:(mc + 1) * 128], qxTm[:, hp, hh, 8 * b:8 * b + 8], True, True, ['mkTs', 'qxTm'], [pk[1 + mc]])
        for mc in range(2):
            act(pT[:, mc, :, :].rearrange("p a b -> p (a b)"), pb[1 + mc][:, 0:512], AF.Exp, [pk[1 + mc]], ['pT'])
        for mc in range(2):
            mm(pb[3][:, 0:512], onesb[:], pT[:, mc, :, :].rearrange("p a b -> p (a b)"), mc == 0, mc == 1, ['onesb', 'pT'], [pk[3]])
        act(rsb[:].rearrange("p a b -> p (a b)"), pb[3][:, 0:512], AF.Ln, [pk[3]], ['rsb'])
        act(rsb[:].rearrange("p a b -> p (a b)"), rsb[:].rearrange("p a b -> p (a b)"), AF.Exp, ['rsb'], ['rsb'], scale=-1.0)

    def cross_be(l, t):
        T = tile_type(t)
        xk = 'xt%d' % (t % 2)
        if T == 'P':
            for h in range(4):
                hp, po = h // 2, (h % 2) * 64
                for mc in range(2):
                    mm(pb[5][po:po + 64, hp * 128:(hp + 1) * 128], mvb[:, l, mc, h * 64:(h + 1) * 64], pT[:, mc, h, :], mc == 0, mc == 1, ['mvb', 'pT'], [pk[5]])
        else:
            for b in range(16):
                load_cast(mvs[:], cmv[l, b].rearrange("(mc p) f -> p mc f", p=128), 'mvs', alt=b % 2)
                for h in range(4):
                    hp, po = h // 2, (h % 2) * 64
                    for mc in range(2):
                        mm(pb[5][po:po + 64, hp * 128 + 8 * b:hp * 128 + 8 * b + 8], mvs[:, mc, h * 64:(h + 1) * 64], pT[:, mc, h, 8 * b:8 * b + 8], mc == 0, mc == 1, ['mvs', 'pT'], [pk[5]])
        for h in range(4):
            hp, po = h // 2, (h % 2) * 64
            tt('dve', oxT[po:po + 64, hp, :], pb[5][po:po + 64, hp * 128:(hp + 1) * 128], rsb[po:po + 64, h, :], ALU.mult, [pk[5], 'rsb'], ['oxT'])
        for half in range(2):
            for kc in range(2):
                mm(pb[6 + half][:, 0:512], oxT[:, kc, :], wo[:, kc, half * 512:(half + 1) * 512], kc == 0, kc == 1, ['oxT', 'wo'], [pk[6 + half]])
            tt('dve', xt[:, t % 2, half * 512:(half + 1) * 512], pb[6 + half][:, 0:512], xt[:, t % 2, half * 512:(half + 1) * 512], ALU.add, [pk[6 + half], xk], [xk])
        if l == 1:
            final_norm(t)
        else:
            dma(xs_d[t * 128:(t + 1) * 128, :], xt[:, t % 2, :], [xk], ['xs%d' % t])

    def final_norm(t):
        xk = 'xt%d' % (t % 2)
        src = xt[:, t % 2, :]
        S.op('dve', lambda e: e.scalar_tensor_tensor(out=ofin[:], in0=src, scalar=1.0, in1=src, op0=ALU.mult, op1=ALU.mult, accum_out=st4[:, 4:5]), [xk], ['ofin', 'st4'])
        rsq(st4[:, 6:7], st4[:, 4:5], ['st4'], 'st4', 1.0 / D, 1e-6)
        ystage = psf[:, 0:8, :].rearrange("p a b -> p (a b)")
        stt(ystage, src, st4[:, 6:7], nrow[:, 1, :], ALU.mult, ALU.mult, [xk, 'st4', 'nrow1', 'psf'], ['psf'])
        dst = y_s if t == 16 else y_p[t * 128:(t + 1) * 128, :]
        dma(dst, ystage, ['psf'], [])

    dma(nrow[:, 1, :], norm_f.partition_broadcast(128), [], ['nrow1'])
    MIX = [('ret', R0, 1024, ret_fe, ret_be, ret_be2), ('hgrn', H0, 1024, hgrn_fe, hgrn_be, hgrn_be2),
           ('gla', G0, 784, gla_fe, gla_be, gla_be2), ('rwkv', W0, 1088, rwkv_fe, rwkv_be, rwkv_be2)]
    PIPE = os.environ.get('NOPIPE', '0') != '1'

    def interleave(lists):
        lists = [x for x in lists if x]
        out = []
        idx = [0] * len(lists)
        tot = sum(len(x) for x in lists)
        while len(out) < tot:
            best = None
            for k, x in enumerate(lists):
                if idx[k] < len(x):
                    frac = idx[k] / float(len(x))
                    if best is None or frac < best[0]:
                        best = (frac, k)
            k = best[1]
            out.append(lists[k][idx[k]]); idx[k] += 1
        return out

    def load_x(l, t):
        if l == 0:
            src = x_s if t == 16 else x_p[t * 128:(t + 1) * 128, :]
            dma(xt[:, t % 2, :], src, [], ['xt%d' % (t % 2)])
        else:
            dma(xt[:, t % 2, :], xs_d[t * 128:(t + 1) * 128, :], ['xs%d' % t], ['xt%d' % (t % 2)])

    for l in range(NLAYERS):
        load_layer_weights(l)
        if l == 1:
            load_wout(1)
        load_params(l)
        for mi, (m, base, width, fn, fn_be, fn_be2) in enumerate(MIX):
            if MIXSEL is not None and m not in MIXSEL:
                continue
            S.barrier()
            WB[0] = base
            for kc in range(8):
                load_cast(win[:, kc, 0:width], w_in[l, kc * 128:(kc + 1) * 128, base:base + width], 'win', alt=kc % 2)
            if mi == 0:
                dma(nrow[:, 0, :], norm_mix[l].partition_broadcast(128), [], ['nrow0'])
            zero_prompt_states()
            def load_h(ti_):
                tt_ = TILES[ti_]
                dma(hT[:, ti_ % 2, :, :].rearrange("p a b -> p (a b)"), hts_d[tt_ * 128:(tt_ + 1) * 128, :], ['hts%d' % tt_], ['hT%d' % (ti_ % 2)])
            if mi == 0:
                load_x(l, TILES[0])
            else:
                load_h(0)
            pl = {'be1': None, 'be1_t': None, 'be2_prev': None, 'be2_old': None}

            def finish_tile(t_):
                if t_ == LASTP:
                    if m == 'rwkv':
                        store_rwkv_prompt_state(l)
                    else:
                        store_prompt_state(m, l)

            def drain():
                if pl['be1'] is None and pl['be2_old'] is None and pl['be2_prev'] is None:
                    return
                S.replay(interleave([pl['be2_old'], pl['be1']]))
                if pl['be1'] is not None:
                    finish_tile(pl['be1_t'])
                if pl['be2_prev'] is not None:
                    S.replay(pl['be2_prev'])
                pl['be1'] = pl['be2_prev'] = pl['be2_old'] = None
                S.barrier()
            for ti, t in enumerate(TILES):
                HI[0] = ti % 2
                def pre0(t=t):
                    norm_T(xt[:, t % 2, :], 'xt%d' % (t % 2), nrow[:, 0, :], 'nrow0', bank=0, st=st4b)
                    dma(hts_d[t * 128:(t + 1) * 128, :], hT[:, HI[0], :, :].rearrange("p a b -> p (a b)"), ['hT%d' % HI[0]], ['hts%d' % t])
                    dma(xacc_d[t * 128:(t + 1) * 128, :], xt[:, t % 2, :], ['xt%d' % (t % 2)], ['xacc%d' % t])
                if mi == 0:
                    pass
                else:
                    if ti + 1 < len(TILES):
                        load_h(ti + 1)
                piped = PIPE and t != 16
                if not piped:
                    drain()
                    S.par[0] = 0; PO[0] = 0; PG[0] = 0; PIPED[0] = False
                    if mi == 0:
                        pre0()
                        if ti + 1 < len(TILES):
                            load_x(l, TILES[ti + 1])
                    fn(l, t); fn_be(l, t); fn_be2(l, t); out_proj(t, mi)
                    finish_tile(t)
                else:
                    PIPED[0] = True
                    S.par[0] = (ti % 2) * (2 if mi == 0 else 1); PO[0] = ti % 2; PG[0] = ti % 3
                    S.begin_capture()
                    if mi == 0:
                        pre0()
                    fn(l, t); fe_items = S.end_capture()
                    S.replay(interleave([pl['be2_old'], pl['be1'], fe_items]))
                    if mi == 0 and ti + 1 < len(TILES):
                        load_x(l, TILES[ti + 1])
                    if pl['be1'] is not None:
                        finish_tile(pl['be1_t'])
                    pl['be2_old'] = pl['be2_prev']
                    S.begin_capture(); fn_be(l, t); pl['be1'] = S.end_capture(); pl['be1_t'] = t
                    S.begin_capture(); fn_be2(l, t); out_proj(t, mi); pl['be2_prev'] = S.end_capture()
            drain()
            S.par[0] = 0; PO[0] = 0; PG[0] = 0; PIPED[0] = False
        if MIXSEL is not None and 'cross' not in MIXSEL:
            continue
        S.barrier()
        dma(nrow[:, 0, :], norm_x[l].partition_broadcast(128), [], ['nrow0'])
        def load_xacc(t):
            dma(xt[:, t % 2, :], xacc_d[t * 128:(t + 1) * 128, :], ['xacc%d' % t], ['xt%d' % (t % 2)])
        HI[0] = 0
        load_xacc(TILES[0])
        pendx = None
        for ti, t in enumerate(TILES):
            piped = PIPE and t != 16
            if not piped:
                if pendx is not None:
                    S.replay(pendx); pendx = None
                    S.barrier()
                if ti + 1 < len(TILES):
                    load_xacc(TILES[ti + 1])
                PX[0] = 0
                cross_fe(l, t); cross_be(l, t)
            else:
                PX[0] = ti % 2
                S.begin_capture(); cross_fe(l, t); fe_items = S.end_capture()
                S.replay(interleave([pendx, fe_items]))
                if ti + 1 < len(TILES):
                    load_xacc(TILES[ti + 1])
                S.begin_capture(); cross_be(l, t); pendx = S.end_capture()
        if pendx is not None:
            S.replay(pendx); pendx = None
        PX[0] = 0
    print('recorded ops', S.nrec)
    S.final_wait_all('sp')
    S.emit()
    return nc


_NC_CACHE = {}


def kernel(**inputs):
    inp = {k: np.ascontiguousarray(np.asarray(v, dtype=np.float32)) for k, v in inputs.items()}
    if 'nc' not in _NC_CACHE:
        _NC_CACHE['nc'] = build_nc()
    nc = _NC_CACHE['nc']
    cst, bH, bS = make_consts()
    shared = {}
    for k in ['norm_mix', 'w_in', 'hgrn_lb_logits', 'hgrn_norm', 'gla_w_gate2', 'gla_b_gate', 'gla_norm', 'rwkv_mu', 'rwkv_w0',
              'rwkv_w2', 'rwkv_a0', 'rwkv_a2', 'rwkv_v0', 'rwkv_v1', 'rwkv_v2', 'rwkv_k_k', 'rwkv_k_a', 'rwkv_gn_g', 'rwkv_gn_b',
              'w_out', 'norm_x', 'wq_x', 'wo_x', 'norm_mem', 'wk_x', 'wv_x', 'norm_f']:
        shared[k] = inp[k]
    shared['rwkv_r_k'] = inp['rwkv_r_k'].reshape(2, 256)
    shared['cst'] = cst; shared['bindH_row'] = bH; shared['bindS_row'] = bS
    in_maps = []
    for c in range(8):
        m = dict(shared)
        m['x_p'] = inp['x_prompt'][c]
        m['x_s'] = inp['x_sample'][16 * c:16 * c + 16].reshape(128, D)
        m['mem_p'] = inp['mem_prompt'][c]
        sl = slice(16 * c, 16 * c + 16)
        m['st_ret'] = np.ascontiguousarray(inp['state_ret'][:, sl])
        m['st_hgrn'] = np.ascontiguousarray(inp['state_hgrn'][:, sl])
        m['st_gla'] = np.ascontiguousarray(inp['state_gla'][:, sl])
        m['st_rwkv'] = np.ascontiguousarray(inp['state_rwkv'][:, sl])
        m['st_shift'] = np.ascontiguousarray(inp['state_rwkv_shift'][:, sl])
        m['cmk'] = np.ascontiguousarray(inp['cache_mem_k'][:, sl].reshape(2, 16, 256, 256))
        m['cmv'] = np.ascontiguousarray(inp['cache_mem_v'][:, sl].reshape(2, 16, 256, 256))
        in_maps.append(m)
    res = run_bass_kernel_spmd(nc, in_maps, core_ids=list(range(8)))
    R = res.results

    def cat(name, axis):
        return np.concatenate([np.asarray(R[c][name], np.float32) for c in range(8)], axis=axis)

    def stackp(name):
        return np.stack([np.asarray(R[c][name], np.float32) for c in range(8)], axis=1)

    y_prompt = np.stack([np.asarray(R[c]['y_p'], np.float32) for c in range(8)], axis=0)
    y_sample = cat('y_s', 0).reshape(128, 8, D)
    outs = (y_prompt, y_sample,
            stackp('ret_p'), cat('ret_s', 1), stackp('hgrn_p'), cat('hgrn_s', 1), stackp('gla_p'), cat('gla_s', 1),
            stackp('rwkv_p'), cat('rwkv_s', 1), stackp('shift_p'), cat('shift_s', 1),
            stackp('mk_p').reshape(2, 8, 256, 4, 64), stackp('mv_p').reshape(2, 8, 256, 4, 64))
    return outs
```

```python
import contextlib
import sys
import numpy as np
import concourse.bass as bass
import concourse.mybir as mybir
from concourse.bass_utils import run_bass_kernel_spmd

F32 = mybir.dt.float32
BF16 = mybir.dt.bfloat16
AF = mybir.ActivationFunctionType
ALU = mybir.AluOpType
AX = mybir.AxisListType

COMPUTE = ('pe', 'act', 'dve', 'pool')
import os
NO_SELF_SYNC = tuple(x for x in os.environ.get('NO_SELF_SYNC', '').split(',') if x)
N_DMA_SEMS = 40
EPOCH = 3000

D = 1024
DIN = 3920
NT = 17
R0, H0, G0, W0 = 0, 1024, 2048, 2832
PAST = 16384
RW_GROUPS = [(0, 128), (128, 128), (256, 32), (288, 128), (416, 128), (544, 128), (672, 128), (800, 32),
             (832, 128), (960, 128)]
GR_R0, GR_R1, GR_WL, GR_K0, GR_K1, GR_V0, GR_V1, GR_AL, GR_G0, GR_G1 = range(10)


class Sched:
    def __init__(self, nc):
        self.nc = nc
        self.engs = ('pe', 'act', 'dve', 'pool', 'sp')
        self.prog = {e: [] for e in self.engs}
        self.cnt = {e: 0 for e in COMPUTE}
        self.last_w = {}
        self.readers = {}
        self.waited = {e: {} for e in self.engs}
        self.dma_cum = [0] * N_DMA_SEMS
        self.dma_next = 0
        self.semnames = set()
        self.limit = None
        self.nrec = 0
        self.cap = None
        self.parsel = {}
        self.par = [0]

    def _deps(self, reads, writes):
        deps = []
        for k in reads:
            if k in self.last_w:
                deps.append(self.last_w[k])
            if k.startswith('pb'):
                deps.extend(self.readers.get(k, ()))
        for k in writes:
            if k in self.last_w:
                deps.append(self.last_w[k])
            deps.extend(self.readers.get(k, ()))
        return deps

    def _waits_for(self, eng, deps):
        best = {}
        for (s, v) in deps:
            if eng == 'pe' and s.startswith('pe_'):
                continue
            if NO_SELF_SYNC and s.startswith(eng + '_') and eng in NO_SELF_SYNC:
                continue
            if v > best.get(s, 0):
                best[s] = v
        out = []
        w = self.waited[eng]
        for s, v in best.items():
            if w.get(s, 0) < v:
                w[s] = v
                out.append((s, v))
        return out

    def _commit(self, dep, reads, writes):
        for k in writes:
            self.last_w[k] = dep
            self.readers[k] = []
        for k in reads:
            if k in writes:
                continue
            self.readers.setdefault(k, []).append(dep)

    def _tag(self):
        f = sys._getframe(2)
        tags = []
        while f is not None and len(tags) < 3:
            if f.f_code.co_name not in ('op', 'dma', 'mm', 'tr', 'act', 'tt', 'ts', 'stt', 'cp', 'red', 'recip', 'load_cast', '<lambda>'):
                tags.append('%s:%d' % (f.f_code.co_name, f.f_lineno))
            f = f.f_back
        return '|'.join(tags)

    def _pk(self, keys):
        ps = self.parsel
        return tuple((k + '#%d' % ps[k][0]) if k in ps else k for k in keys)

    def begin_capture(self):
        if not hasattr(self, 'capstack'):
            self.capstack = []
        self.capstack.append(self.cap)
        self.cap = []

    def end_capture(self):
        c = self.cap
        self.cap = self.capstack.pop()
        return c

    def replay(self, items):
        if self.cap is not None:
            self.cap.extend(items)
            return
        for it in items:
            if it[0] == 'op':
                self.op(it[1], it[2], it[3], it[4], _tag=it[5], _raw=True)
            else:
                self.dma(it[1], it[2], it[3], eng=it[4], _raw=True)

    def barrier(self):
        if os.environ.get('NOBAR', '0') == '1':
            return
        for e in self.engs:
            self.final_wait_all(e)

    def op(self, eng, fn, reads=(), writes=(), _tag=None, _raw=False):
        if not _raw:
            reads = self._pk(reads); writes = self._pk(writes)
            _tag = self._tag()
            if self.cap is not None:
                self.cap.append(('op', eng, fn, reads, writes, _tag))
                return
        self.nrec += 1
        if self.limit is not None and self.nrec > self.limit:
            return
        reads = tuple(reads); writes = tuple(writes)
        tag = _tag
        fn0 = fn
        fn = lambda e: fn0(e).annotate(tag)
        waits = self._waits_for(eng, self._deps(reads, writes))
        c = self.cnt[eng]
        self.cnt[eng] += 1
        sname = '%s_%d' % (eng, c // EPOCH)
        self.semnames.add(sname)
        dep = (sname, c % EPOCH + 1)
        self.prog[eng].append((waits, fn, (sname, 1)))
        self._commit(dep, reads, writes)

    def dma(self, fns, reads=(), writes=(), eng='sp', _raw=False):
        if not _raw:
            reads = self._pk(reads); writes = self._pk(writes)
            if self.cap is not None:
                self.cap.append(('dma', fns, reads, writes, eng))
                return
        self.nrec += 1
        if self.limit is not None and self.nrec > self.limit:
            return
        reads = tuple(reads); writes = tuple(writes)
        if not isinstance(fns, (list, tuple)):
            fns = [fns]
        si = self.dma_next
        self.dma_next = (self.dma_next + 1) % N_DMA_SEMS
        sname = 'dma%d' % si
        self.semnames.add(sname)
        deps = self._deps(reads, writes)
        if self.dma_cum[si] > 0:
            deps.append((sname, self.dma_cum[si]))
        waits = self._waits_for(eng, deps)
        first = True
        for fn in fns:
            self.dma_cum[si] += 16
            self.prog[eng].append((waits if first else [], fn, (sname, 16)))
            first = False
        dep = (sname, self.dma_cum[si])
        self._commit(dep, reads, writes)

    def final_wait_all(self, eng='sp'):
        deps = list(self.last_w.values())
        for lst in self.readers.values():
            deps.extend(lst)
        waits = self._waits_for(eng, deps)
        self.prog[eng].append((waits, None, None))

    def emit(self):
        nc = self.nc
        sems = {}
        with contextlib.ExitStack() as st:
            for n in sorted(self.semnames):
                sems[n] = st.enter_context(nc.semaphore('s_' + n))
            block = st.enter_context(nc.Block())

            def run(engname):
                def body(engobj):
                    for waits, fn, inc in self.prog[engname]:
                        for s, v in waits:
                            engobj.wait_ge(sems[s], v)
                        if fn is None:
                            continue
                        ins = fn(engobj)
                        ins.then_inc(sems[inc[0]], inc[1])
                return body

            block.tensor(run('pe'))
            block.scalar(run('act'))
            block.vector(run('dve'))
            block.gpsimd(run('pool'))
            block.sync(run('sp'))


CST = {}


def _cst_layout():
    off = 0
    lay = {}
    for name, w in [('ident', 128), ('UiP', 128), ('UiH', 128), ('UiS', 128), ('UsP', 128), ('UsS', 128),
                    ('UsTP', 128), ('UsTS', 128), ('cos', NT * 32), ('sin', NT * 32), ('gqP', 8), ('gqS', 8),
                    ('geP', 2), ('geS', 2), ('bindH', 4), ('bindS', 16), ('headind', 2), ('bones', 128)]:
        lay[name] = (off, w)
        off += w
    return lay, off


CL, NCST = _cst_layout()


def make_consts():
    c = np.zeros((128, NCST), np.float32)

    def put(name, arr):
        o, w = CL[name]
        c[:, o:o + w] = np.asarray(arr, np.float32).reshape(128, w)

    j = np.arange(128)[:, None]
    i = np.arange(128)[None, :]
    put('ident', (j == i))
    for nm, bl in [('P', 128), ('H', 32), ('S', 8)]:
        same = (j // bl) == (i // bl)
        put('Ui' + nm, same & (j <= i))
        if nm != 'H':
            put('Us' + nm, same & (j < i))
            put('UsT' + nm, (same & (j < i)).T)
    half = 32
    inv = 10000.0 ** (-np.arange(half, dtype=np.float32) / half)
    cos = np.zeros((128, NT, 32), np.float32)
    sin = np.zeros((128, NT, 32), np.float32)
    for t in range(NT):
        if t < 16:
            pos = (t * 128 + np.arange(128)).astype(np.float32)
        else:
            pos = (PAST + np.arange(128) % 8).astype(np.float32)
        ang = pos[:, None] * inv[None, :]
        cos[:, t] = np.cos(ang)
        sin[:, t] = np.sin(ang)
    put('cos', cos)
    put('sin', sin)
    gam = 1.0 - 2.0 ** (-5.0 - np.arange(4, dtype=np.float64))
    for nm, bl in [('P', 128), ('S', 8)]:
        pib = (np.arange(128) % bl)[:, None].astype(np.float64)
        gq = np.concatenate([gam[None, :] ** (pib + 1), (gam[None, :] ** (-(pib + 1))) * (64 ** -0.5)], axis=1)
        put('gq' + nm, gq)
        ge = np.zeros((128, 2))
        for p in range(128):
            for hp in range(2):
                ge[p, hp] = gam[2 * hp + p // 64] ** bl
        put('ge' + nm, ge)
    put('bindH', (np.arange(128)[:, None] // 32) == np.arange(4)[None, :])
    put('bindS', (np.arange(128)[:, None] // 8) == np.arange(16)[None, :])
    put('headind', (np.arange(128)[:, None] // 64) == np.arange(2)[None, :])
    put('bones', (j // 64) == (i // 64))
    bH = ((np.arange(128)[None, :] // 32) == np.arange(4)[:, None]).astype(np.float32).reshape(-1)
    bS = ((np.arange(128)[None, :] // 8) == np.arange(16)[:, None]).astype(np.float32).reshape(-1)
    return c, bH, bS


def build_nc(TILES=tuple(range(NT)), NLAYERS=2, LIMIT=None, MIXSEL=None):
    LASTP = max([t for t in TILES if t < 16] + [-1])
    nc = bass.Bass("TRN2", target_bir_lowering=False)
    S = Sched(nc)
    S.limit = LIMIT

    def din(name, shape):
        return nc.dram_tensor(name, list(shape), F32, kind="ExternalInput").ap()

    def dout(name, shape):
        return nc.dram_tensor(name, list(shape), F32, kind="ExternalOutput").ap()

    x_p = din("x_p", [2048, D]); x_s = din("x_s", [128, D]); mem_p = din("mem_p", [256, D])
    st_ret = din("st_ret", [2, 16, 4, 64, 64]); st_hgrn = din("st_hgrn", [2, 16, 4, 64, 64])
    st_gla = din("st_gla", [2, 16, 4, 32, 64]); st_rwkv = din("st_rwkv", [2, 16, 4, 64, 64])
    st_shift = din("st_shift", [2, 16, 1088])
    cmk = din("cmk", [2, 16, 256, 256]); cmv = din("cmv", [2, 16, 256, 256])
    norm_mix = din("norm_mix", [2, D]); w_in = din("w_in", [2, D, DIN])
    lb_logits = din("hgrn_lb_logits", [2, 256]); hgrn_norm = din("hgrn_norm", [2, 64])
    gla_wg2 = din("gla_w_gate2", [2, 16, 128]); gla_bg = din("gla_b_gate", [2, 128]); gla_norm = din("gla_norm", [2, 64])
    rw_mu = din("rwkv_mu", [2, 1088]); rw_w0 = din("rwkv_w0", [2, 256]); rw_w2 = din("rwkv_w2", [2, 32, 256])
    rw_a0 = din("rwkv_a0", [2, 256]); rw_a2 = din("rwkv_a2", [2, 32, 256])
    rw_v0 = din("rwkv_v0", [1, 256]); rw_v1 = din("rwkv_v1", [1, 256, 32]); rw_v2 = din("rwkv_v2", [1, 32, 256])
    rw_kk = din("rwkv_k_k", [2, 256]); rw_ka = din("rwkv_k_a", [2, 256]); rw_rk = din("rwkv_r_k", [2, 256])
    rw_gg = din("rwkv_gn_g", [2, 256]); rw_gb = din("rwkv_gn_b", [2, 256])
    w_out = din("w_out", [2, D, D]); norm_x = din("norm_x", [2, D]); wq_x = din("wq_x", [2, D, 256])
    wo_x = din("wo_x", [2, 256, D]); norm_mem = din("norm_mem", [2, D]); wk_x = din("wk_x", [2, D, 256])
    wv_x = din("wv_x", [2, D, 256]); norm_f = din("norm_f", [D])
    cst_d = din("cst", [128, NCST]); bindH_d = din("bindH_row", [4 * 128]); bindS_d = din("bindS_row", [16 * 128])

    y_p = dout("y_p", [2048, D]); y_s = dout("y_s", [128, D])
    o_ret_p = dout("ret_p", [2, 4, 64, 64]); o_ret_s = dout("ret_s", [2, 16, 4, 64, 64])
    o_hgrn_p = dout("hgrn_p", [2, 4, 64, 64]); o_hgrn_s = dout("hgrn_s", [2, 16, 4, 64, 64])
    o_gla_p = dout("gla_p", [2, 4, 32, 64]); o_gla_s = dout("gla_s", [2, 16, 4, 32, 64])
    o_rwkv_p = dout("rwkv_p", [2, 4, 64, 64]); o_rwkv_s = dout("rwkv_s", [2, 16, 4, 64, 64])
    o_shift_p = dout("shift_p", [2, 1088]); o_shift_s = dout("shift_s", [2, 16, 1088])
    o_mk = dout("mk_p", [2, 256, 256]); o_mv = dout("mv_p", [2, 256, 256])
    xs_d = nc.dram_tensor("xs_scratch", [NT * 128, D], F32, kind="Internal").ap()
    hts_d = nc.dram_tensor("hts_scratch", [NT * 128, D], BF16, kind="Internal").ap()
    xacc_d = nc.dram_tensor("xacc_scratch", [NT * 128, D], F32, kind="Internal").ap()
    vfs_d = nc.dram_tensor("vfs_scratch", [NT * 128, 256], F32, kind="Internal").ap()

    def sb(name, shape, dt=F32):
        return nc.alloc_sbuf_tensor("sb_" + name, list(shape), dt)

    win = sb("win", [128, 8, 1088], BF16)
    WB = [0]
    wout = sb("wout", [128, 8, D], BF16)
    wq = sb("wq", [128, 8, 256], BF16)
    wo = sb("wo", [128, 2, D], BF16)
    xt = sb("xt", [128, 2, D])
    xa = sb("xa", [128, 1, D])
    cst = sb("cst", [128, NCST])
    identb = sb("identb", [128, 128], BF16)
    mask4 = {T: sb("mask4" + T, [128, 4, 128], BF16) for T in 'PHS'}
    mAB = {T: sb("mAB" + T, [128, 256], BF16) for T in 'PS'}
    mAK = {T: sb("mAK" + T, [128, 256], BF16) for T in 'PS'}
    mABt = {T: sb("mABt" + T, [128, 128], BF16) for T in 'PS'}
    bindFM = {'H': sb("bindFMH", [128, 4, 128], BF16), 'S': sb("bindFMS", [128, 16, 128], BF16)}
    headindb = sb("headindb", [128, 2], BF16)
    bonesb = sb("bonesb", [128, 128], BF16)
    nrow = sb("nrow", [128, 2, D])
    NPB = 64 + 64 + 128 + 256 * 6
    prmb = sb("prmb", [128, NPB])
    PB = {}
    o = 0
    for nm, w in [('hnorm', 64), ('gnorm', 64), ('glab', 128), ('lb', 256), ('oml', 256), ('w0', 256), ('v0', 256),
                  ('gng', 256), ('gnb', 256)]:
        PB[nm] = (o, w); o += w
    prmp = sb("prmp", [128, 40])
    PP = {'na0': 30, 'lb': 0, 'oml': 2, 'noml': 4, 'a0': 6, 'kk': 8, 'ka': 10, 'omka': 12, 'rk': 14, 'mu': 16, 'l0': 26, 'l1': 28}
    wg2 = sb("wg2", [16, 128], BF16); w2 = sb("w2", [32, 256], BF16); a2 = sb("a2", [32, 256], BF16)
    v1 = sb("v1", [128, 2, 32], BF16); v2 = sb("v2", [32, 256], BF16)
    hnb = sb("hnb", [128, D], BF16)
    hT = sb("hT", [128, 2, 8, 128], BF16)
    HI = [0]
    st4 = sb("st4", [128, 8])
    qk32 = sb("qk32", [128, 512]); rt1 = sb("rt1", [128, 8, 32]); rt2 = sb("rt2", [128, 8, 32])
    qkTM = sb("qkTM", [128, 512], BF16)
    qT = sb("qT", [128, 2, 128], BF16); kT = sb("kT", [128, 2, 128], BF16)
    kTM = sb("kTM", [128, 256], BF16); bTM = sb("bTM", [128, 256], BF16)
    vb = sb("vb", [128, 256], BF16)
    gsil = sb("gsil", [128, 256])
    sTm = sb("sTm", [128, 4, 128], BF16)
    Qblk = sb("Qblk", [128, 1, 16, 128], BF16)
    SAblk = Qblk[:].rearrange("p a b c -> p (a b c)").rearrange("p (h s d) -> p h s d", h=2, s=16)
    Vblk = sb("Vblk", [128, 2, 16, 64], BF16)
    of32 = sb("of32", [128, 256]); osq = sb("osq", [128, 256]); ofin = sb("ofin", [128, D], BF16)
    oT = sb("oT", [128, 8, 128], BF16)
    ffm = sb("ffm", [128, 4, 128])
    ffm2 = sb("ffm2", [128, 4, 128])
    lfTM = sb("lfTM", [128, 256])
    Eb = sb("Eb", [128, 2, 128]); Einv = sb("Einv", [128, 2, 128]); Ecx = sb("Ecx", [128, 2, 128])
    Eend = sb("Eend", [128, 2, 16])
    glT = sb("glT", [32, 128], BF16); alT = sb("alT", [32, 128], BF16)
    _SfP = sb("SfP", [128, 2, 5, 64]); _SbP = sb("SbP", [128, 2, 4, 128], BF16); SbP_ = _SbP
    SfP = {m: _SfP for m in ('ret', 'hgrn', 'gla', 'rwkv')}
    SbP = {m: _SbP for m in ('ret', 'hgrn', 'gla', 'rwkv')}
    SfS = sb("SfS", [128, 2, 16, 64]); SbS = sb("SbS", [128, 2, 16, 128], BF16); SnS = SfS
    qTm = sb("qTm", [128, 4, 128], BF16); arTm = sb("arTm", [128, 2, 2, 2, 128], BF16); qxTm = sb("qxTm", [128, 2, 2, 128], BF16)
    stmp = sb("stmp", [128, 64])
    Sbdn = sb("Sbdn", [128, 8, 128]); stg = sb("stg", [128, 8, 64]); IIf = sb("IIf", [128, 64])
    pbuf = [sb("pbuf%d" % i, [128, 10, 129]) for i in range(2)]
    psf = sb("psf", [128, 10, 128])
    wstage = psf[:].rearrange("p a b -> p (a b)")
    shT = sb("shT", [128, 10, 16])
    aT = sb("aT", [128, 2, 128]); kkT = sb("kkT", [128, 2, 128]); k2T = sb("k2T", [128, 2, 128])
    arT = sb("arT", [128, 2, 2, 128], BF16)
    btT = sb("btT", [128, 2, 128], BF16); ktT = sb("ktT", [128, 2, 128], BF16)
    rkb = sb("rkb", [128, 2, 128], BF16)
    Aab = sb("Aab", [128, 4, 256], BF16); Aak = sb("Aak", [128, 4, 256], BF16)
    Mi = [sb("Mi%d" % i, [128, 4, 128], BF16) for i in range(2)]
    MiT = [sb("MiT%d" % i, [128, 4, 128], BF16) for i in range(2)]
    Pm = [sb("Pm%d" % i, [128, 4, 128], BF16) for i in range(2)]
    XT = sb("XT", [128, 256], BF16); SAT = sb("SAT", [128, 256], BF16)
    vTM32 = sb("vTM32", [128, 256]); vmix = sb("vmix", [128, 256]); l1T = sb("l1T", [32, 128], BF16)
    bsum = sb("bsum", [128, 4]); lwn = sb("lwn", [128, 256]); twl = sb("twl", [32, 128], BF16)
    vTb = sb("vTb", [128, 2, 128], BF16)
    mkT = sb("mkT", [128, 2, 2, 256], BF16)
    mvb = sb("mvb", [128, 2, 2, 256], BF16)
    memb = sb("memb", [128, 2, 256], BF16)
    mkTs = sb("mkTs", [128, 2, 256], BF16)
    mvs = sb("mvs", [128, 2, 256], BF16)
    pT = sb("pT", [128, 2, 4, 128], BF16)
    oxT = sb("oxT", [128, 2, 128], BF16)
    rsb = sb("rsb", [128, 4, 128])
    mstage = of32
    onesb = sb("onesb", [128, 128], BF16)
    class ParBuf:
        def __init__(self, bufs, sel=None):
            self.bufs = bufs
            self.sel = S.par if sel is None else sel

        def __getitem__(self, idx):
            return self.bufs[self.sel[0]][idx]
    ovl_bf = xt[:].rearrange("p a b -> p (a b)").bitcast(BF16)
    ovl_f = xt[:].rearrange("p a b -> p (a b)")
    _o = [0]

    def carve_bf(shape):
        n = int(np.prod(shape[1:]))
        v = ovl_bf[:, _o[0]:_o[0] + n]
        _o[0] += n
        if len(shape) == 3:
            v = v.rearrange("p (a b) -> p a b", b=shape[2])
        elif len(shape) == 4:
            v = v.rearrange("p (a b c) -> p a b c", b=shape[2], c=shape[3])
        return v

    def carve_f(shape):
        assert _o[0] % 2 == 0
        n = int(np.prod(shape[1:]))
        v = ovl_f[:, _o[0] // 2:_o[0] // 2 + n]
        _o[0] += 2 * n
        if len(shape) == 3:
            v = v.rearrange("p (a b) -> p a b", b=shape[2])
        return v
    pbo = pbuf[0][:].rearrange("p a b -> p (a b)")
    _o2 = [0]

    def carve2(shape, dt):
        n = int(np.prod(shape[1:]))
        if dt == BF16:
            v = pbo.bitcast(BF16)[:, _o2[0]:_o2[0] + n]; _o2[0] += n
        else:
            assert _o2[0] % 2 == 0
            v = pbo[:, _o2[0] // 2:_o2[0] // 2 + n]; _o2[0] += 2 * n
        if len(shape) == 3:
            v = v.rearrange("p (a b) -> p a b", b=shape[2])
        return v
    qT_c = carve2([128, 2, 128], BF16); kT_c = carve2([128, 2, 128], BF16); kTM_c = carve2([128, 256], BF16); vb_c = carve2([128, 256], BF16); gsil_c = carve2([128, 256], F32)
    qT = ParBuf([qT[:], carve_bf([128, 2, 128])]); kT = ParBuf([kT[:], carve_bf([128, 2, 128])])
    kTM = ParBuf([kTM[:], carve_bf([128, 256])]); bTM = ParBuf([bTM[:], carve_bf([128, 256])])
    vb = ParBuf([vb[:], carve_bf([128, 256])])
    arT = ParBuf([arT[:], carve_bf([128, 2, 2, 128])]); btT = ParBuf([btT[:], carve_bf([128, 2, 128])]); ktT = ParBuf([ktT[:], carve_bf([128, 2, 128])])
    PO = [0]; PG = [0]; PX = [0]
    pb1f = pbuf[1][:].rearrange("p a b -> p (a b)")
    pT = ParBuf([pT[:], pb1f.bitcast(BF16)[:, 0:1024].rearrange("p (a b c) -> p a b c", a=2, b=4)], PX)
    rsb = ParBuf([rsb[:], pb1f[:, 512:1024].rearrange("p (a b) -> p a b", b=128)], PX)
    sfo = SfS[:].rearrange("p a b c -> p (a b c)")
    gsil = ParBuf([gsil[:], sfo[:, 0:256], sfo[:, 256:512]], PG); vTM32 = ParBuf([vTM32[:], carve_f([128, 256])])
    of32 = ParBuf([of32[:], sfo[:, 512:768]], PO)
    bon = ParBuf([sfo[:, 768:1024], sfo[:, 1024:1280]], PO)
    PIPED = [False]
    Eend = ParBuf([Eend[:], carve_f([128, 2, 16])]); bsum = ParBuf([bsum[:], carve_f([128, 4])])
    qT.bufs.append(qT_c); kT.bufs.append(kT_c); kTM.bufs.append(kTM_c); vb.bufs.append(vb_c)
    S.parsel = {k: S.par for k in ('qT', 'kT', 'kTM', 'bTM', 'vb', 'arT', 'btT', 'ktT', 'vTM32', 'Eend', 'bsum')}
    S.parsel['pT'] = PX; S.parsel['rsb'] = PX
    S.parsel['gsil'] = PG; S.parsel['of32'] = PO; S.parsel['bon'] = PO
    print("sbuf bytes remaining", nc.sbuf_bytes_remaining, "overlay used (bf16 elems)", _o[0])

    pb = [nc.alloc_psum_tensor("pb%d" % i, [128, 512], F32) for i in range(8)]
    pk = ['pb%d' % i for i in range(8)]

    def C(name):
        o_, w_ = CL[name]
        return cst[:, o_:o_ + w_]

    def PBr(name):
        o_, w_ = PB[name]
        return prmb[:, o_:o_ + w_]

    def mm(out, lhsT, rhs, start, stop, reads, writes, tp=None):
        kw = {}
        if tp is not None:
            kw['tile_position'] = tp
        S.op('pe', lambda e: e.matmul(out, lhsT=lhsT, rhs=rhs, start=start, stop=stop, **kw), reads, writes)

    def tr(out, in_, ident, reads, writes):
        S.op('pe', lambda e: e.transpose(out=out, in_=in_, identity=ident), reads, writes)

    def act(out, in_, func, reads, writes, scale=1.0, bias=0.0, accum=None):
        kw = {}
        if accum is not None:
            kw['accum_out'] = accum
        S.op('act', lambda e: e.activation(out=out, in_=in_, func=func, scale=scale, bias=bias, **kw), reads, writes)

    P2D = os.environ.get('POOL2DVE', '0') == '1'

    def tt(eng, out, in0, in1, op, reads, writes):
        if P2D and eng == 'pool':
            eng = 'dve'
        S.op(eng, lambda e: e.tensor_tensor(out=out, in0=in0, in1=in1, op=op), reads, writes)

    def ts(eng, out, in0, s1, s2, op0, op1, reads, writes):
        S.op(eng, lambda e: e.tensor_scalar(out=out, in0=in0, scalar1=s1, scalar2=s2, op0=op0, op1=op1), reads, writes)

    def stt(out, in0, scalar, in1, op0, op1, reads, writes):
        S.op('dve', lambda e: e.scalar_tensor_tensor(out=out, in0=in0, scalar=scalar, in1=in1, op0=op0, op1=op1), reads, writes)

    def sigm(out, in_, reads, wkey, scale=1.0, nbias=0.0, eng2='dve'):
        act(out, in_, AF.Exp, reads, [wkey], scale=-scale, bias=nbias)
        act(out, out, AF.Ln, [wkey], [wkey], bias=1.0)
        act(out, out, AF.Exp, [wkey], [wkey], scale=-1.0)

    def rsq(out, in_, reads, wkey, scale, eps):
        act(out, in_, AF.Ln, reads, [wkey], scale=scale, bias=eps)
        act(out, out, AF.Exp, [wkey], [wkey], scale=-0.5)

    def cp(eng, out, in_, reads, writes):
        if eng == 'act':
            act(out, in_, AF.Copy, reads, writes)
        else:
            S.op(eng, lambda e: e.tensor_copy(out=out, in_=in_), reads, writes)

    def red(out, in_, reads, writes):
        S.op('dve', lambda e: e.tensor_reduce(out=out, in_=in_, axis=AX.X, op=ALU.add), reads, writes)

    def recip(out, in_, reads, writes):
        S.op('dve', lambda e: e.reciprocal(out=out, in_=in_), reads, writes)

    def dma(out, in_, reads, writes, eng='sp'):
        S.dma(lambda e: e.dma_start(out=out, in_=in_, allow_slow_non_contiguous=True), reads, writes, eng=eng)

    pbb = [p[:].bitcast(BF16) for p in pb]

    def load_cast(dst, src, dkey, alt=None):
        shp = dst.shape
        p = shp[0]
        if alt == 1:
            stg_, skey_, eng_ = SfS[:].rearrange("p a b c -> p (a b c)"), 'SfS', 'act'
        elif alt == 0:
            stg_, skey_, eng_ = wstage, 'psf', 'dve'
        else:
            stg_, skey_, eng_ = wstage, 'psf', 'pool'
        if len(shp) == 2:
            stv = stg_[0:p, 0:shp[1]]
        else:
            stv = stg_[0:p, 0:shp[1] * shp[2]].rearrange("p (a b) -> p a b", b=shp[2])
        dma(stv, src, [], [skey_])
        cp(eng_, dst, stv, [skey_], [dkey])

    S.op('pool', lambda e: e.memset(Sbdn[:].rearrange('p a b -> p (a b)'), 0.0), [], ['Sbdn'])
    S.op('pool', lambda e: e.memset(qxTm[:].rearrange('p a b c -> p (a b c)'), 0.0), [], ['qxTm'])
    S.op('pool', lambda e: e.memset(shT[:].rearrange('p a b -> p (a b)'), 0.0), [], ['shT'])
    S.op('pool', lambda e: e.memset(prmp[:], 0.0), [], ['prmp'])
    for i_ in range(2):
        S.op('pool', lambda e, i_=i_: e.memset(pbuf[i_][:].rearrange('p a b -> p (a b)'), 0.0), [], ['pbuf%d' % i_])
    for i_ in range(8):
        S.op('dve', lambda e, i_=i_: e.memset(pb[i_][:], 0.0), [], [pk[i_]])
    dma(cst[:], cst_d, [], ['cst'])
    load_cast(bindFM['H'][:].rearrange("p a b -> p (a b)"), bindH_d.partition_broadcast(128), 'bindFMH')
    for hf in range(2):
        load_cast(bindFM['S'][:].rearrange("p a b -> p (a b)")[:, hf * 1024:(hf + 1) * 1024], bindS_d[hf * 1024:(hf + 1) * 1024].partition_broadcast(128), 'bindFMS')
    for l in range(2):
        for kq in range(2):
            load_cast(wout[:, 4 * kq:4 * kq + 4, (2 * l) * 256:(2 * l + 1) * 256], wk_x[l, kq * 512:(kq + 1) * 512, :].rearrange("(kc p) n -> p kc n", p=128), 'wout', alt=0)
            load_cast(wout[:, 4 * kq:4 * kq + 4, (2 * l + 1) * 256:(2 * l + 2) * 256], wv_x[l, kq * 512:(kq + 1) * 512, :].rearrange("(kc p) n -> p kc n", p=128), 'wout', alt=1)
    cp('dve', identb[:], C('ident'), ['cst'], ['identb'])
    for hh_ in range(2):
        cp('dve', IIf[hh_ * 64:(hh_ + 1) * 64, :], C('ident')[hh_ * 64:(hh_ + 1) * 64, hh_ * 64:(hh_ + 1) * 64], ['cst'], ['IIf'])
    cp('dve', headindb[:], C('headind'), ['cst'], ['headindb'])
    cp('dve', bonesb[:], C('bones'), ['cst'], ['bonesb'])
    S.op('dve', lambda e: e.memset(onesb[:], 1.0), [], ['onesb'])
    for T in 'PHS':
        for h in range(4):
            cp('dve', mask4[T][:, h, :], C('Ui' + T), ['cst'], ['mask4' + T])
    for T in 'PS':
        act(mAB[T][:, 0:128], C('Us' + T), AF.Copy, ['cst'], ['mAB' + T], scale=-1.0)
        act(mAB[T][:, 128:256], C('Ui' + T), AF.Copy, ['cst'], ['mAB' + T], scale=-1.0)
        act(mAK[T][:, 0:128], C('Us' + T), AF.Copy, ['cst'], ['mAK' + T])
        act(mAK[T][:, 128:256], C('Ui' + T), AF.Copy, ['cst'], ['mAK' + T])
        act(mABt[T][:], C('UsT' + T), AF.Copy, ['cst'], ['mABt' + T], scale=-1.0)

    def load_layer_weights(l):
        for kq in range(2):
            load_cast(wq[:, 4 * kq:4 * kq + 4, :], wq_x[l, kq * 512:(kq + 1) * 512, :].rearrange("(kc p) n -> p kc n", p=128), 'wq', alt=0)
            load_cast(wo[:, kq, :], wo_x[l, kq * 128:(kq + 1) * 128, :], 'wo', alt=1)

    def load_wout(l):
        for kc in range(8):
            load_cast(wout[:, kc, :], w_out[l, kc * 128:(kc + 1) * 128, :], 'wout', alt=kc % 2)


    def norm_T(src, src_key, grow, grow_key, eps=1e-6):
        S.op('dve', lambda e: e.scalar_tensor_tensor(out=hnb[:], in0=src, scalar=1.0, in1=src, op0=ALU.mult, op1=ALU.mult, accum_out=st4[:, 0:1]), [src_key], ['hnb', 'st4'])
        rsq(st4[:, 2:3], st4[:, 0:1], ['st4'], 'st4', 1.0 / D, eps)
        stt(hnb[:], src, st4[:, 2:3], grow, ALU.mult, ALU.mult, [src_key, 'st4', grow_key, 'hnb'], ['hnb'])
        for c in range(8):
            tr(pbb[4][:, c * 128:(c + 1) * 128], hnb[:, c * 128:(c + 1) * 128], identb[:], ['hnb', 'identb'], [pk[4]])
        cp('act', hT[:, HI[0], :, :].rearrange("p a b -> p (a b)"), pbb[4][:, 0:1024], [pk[4]], ['hT%d' % HI[0]])

    for l in range(2):
        dma(nrow[:, 0, :], norm_mem[l].partition_broadcast(128), [], ['nrow0'])
        for mc in range(2):
            memt = psf[:, 0:8, :].rearrange("p a b -> p (a b)")
            dma(memt, mem_p[mc * 128:(mc + 1) * 128, :], [], ['psf'])
            norm_T(memt, 'psf', nrow[:, 0, :], 'nrow0')
            for which in range(2):
                col0 = (2 * l + which) * 256
                for kc in range(8):
                    mm(pb[2][:, 0:256], hT[:, HI[0], kc, :], wout[:, kc, col0:col0 + 256], kc == 0, kc == 7, ['hT%d' % HI[0], 'wout'], [pk[2]])
                cp('act', mstage[:], pb[2][:, 0:256], [pk[2]], ['of32'])
                if which == 1:
                    cp('dve', mvb[:, l, mc, :], pb[2][:, 0:256], [pk[2]], ['mvb'])
                dma((o_mk if which == 0 else o_mv)[l, mc * 128:(mc + 1) * 128, :], mstage[:], ['of32'], [])
            for hp in range(2):
                col0 = (2 * l) * 256 + hp * 128
                for kc in range(8):
                    mm(pb[3][:, hp * 128:(hp + 1) * 128], wout[:, kc, col0:col0 + 128], hT[:, HI[0], kc, :], kc == 0, kc == 7, ['hT%d' % HI[0], 'wout'], [pk[3]])
            cp('dve', mkT[:, l, :, mc * 128:(mc + 1) * 128], pb[3][:, 0:256].rearrange("p (a b) -> p a b", a=2), [pk[3]], ['mkT'])
    load_wout(0)

    def load_params(l):
        for nm, src in [('hnorm', hgrn_norm[l]), ('gnorm', gla_norm[l]), ('glab', gla_bg[l]), ('w0', rw_w0[l]),
                        ('v0', rw_v0[0]), ('gng', rw_gg[l]), ('gnb', rw_gb[l])]:
            dma(PBr(nm), src.partition_broadcast(128), [], ['prmb'])
        dma(qk32[:, 0:256], lb_logits[0].partition_broadcast(128), [], ['qk32'])
        dma(qk32[:, 256:512], lb_logits[1].partition_broadcast(128), [], ['qk32'])
        tt('dve', qk32[:, 0:256], qk32[:, 256:512], qk32[:, 0:256], ALU.subtract, ['qk32'], ['qk32'])
        sigm(PBr('lb'), qk32[:, 0:256], ['qk32'], 'prmb')
        ts('dve', PBr('lb'), PBr('lb'), float(l), None, ALU.mult, ALU.bypass, ['prmb'], ['prmb'])
        ts('dve', PBr('oml'), PBr('lb'), -1.0, 1.0, ALU.mult, ALU.add, ['prmb'], ['prmb'])
        def pp(col, src256):
            dma(prmp[:, col:col + 2], src256.rearrange("(g p) -> p g", p=128), [], ['prmp'])
        pp(PP['l0'], lb_logits[0]); pp(PP['l1'], lb_logits[1])
        pp(PP['a0'], rw_a0[l]); pp(PP['kk'], rw_kk[l]); pp(PP['ka'], rw_ka[l]); pp(PP['rk'], rw_rk[l])
        for gi, (c0, w) in enumerate(RW_GROUPS):
            dma(prmp[0:w, PP['mu'] + gi:PP['mu'] + gi + 1], rw_mu[l, c0:c0 + w].rearrange("(p o) -> p o", o=1), [], ['prmp'])
        tt('dve', prmp[:, PP['lb']:PP['lb'] + 2], prmp[:, PP['l1']:PP['l1'] + 2], prmp[:, PP['l0']:PP['l0'] + 2], ALU.subtract, ['prmp'], ['prmp'])
        sigm(prmp[:, PP['lb']:PP['lb'] + 2], prmp[:, PP['lb']:PP['lb'] + 2], ['prmp'], 'prmp')
        ts('dve', prmp[:, PP['lb']:PP['lb'] + 2], prmp[:, PP['lb']:PP['lb'] + 2], float(l), None, ALU.mult, ALU.bypass, ['prmp'], ['prmp'])
        ts('dve', prmp[:, PP['oml']:PP['oml'] + 2], prmp[:, PP['lb']:PP['lb'] + 2], -1.0, 1.0, ALU.mult, ALU.add, ['prmp'], ['prmp'])
        ts('dve', prmp[:, PP['noml']:PP['noml'] + 2], prmp[:, PP['oml']:PP['oml'] + 2], -1.0, None, ALU.mult, ALU.bypass, ['prmp'], ['prmp'])
        ts('dve', prmp[:, PP['omka']:PP['omka'] + 2], prmp[:, PP['ka']:PP['ka'] + 2], -1.0, 1.0, ALU.mult, ALU.add, ['prmp'], ['prmp'])
        ts('dve', prmp[:, PP['na0']:PP['na0'] + 2], prmp[:, PP['a0']:PP['a0'] + 2], -1.0, None, ALU.mult, ALU.bypass, ['prmp'], ['prmp'])
        load_cast(wg2[:], gla_wg2[l], 'wg2', alt=0)
        load_cast(w2[:], rw_w2[l], 'w2', alt=1)
        load_cast(a2[:], rw_a2[l], 'a2', alt=0)
        load_cast(v1[:], rw_v1[0].rearrange("(g p) r -> p g r", p=128), 'v1', alt=1)
        load_cast(v2[:], rw_v2[0], 'v2', alt=0)

    def sbd_view(sample, G):
        if sample:
            return SbS[:] if G == 2 else SbS[:].rearrange("p g s w -> p (g s w)").rearrange("p (g s w) -> p g s w", g=1, s=16)
        return SbP_[:] if G == 2 else SbP_[:].rearrange("p g s w -> p (g s w)").rearrange("p (g s w) -> p g s w", g=1, s=4)

    def cast_state(sample, G, dk, g, s0, s1, srcSf, skey, sbkey):
        hpg = 4 // G
        Sbd = sbd_view(sample, G)
        for hh in range(hpg):
            po = hh * dk
            cp('act', Sbd[po:po + dk, g, s0:s1, hh * 64:(hh + 1) * 64], srcSf[po:po + dk, g, s0:s1, :], [skey], [sbkey])

    def lin_core(l, t, m, G, dk, ttype, gstate):
        nblk = {'P': 1, 'H': 4, 'S': 16}[ttype]
        sample = (ttype == 'S')
        hpg = 4 // G
        W = hpg * 64
        qTm_v = qTm[:].rearrange("p (g h) t -> p g h t", g=G)
        Sbd = sbd_view(sample, G)
        for g in range(G):
            for hh in range(hpg):
                po = hh * dk
                cp('act' if hh % 2 == 0 else 'dve', qTm_v[po:po + dk, g, hh, :], qT[po:po + dk, g, :], ['qT'], ['qTm'])
        for h in range(4):
            g, hh = h // hpg, h % hpg
            mm(pb[5][:, h * 128:(h + 1) * 128], kT[:, g, :], qTm_v[:, g, hh, :], True, True, ['kT', 'qTm'], [pk[5]])
        tt('dve', sTm[:].rearrange("p a b -> p (a b)"), pb[5][:, 0:512], mask4[ttype][:].rearrange("p a b -> p (a b)"), ALU.mult, [pk[5], 'mask4' + ttype], ['sTm'])
        if sample:
            Sf, skey, sbkey = SfS, 'SfS', 'SbS'
        else:
            Sf, skey, sbkey = SfP[m], 'SfP', 'SbP'

        def Eap(g, s):
            if m == 'ret':
                return C('ge' + ('S' if sample else 'P'))[:, g:g + 1]
            return Eend[:, g, s:s + 1]
        chunks = [(c0, min(nblk, c0 + 8)) for c0 in range(0, nblk, 8)]
        ubank = [7, 5]
        for g in range(G):
            for pr in range(hpg // 2):
                h0 = g * hpg + 2 * pr
                if nblk > 1:
                    bind = C('bind' + ttype)
                    tt('pool', Vblk[:, :, 0:nblk, :], vb[:, h0 * 64:(h0 + 2) * 64].rearrange("p (h d) -> p h d", d=64).unsqueeze(2).to_broadcast([128, 2, nblk, 64]),
                       bind.unsqueeze(1).unsqueeze(3).to_broadcast([128, 2, nblk, 64]), ALU.mult, ['vb', 'cst'], ['Vblk'])
                for ci, (c0, c1) in enumerate(chunks):
                    ncol = (c1 - c0) * 64
                    for j in range(2):
                        h = h0 + j
                        po = (h % hpg) * dk
                        tp = (0, po) if po == 96 else None
                        rhs = Vblk[:, j, c0:c1, :].rearrange("p a b -> p (a b)") if nblk > 1 else vb[:, h * 64:(h + 1) * 64]
                        bk = ubank[ci]
                        mm(pb[bk][po:po + dk, 0:ncol], kTM[:, g * 128 + po:g * 128 + po + dk], rhs, True, True, ['kTM', 'Vblk' if nblk > 1 else 'vb'], [pk[bk]], tp=tp)
            for ci, (c0, c1) in enumerate(chunks):
                ncol = (c1 - c0) * 64
                bk = ubank[ci]
                if sample:
                    tt('dve', Sf[:, g, c0:c1, :], pb[bk][:, 0:ncol].rearrange("p (a b) -> p a b", b=64), Sf[:, g, c0:c1, :], ALU.add, [pk[bk], skey], [skey])
                    if m == 'ret':
                        ts('dve', Sf[:, g, c0:c1, :], Sf[:, g, c0:c1, :], Eap(g, 0), None, ALU.mult, ALU.bypass, [skey, 'cst'], [skey])
                    else:
                        tt('pool', Sf[:, g, c0:c1, :], Sf[:, g, c0:c1, :], Eend[:, g, c0:c1].unsqueeze(2).to_broadcast([128, c1 - c0, 64]), ALU.mult, [skey, 'Eend'], [skey])
                else:
                    for s in range(nblk):
                        tt('dve', stmp[:], pb[bk][:, s * 64:(s + 1) * 64], Sf[:, g, s, :], ALU.add, [pk[bk], skey], ['stmp'])
                        ts('dve', Sf[:, g, s + 1, :], stmp[:], Eap(g, s), None, ALU.mult, ALU.bypass, ['stmp', 'Eend', 'cst'], [skey])
                    if nblk > 1:
                        cast_state(False, G, dk, g, 1, nblk, Sf, skey, sbkey)
        for g in range(G):
            if nblk > 1:
                tt('pool', Qblk[:, 0, 0:nblk, :], qT[:, g, :].unsqueeze(1).to_broadcast([128, nblk, 128]), bindFM[ttype][:, 0:nblk, :], ALU.mult, ['qT', 'bindFM' + ttype], ['Qblk'])
            for s in range(nblk):
                lhs = Qblk[:, 0, s, :] if nblk > 1 else qT[:, g, :]
                mm(pb[6][:, g * W:(g + 1) * W], lhs, Sbd[:, g, s, 0:W], s == 0, False, ['Qblk' if nblk > 1 else 'qT', sbkey], [pk[6]])
            for hh in range(hpg):
                h = g * hpg + hh
                mm(pb[6][:, h * 64:(h + 1) * 64], sTm[:, h, :], vb[:, h * 64:(h + 1) * 64], False, hh == hpg - 1, ['sTm', 'vb'], [pk[6]])
        if not sample:
            for g in range(G):
                cp('pool', Sf[:, g, 0, :], Sf[:, g, nblk, :], [skey], [skey])
                cast_state(False, G, dk, g, 0, 1, Sf, skey, sbkey)

    def state_dram(m, l, which):
        src = {'ret': (st_ret, o_ret_p, o_ret_s), 'hgrn': (st_hgrn, o_hgrn_p, o_hgrn_s), 'gla': (st_gla, o_gla_p, o_gla_s),
               'rwkv': (st_rwkv, o_rwkv_p, o_rwkv_s)}[m]
        return src[which][l]

    def load_sample_state(m, l):
        d = state_dram(m, l, 0)
        if m == 'gla':
            dma(SfS[:, 0, :, :], d.rearrange("b h k v -> (h k) b v"), [], ['SfS'])
            cast_state(True, 1, 32, 0, 0, 16, SfS, 'SfS', 'SbS')
        else:
            for hp in range(2):
                dma(SfS[:, hp, :, :], d[:, 2 * hp:2 * hp + 2].rearrange("b h k v -> (h k) b v"), [], ['SfS'])
            for g in range(2):
                cast_state(True, 2, 64, g, 0, 16, SfS, 'SfS', 'SbS')

    def store_sample_state(m, l):
        d = state_dram(m, l, 2)
        if m == 'gla':
            dma(d.rearrange("b h k v -> (h k) b v"), SnS[:, 0, :, :], ['SfS'], [])
        else:
            for hp in range(2):
                dma(d[:, 2 * hp:2 * hp + 2].rearrange("b h k v -> (h k) b v"), SnS[:, hp, :, :], ['SfS'], [])

    def store_prompt_state(m, l):
        d = state_dram(m, l, 1)
        if m == 'gla':
            dma(d.rearrange("h k v -> (h k) v"), SfP[m][:, 0, 0, :], ['SfP'], [])
        else:
            for hp in range(2):
                dma(d[2 * hp:2 * hp + 2].rearrange("h k v -> (h k) v"), SfP[m][:, hp, 0, :], ['SfP'], [])

    def zero_prompt_states():
        S.op('dve', lambda e: e.memset(_SfP[:].rearrange("p a b c -> p (a b c)"), 0.0), [], ['SfP'])
        S.op('dve', lambda e: e.memset(_SbP[:].rearrange("p a b c -> p (a b c)"), 0.0), [], ['SbP'])
        S.op('pool', lambda e: e.memset(SbS[:].rearrange("p a b c -> p (a b c)"), 0.0), [], ['SbS'])
        S.op('pool', lambda e: e.memset(qTm[:].rearrange("p a b -> p (a b)"), 0.0), [], ['qTm'])
        S.op('pool', lambda e: e.memset(arTm[:].rearrange("p a b c d -> p (a b c d)"), 0.0), [], ['arTm'])

    def make_gate(gate_ap, gate_keys):
        sigm(gsil[:], gate_ap, gate_keys, 'gsil', eng2='pool')
        tt('dve', gsil[:], gsil[:], gate_ap, ALU.mult, ['gsil'] + list(gate_keys), ['gsil'])

    def o_evac():
        cp('act', of32[:], pb[6][:, 0:256], [pk[6]], ['of32'])

    def post_norm(kind, gate_ap, gate_keys, col0, gain_name=None, eps=1e-6, extra=None):
        o3 = of32[:].rearrange("p (h d) -> p h d", d=64)
        if kind == 'gn':
            red(st4[:, 0:4], o3, ['of32'], ['st4'])
            ts('dve', st4[:, 0:4], st4[:, 0:4], 1.0 / 64, None, ALU.mult, ALU.bypass, ['st4'], ['st4'])
            tt('dve', o3, o3, st4[:, 0:4].unsqueeze(2).to_broadcast([128, 4, 64]), ALU.subtract, ['of32', 'st4'], ['of32'])
        tt('pool', osq[:], of32[:], of32[:], ALU.mult, ['of32'], ['osq'])
        red(st4[:, 4:8], osq[:].rearrange("p (h d) -> p h d", d=64), ['osq'], ['st4'])
        rsq(st4[:, 4:8], st4[:, 4:8], ['st4'], 'st4', 1.0 / 64, eps)
        tt('dve', o3, o3, st4[:, 4:8].unsqueeze(2).to_broadcast([128, 4, 64]), ALU.mult, ['of32', 'st4'], ['of32'])
        if gain_name is not None:
            tt('pool', o3, o3, PBr(gain_name).unsqueeze(1).to_broadcast([128, 4, 64]), ALU.mult, ['of32', 'prmb'], ['of32'])
        if extra is not None:
            extra()
        if gate_ap is not None:
            make_gate(gate_ap, gate_keys)
        tt('dve', ofin[:, col0:col0 + 256], of32[:], gsil[:], ALU.mult, ['of32', 'gsil'], ['ofin'])

    def tile_type(t):
        return 'S' if t == 16 else 'P'

    def proj_TM(col_ranges, bank):
        o_ = 0
        for (c0, w) in col_ranges:
            for kc in range(8):
                mm(pb[bank][:, o_:o_ + w], hT[:, HI[0], kc, :], win[:, kc, c0 - WB[0]:c0 - WB[0] + w], kc == 0, kc == 7, ['hT%d' % HI[0], 'win'], [pk[bank]])
            o_ += w

    def proj_FM(groups, bank):
        for i, (c0, w) in enumerate(groups):
            for kc in range(8):
                mm(pb[bank][0:w, i * 128:(i + 1) * 128], win[:, kc, c0 - WB[0]:c0 - WB[0] + w], hT[:, HI[0], kc, :], kc == 0, kc == 7, ['hT%d' % HI[0], 'win'], [pk[bank]])

    def ret_fe(l, t):
        T = tile_type(t)
        proj_TM([(R0, 512)], 2)
        proj_TM([(R0 + 512, 512)], 3)
        cp('act', qk32[:], pb[2][:, 0:512], [pk[2]], ['qk32'])
        cp('act', vb[:], pb[3][:, 0:256], [pk[3]], ['vb'])
        S.begin_capture()
        x3 = qk32[:].rearrange("p (h d) -> p h d", d=64)
        cosb = C('cos')[:, t * 32:(t + 1) * 32].unsqueeze(1).to_broadcast([128, 8, 32])
        sinb = C('sin')[:, t * 32:(t + 1) * 32].unsqueeze(1).to_broadcast([128, 8, 32])
        tt('dve', rt1[:], x3[:, :, 0:32], cosb, ALU.mult, ['qk32', 'cst'], ['rt1'])
        tt('pool', rt2[:], x3[:, :, 32:64], sinb, ALU.mult, ['qk32', 'cst'], ['rt2'])
        tt('dve', rt1[:], rt1[:], rt2[:], ALU.subtract, ['rt1', 'rt2'], ['rt1'])
        tt('pool', rt2[:], x3[:, :, 0:32], sinb, ALU.mult, ['qk32', 'cst', 'rt2'], ['rt2'])
        tt('dve', x3[:, :, 32:64], x3[:, :, 32:64], cosb, ALU.mult, ['qk32', 'cst'], ['qk32'])
        tt('dve', x3[:, :, 32:64], x3[:, :, 32:64], rt2[:], ALU.add, ['qk32', 'rt2'], ['qk32'])
        cp('pool', x3[:, :, 0:32], rt1[:], ['rt1'], ['qk32'])
        gq = C('gq' + T).unsqueeze(2).to_broadcast([128, 8, 64])
        tt('dve', qkTM[:].rearrange("p (h d) -> p h d", d=64), x3, gq, ALU.mult, ['qk32', 'cst'], ['qkTM'])
        for i in range(4):
            tr(pbb[2][:, i * 128:(i + 1) * 128], qkTM[:, i * 128:(i + 1) * 128], identb[:], ['qkTM', 'identb'], [pk[2]])
        cp('act', qT[:].rearrange("p a b -> p (a b)"), pbb[2][:, 0:256], [pk[2]], ['qT'])
        cp('act', kT[:].rearrange("p a b -> p (a b)"), pbb[2][:, 256:512], [pk[2]], ['kT'])
        cp('pool', kTM[:], qkTM[:, 256:512], ['qkTM'], ['kTM'])
        sA = S.end_capture()
        S.begin_capture()
        make_gate(pb[3][:, 256:512], [pk[3]])
        sC = S.end_capture()
        S.replay(merge2(sA, sC))

    def ret_be(l, t):
        T = tile_type(t)
        if T == 'S':
            load_sample_state('ret', l)
        lin_core(l, t, 'ret', 2, 64, T, None)
        o_evac()

    def ret_be2(l, t):
        T = tile_type(t)
        post_norm('gn', None, [], 0, None, eps=1e-5)
        if T == 'S':
            store_sample_state('ret', l)

    def merge2(a, b):
        out = []
        ia = ib = 0
        na, nb = len(a), len(b)
        while ia < na or ib < nb:
            if ib >= nb or (ia < na and ia * nb <= ib * na):
                out.append(a[ia]); ia += 1
            else:
                out.append(b[ib]); ib += 1
        return out

    def mergeN(lists):
        out = lists[0]
        for x in lists[1:]:
            out = merge2(out, x)
        return out

    def hgrn_fe(l, t):
        T = 'S' if t == 16 else 'H'
        nblk = 16 if t == 16 else 4
        proj_FM([(H0, 128), (H0 + 128, 128), (H0 + 256, 128), (H0 + 384, 128)], 0)
        proj_TM([(H0 + 256, 512)], 2)
        proj_TM([(H0 + 768, 256)], 3)
        S.begin_capture()
        act(lfTM[:], pb[2][:, 0:256], AF.Exp, [pk[2]], ['lfTM'], scale=-1.0)
        tt('dve', qk32[:, 0:256], lfTM[:], PBr('lb'), ALU.mult, ['lfTM', 'prmb'], ['qk32'])
        act(qk32[:, 0:256], qk32[:, 0:256], AF.Ln, ['qk32'], ['qk32'], bias=1.0)
        act(lfTM[:], lfTM[:], AF.Ln, ['lfTM'], ['lfTM'], bias=1.0)
        tt('dve', lfTM[:], qk32[:, 0:256], lfTM[:], ALU.subtract, ['lfTM', 'qk32'], ['lfTM'])
        cp('act', vb[:], pb[2][:, 256:512], [pk[2]], ['vb'])
        for g in range(2):
            mm(pb[1][:, g * 128:(g + 1) * 128], lfTM[:, g * 128:(g + 1) * 128], C('Ui' + T), True, True, ['lfTM', 'cst'], [pk[1]])
        act(Eb[:].rearrange("p a b -> p (a b)"), pb[1][:, 0:256], AF.Exp, [pk[1]], ['Eb'])
        act(Einv[:].rearrange("p a b -> p (a b)"), pb[1][:, 0:256], AF.Exp, [pk[1]], ['Einv'], scale=-1.0)
        blen = 128 // nblk
        cp('pool', Eend[:, :, 0:nblk], Eb[:, :, blen - 1:128:blen], ['Eb'], ['Eend'])
        sA = S.end_capture()
        S.begin_capture()
        ffl = ffm[:].rearrange("p a b -> p (a b)")
        act(ffl, pb[0][:, 0:512], AF.Exp, [pk[0]], ['ffm'], scale=-1.0)
        act(ffm2[:].rearrange("p a b -> p (a b)"), ffl, AF.Ln, ['ffm'], ['ffm2'], bias=1.0)
        act(ffm2[:].rearrange("p a b -> p (a b)"), ffm2[:].rearrange("p a b -> p (a b)"), AF.Exp, ['ffm2'], ['ffm2'], scale=-1.0)
        tt('dve', ffm[:, 0:2, :], ffm2[:, 0:2, :], pb[0][:, 0:256].rearrange("p (a b) -> p a b", b=128), ALU.mult, ['ffm2', pk[0]], ['ffm'])
        tt('pool', ffm[:, 2:4, :], ffm[:, 2:4, :], ffm2[:, 2:4, :], ALU.mult, ['ffm', 'ffm2'], ['ffm'])
        for g in range(2):
            ts('dve', ffm[:, 2 + g, :], ffm[:, 2 + g, :], prmp[:, PP['oml'] + g:PP['oml'] + g + 1], None, ALU.mult, ALU.bypass, ['ffm', 'prmp'], ['ffm'])
        sB = S.end_capture()
        S.begin_capture()
        make_gate(pb[3][:, 0:256], [pk[3]])
        sC = S.end_capture()
        S.replay(mergeN([sA, sB, sC]))
        tt('dve', qT[:], ffm[:, 0:2, :], Eb[:], ALU.mult, ['ffm', 'Eb'], ['qT'])
        tt('pool', kT[:], ffm[:, 2:4, :], Einv[:], ALU.mult, ['ffm', 'Einv'], ['kT'])
        for g in range(2):
            tr(pbb[2][:, g * 128:(g + 1) * 128], kT[:, g, :], identb[:], ['kT', 'identb'], [pk[2]])
        cp('act', kTM[:], pbb[2][:, 0:256], [pk[2]], ['kTM'])

    def hgrn_be(l, t):
        T = 'S' if t == 16 else 'H'
        if T == 'S':
            load_sample_state('hgrn', l)
        lin_core(l, t, 'hgrn', 2, 64, T, None)
        o_evac()

    def hgrn_be2(l, t):
        T = 'S' if t == 16 else 'H'
        post_norm('rms', None, [], 256, 'hnorm')
        if T == 'S':
            store_sample_state('hgrn', l)

    def gla_fe(l, t):
        T = tile_type(t)
        nblk = 16 if T == 'S' else 1
        proj_FM([(G0, 128), (G0 + 128, 128), (G0 + 512, 16)], 0)
        proj_TM([(G0 + 256, 256), (G0 + 528, 256)], 2)
        cp('act', glT[0:16, :], pb[0][0:16, 256:384], [pk[0]], ['glT'])
        cp('act', vb[:], pb[2][:, 0:256], [pk[2]], ['vb'])
        S.begin_capture()
        mm(pb[3][:, 0:128], glT[0:16, :], wg2[:], True, True, ['glT', 'wg2'], [pk[3]])
        tt('dve', lfTM[:, 0:128], pb[3][:, 0:128], PBr('glab'), ALU.add, [pk[3], 'prmb'], ['lfTM'])
        act(lfTM[:, 0:128], lfTM[:, 0:128], AF.Exp, ['lfTM'], ['lfTM'], scale=-1.0)
        act(lfTM[:, 0:128], lfTM[:, 0:128], AF.Ln, ['lfTM'], ['lfTM'], bias=1.0)
        mm(pb[1][:, 0:128], lfTM[:, 0:128], C('Ui' + T), True, True, ['lfTM', 'cst'], [pk[1]])
        act(Eb[:, 0, :], pb[1][:, 0:128], AF.Exp, [pk[1]], ['Eb'], scale=-1.0 / 16)
        act(Einv[:, 0, :], pb[1][:, 0:128], AF.Exp, [pk[1]], ['Einv'], scale=1.0 / 16)
        blen = 128 // nblk
        cp('pool', Eend[:, 0, 0:nblk], Eb[:, 0, blen - 1:128:blen], ['Eb'], ['Eend'])
        stt(qT[:, 0, :], pb[0][:, 0:128], float(32 ** -0.5), Eb[:, 0, :], ALU.mult, ALU.mult, [pk[0], 'Eb'], ['qT'])
        tt('dve', kT[:, 0, :], pb[0][:, 128:256], Einv[:, 0, :], ALU.mult, [pk[0], 'Einv'], ['kT'])
        tr(pbb[3][:, 0:128], kT[:, 0, :], identb[:], ['kT', 'identb'], [pk[3]])
        cp('act', kTM[:, 0:128], pbb[3][:, 0:128], [pk[3]], ['kTM'])
        sA = S.end_capture()
        S.begin_capture()
        make_gate(pb[2][:, 256:512], [pk[2]])
        sC = S.end_capture()
        S.replay(merge2(sA, sC))

    def gla_be(l, t):
        T = tile_type(t)
        if T == 'S':
            load_sample_state('gla', l)
        lin_core(l, t, 'gla', 1, 32, T, None)
        o_evac()

    def gla_be2(l, t):
        T = tile_type(t)
        post_norm('rms', None, [], 512, 'gnorm')
        if T == 'S':
            store_sample_state('gla', l)

    def rwkv_fe(l, t):
        T = tile_type(t)
        sample = (T == 'S')
        nblk = 16 if sample else 1
        blen = 128 // nblk
        nlev = 3 if sample else 7
        pbc = pbuf[t % 2]; pkey = 'pbuf%d' % (t % 2)
        pbn = pbuf[(t + 1) % 2]; pnkey = 'pbuf%d' % ((t + 1) % 2)
        m = 'rwkv'
        for b0 in range(0, 10, 4):
            grp = RW_GROUPS[b0:b0 + 4]
            bank = (b0 // 4) % 2
            proj_FM([(W0 + c0, w) for (c0, w) in grp], bank)
            n = len(grp)
            cp('act', pbc[:, b0:b0 + n, 1:129], pb[bank][:, 0:n * 128].rearrange("p (a b) -> p a b", b=128), [pk[bank]], [pkey])
        if sample:
            fns = []
            for gi, (c0, w) in enumerate(RW_GROUPS):
                fns.append(lambda e, gi=gi, c0=c0, w=w: e.dma_start(out=shT[0:w, gi, :], in_=st_shift[l][:, c0:c0 + w].rearrange("b f -> f b"), allow_slow_non_contiguous=True))
            S.dma(fns, [], ['shT'])
        elif t == 0:
            S.op('pool', lambda e: e.memset(pbc[:, :, 0:1], 0.0), [pkey], [pkey])
        if sample or t == LASTP:
            ncol = 16 if sample else 1
            for (c0, w, bank) in [(0, 512, 2), (512, 512, 3), (1024, 64, 1)]:
                for kc in range(8):
                    lhs = hT[:, HI[0], kc, 7:128:8] if sample else hT[:, HI[0], kc, 127:128]
                    mm(pb[bank][0:ncol, 0:w], lhs, win[:, kc, c0:c0 + w], kc == 0, kc == 7, ['hT%d' % HI[0], 'win'], [pk[bank]])
                cp('act', qk32[0:ncol, 0:w], pb[bank][0:ncol, 0:w], [pk[bank]], ['qk32'])
                if sample:
                    dma(o_shift_s[l][:, c0:c0 + w], qk32[0:16, 0:w], ['qk32'], [])
                else:
                    dma(o_shift_p[l:l + 1, c0:c0 + w], qk32[0:1, 0:w], ['qk32'], [])
        tt('dve', psf[:], pbc[:, :, 0:128], pbc[:, :, 1:129], ALU.subtract, [pkey], ['psf'])
        if sample:
            tt('pool', psf[:, :, 0:128:8], shT[:], pbc[:, :, 1:129:8], ALU.subtract, [pkey, 'shT', 'psf'], ['psf'])
        tt('dve', psf[:], psf[:], prmp[:, PP['mu']:PP['mu'] + 10].unsqueeze(2).to_broadcast([128, 10, 128]), ALU.mult, ['psf', 'prmp'], ['psf'])
        tt('dve', psf[:], psf[:], pbc[:, :, 1:129], ALU.add, ['psf', pkey], ['psf'])
        if not sample and t < LASTP:
            cp('pool', pbn[:, :, 0:1], pbc[:, :, 128:129], [pkey, pnkey], [pnkey])
        S.begin_capture()
        act(ffm2[0:32, 0, :], psf[0:32, GR_WL, :], AF.Exp, ['psf'], ['ffm2'], scale=2.0)
        act(ffm2[0:32, 0, :], ffm2[0:32, 0, :], AF.Ln, ['ffm2'], ['ffm2'], bias=1.0)
        act(ffm2[0:32, 0, :], ffm2[0:32, 0, :], AF.Exp, ['ffm2'], ['ffm2'], scale=-1.0)
        ts('dve', twl[:], ffm2[0:32, 0, :], -2.0, 1.0, ALU.mult, ALU.add, ['ffm2'], ['twl'])
        mm(pb[3][:, 0:256], twl[:], w2[:], True, True, ['twl', 'w2'], [pk[3]])
        tt('dve', lwn[:], pb[3][:, 0:256], PBr('w0'), ALU.add, [pk[3], 'prmb'], ['lwn'])
        act(lwn[:], lwn[:], AF.Exp, ['lwn'], ['lwn'], scale=-1.0)
        act(lwn[:], lwn[:], AF.Ln, ['lwn'], ['lwn'], bias=1.0)
        act(lwn[:], lwn[:], AF.Exp, ['lwn'], ['lwn'], scale=-1.0, bias=-0.5)
        for g in range(2):
            mm(pb[1][:, g * 128:(g + 1) * 128], lwn[:, g * 128:(g + 1) * 128], C('Ui' + T), True, True, ['lwn', 'cst'], [pk[1]])
        for g in range(2):
            mm(pb[1][:, 256 + g * 128:256 + (g + 1) * 128], lwn[:, g * 128:(g + 1) * 128], C('Us' + T), True, True, ['lwn', 'cst'], [pk[1]])
        act(Eb[:].rearrange("p a b -> p (a b)"), pb[1][:, 0:256], AF.Exp, [pk[1]], ['Eb'], scale=-1.0)
        act(Einv[:].rearrange("p a b -> p (a b)"), pb[1][:, 0:256], AF.Exp, [pk[1]], ['Einv'])
        act(Ecx[:].rearrange("p a b -> p (a b)"), pb[1][:, 256:512], AF.Exp, [pk[1]], ['Ecx'], scale=-1.0)
        cp('pool', Eend[:, :, 0:nblk], Eb[:, :, blen - 1:128:blen], ['Eb'], ['Eend'])
        seg_a = S.end_capture()
        S.begin_capture()
        cp('act', alT[:], psf[0:32, GR_AL, :], ['psf'], ['alT'])
        for g in range(2):
            mm(pb[0][:, g * 128:(g + 1) * 128], a2[:, g * 128:(g + 1) * 128], alT[:], True, True, ['a2', 'alT'], [pk[0]])
        for g in range(2):
            S.op('act', lambda e, g=g: e.activation(out=aT[:, g, :], in_=pb[0][:, g * 128:(g + 1) * 128], func=AF.Exp, bias=prmp[:, PP['na0'] + g:PP['na0'] + g + 1], scale=-1.0),
                 [pk[0], 'prmp'], ['aT'])
        act(aT[:], aT[:], AF.Ln, ['aT'], ['aT'], bias=1.0)
        act(aT[:], aT[:], AF.Exp, ['aT'], ['aT'], scale=-1.0)
        for g in range(2):
            ts('dve', kkT[:, g, :], psf[:, GR_K0 + g, :], prmp[:, PP['kk'] + g:PP['kk'] + g + 1], None, ALU.mult, ALU.bypass, ['psf', 'prmp'], ['kkT'])
        tt('pool', ffm[:, 0:2, :], kkT[:], kkT[:], ALU.mult, ['kkT'], ['ffm'])
        cp('act', rkb[:], ffm[:, 0:2, :], ['ffm'], ['rkb'])
        for g in range(2):
            mm(pb[0][:, 256 + g * 128:256 + (g + 1) * 128], bonesb[:], rkb[:, g, :], True, True, ['bonesb', 'rkb'], [pk[0]])
        ts('dve', ffm[:, 2:4, :], pb[0][:, 256:512].rearrange("p (a b) -> p a b", b=128), 1e-19, None, ALU.max, ALU.bypass, [pk[0]], ['ffm'])
        act(ffm[:, 2:4, :], ffm[:, 2:4, :], AF.Ln, ['ffm'], ['ffm'])
        act(ffm[:, 2:4, :], ffm[:, 2:4, :], AF.Exp, ['ffm'], ['ffm'], scale=-0.5)
        tt('dve', kkT[:], kkT[:], ffm[:, 2:4, :], ALU.mult, ['kkT', 'ffm'], ['kkT'])
        for g in range(2):
            ts('dve', k2T[:, g, :], aT[:, g, :], prmp[:, PP['ka'] + g:PP['ka'] + g + 1], prmp[:, PP['omka'] + g:PP['omka'] + g + 1], ALU.mult, ALU.add, ['aT', 'prmp'], ['k2T'])
        tt('pool', k2T[:], k2T[:], psf[:, GR_K0:GR_K0 + 2, :], ALU.mult, ['k2T', 'psf'], ['k2T'])
        seg_b = S.end_capture()
        S.replay(merge2(seg_a, seg_b))
        S.begin_capture()
        tt('dve', arT[:, :, 0, :], kkT[:], Ecx[:], ALU.mult, ['kkT', 'Ecx'], ['arT'])
        tt('pool', arT[:, :, 1, :], psf[:, GR_R0:GR_R0 + 2, :], Eb[:], ALU.mult, ['psf', 'Eb'], ['arT'])
        tt('dve', ffm2[:, 0:2, :], kkT[:], aT[:], ALU.mult, ['kkT', 'aT'], ['ffm2'])
        tt('dve', btT[:], ffm2[:, 0:2, :], Einv[:], ALU.mult, ['ffm2', 'Einv'], ['btT'])
        tt('pool', ktT[:], k2T[:], Einv[:], ALU.mult, ['k2T', 'Einv'], ['ktT'])
        tt('pool', ffm2[:, 2:4, :], psf[:, GR_R0:GR_R0 + 2, :], k2T[:], ALU.mult, ['psf', 'k2T'], ['ffm2'])
        for g in range(2):
            ts('dve', rkb[:, g, :], ffm2[:, 2 + g, :], prmp[:, PP['rk'] + g:PP['rk'] + g + 1], None, ALU.mult, ALU.bypass, ['ffm2', 'prmp', 'rkb'], ['rkb'])
        sS = S.end_capture()
        S.begin_capture()
        for g in range(2):
            S.op('pe', lambda e, g=g: e.transpose(out=pb[3][:, 256 + g * 128:256 + (g + 1) * 128], in_=psf[:, GR_V0 + g, :], identity=C('ident')), ['psf', 'cst'], [pk[3]])
        cp('act', vTM32[:], pb[3][:, 256:512], [pk[3]], ['vTM32'])
        if l == 0:
            dma(vfs_d[t * 128:(t + 1) * 128, :], vTM32[:], ['vTM32'], ['vfs%d' % t])
        else:
            cp('act', vTb[:], psf[:, GR_V0:GR_V0 + 2, :], ['psf'], ['vTb'])
            for g in range(2):
                mm(pb[3][0:32, 0:128], v1[:, g, :], vTb[:, g, :], g == 0, g == 1, ['v1', 'vTb'], [pk[3]])
            cp('act', l1T[:], pb[3][0:32, 0:128], [pk[3]], ['l1T'])
            mm(pb[3][:, 0:256], l1T[:], v2[:], True, True, ['l1T', 'v2'], [pk[3]])
            tt('dve', vmix[:], pb[3][:, 0:256], PBr('v0'), ALU.add, [pk[3], 'prmb'], ['vmix'])
            sigm(vmix[:], vmix[:], ['vmix'], 'vmix')
            dma(lfTM[:], vfs_d[t * 128:(t + 1) * 128, :], ['vfs%d' % t], ['lfTM'])
            tt('pool', lfTM[:], lfTM[:], vTM32[:], ALU.subtract, ['lfTM', 'vTM32'], ['lfTM'])
            tt('pool', lfTM[:], lfTM[:], vmix[:], ALU.mult, ['lfTM', 'vmix'], ['lfTM'])
            tt('pool', vTM32[:], vTM32[:], lfTM[:], ALU.add, ['vTM32', 'lfTM'], ['vTM32'])
        cp('act', vb[:], vTM32[:], ['vTM32'], ['vb'])
        sV = S.end_capture()
        S.replay(merge2(sS, sV))
        S.begin_capture()
        for g in range(2):
            mm(pb[3][:, 256 + 2 * g:256 + 2 * g + 2], rkb[:, g, :], headindb[:], True, True, ['rkb', 'headindb'], [pk[3]])
        cp('act', bsum[:], pb[3][:, 256:260], [pk[3]], ['bsum'])
        for g in range(2):
            tr(pbb[2][:, g * 128:(g + 1) * 128], ktT[:, g, :], identb[:], ['ktT', 'identb'], [pk[2]])
            tr(pbb[2][:, 256 + g * 128:256 + (g + 1) * 128], btT[:, g, :], identb[:], ['btT', 'identb'], [pk[2]])
        cp('act', kTM[:], pbb[2][:, 0:256], [pk[2]], ['kTM'])
        act(bTM[:], pbb[2][:, 256:512], AF.Copy, [pk[2]], ['bTM'], scale=-1.0)
        sX = S.end_capture()
        S.begin_capture()
        sigm(ffm2[:, 2:4, :], psf[:, GR_G0:GR_G0 + 2, :], ['psf'], 'ffm2', eng2='pool')
        tt('pool', sTm[:, 0:2, :], ffm2[:, 2:4, :], psf[:, GR_G0:GR_G0 + 2, :], ALU.mult, ['ffm2', 'psf'], ['sTm'])
        for g in range(2):
            tr(pbb[2][:, 512 + g * 128:512 + (g + 1) * 128], sTm[:, g, :], identb[:], ['sTm', 'identb'], [pk[2]])
        cp('act', gsil[:], pbb[2][:, 512:768], [pk[2]], ['gsil'])
        sG = S.end_capture()
        S.replay(merge2(sX, sG))

    def rwkv_be(l, t):
        T = tile_type(t)
        sample = (T == 'S')
        nblk = 16 if sample else 1
        blen = 128 // nblk
        nlev = 3 if sample else 7
        m = 'rwkv'
        for g in range(2):
            for hh in range(2):
                po = hh * 64
                cp('act' if hh == 0 else 'dve', arTm[po:po + 64, g, hh, :, :], arT[po:po + 64, g, :, :], ['arT'], ['arTm'])
        for (lhsT_, lkey, dst, dkey, msk, mkey) in [(btT, 'btT', Aab, 'Aab', mAB[T], 'mAB' + T), (ktT, 'ktT', Aak, 'Aak', mAK[T], 'mAK' + T)]:
            for g in range(2):
                for hh in range(2):
                    mm(pb[5][:, hh * 256:(hh + 1) * 256], lhsT_[:, g, :], arTm[:, g, hh, :, :].rearrange("p a b -> p (a b)"), True, True, [lkey, 'arTm'], [pk[5]])
                for hh in range(2):
                    tt('dve', dst[:, 2 * g + hh, :], pb[5][:, hh * 256:(hh + 1) * 256], msk[:], ALU.mult, [pk[5], mkey], [dkey])
        for h in range(4):
            g, hh = h // 2, h % 2
            mm(pb[5][:, h * 128:(h + 1) * 128], arTm[:, g, hh, 0, :], btT[:, g, :], True, True, ['btT', 'arTm'], [pk[5]])
        for h in range(4):
            tt('dve', MiT[0][:, h, :], pb[5][:, h * 128:(h + 1) * 128], mABt[T][:], ALU.mult, [pk[5], 'mABt' + T], ['MiT0'])
        for h in range(4):
            tt('pool', Pm[0][:, h, :], Aab[:, h, 0:128], identb[:], ALU.add, ['Aab', 'identb'], ['Pm0'])
        curP = 0
        for lev in range(1, nlev):
            nxt = lev % 2
            prv = (lev - 1) % 2
            last = (lev == nlev - 1)
            for h in range(4):
                Mprev = Aab[:, h, 0:128] if lev == 1 else Mi[prv][:, h, :]
                mkey = 'Aab' if lev == 1 else 'Mi%d' % prv
                mm(pb[5][:, h * 128:(h + 1) * 128], Mprev, MiT[prv][:, h, :], True, True, [mkey, 'MiT%d' % prv], [pk[5]])
            cp('act', MiT[nxt][:].rearrange("p a b -> p (a b)"), pb[5][:, 0:512], [pk[5]], ['MiT%d' % nxt])
            if not last:
                for h in range(4):
                    Mprev = Aab[:, h, 0:128] if lev == 1 else Mi[prv][:, h, :]
                    mkey = 'Aab' if lev == 1 else 'Mi%d' % prv
                    mm(pb[7][:, h * 128:(h + 1) * 128], MiT[prv][:, h, :], Mprev, True, True, [mkey, 'MiT%d' % prv], [pk[7]])
                cp('act', Mi[nxt][:].rearrange("p a b -> p (a b)"), pb[7][:, 0:512], [pk[7]], ['Mi%d' % nxt])
            for h in range(4):
                mm(pb[6][:, h * 128:(h + 1) * 128], MiT[nxt][:, h, :], Pm[curP][:, h, :], True, True, ['MiT%d' % nxt, 'Pm%d' % curP], [pk[6]])
            tt('dve', Pm[1 - curP][:].rearrange("p a b -> p (a b)"), pb[6][:, 0:512], Pm[curP][:].rearrange("p a b -> p (a b)"), ALU.add, [pk[6], 'Pm%d' % curP], ['Pm%d' % (1 - curP)])
            curP = 1 - curP
        Tm = Pm[curP]; tkey = 'Pm%d' % curP
        if sample:
            d = st_rwkv[l]
            for hp in range(2):
                for half in range(2):
                    b0 = half * 8
                    fns = []
                    for hh in range(2):
                        fns.append(lambda e, hp=hp, hh=hh, b0=b0: e.dma_start(out=Sbdn[hh * 64:(hh + 1) * 64, :, hh * 64:(hh + 1) * 64],
                                                                             in_=d[b0:b0 + 8, 2 * hp + hh].rearrange("b v k -> v b k"), allow_slow_non_contiguous=True))
                    S.dma(fns, [], ['Sbdn'])
                    for bb in range(8):
                        mm(pb[7][:, bb * 64:(bb + 1) * 64], Sbdn[:, bb, :], IIf[:], True, True, ['Sbdn', 'IIf'], [pk[7]])
                    cp('dve', SfS[:, hp, b0:b0 + 8, :], pb[7][:, 0:512].rearrange("p (a b) -> p a b", b=64), [pk[7]], ['SfS'])
            for g in range(2):
                cast_state(True, 2, 64, g, 0, 16, SfS, 'SfS', 'SbS')
            sbkey, Sf, skey = 'SbS', SfS, 'SfS'
        else:
            sbkey, Sf, skey = 'SbP', SfP[m], 'SfP'
        Sbd = sbd_view(sample, 2)
        for g in range(2):
            if sample:
                tt('pool', Qblk[:, 0, 0:nblk, :], arT[:, g, 0, :].unsqueeze(1).to_broadcast([128, nblk, 128]), bindFM['S'][:], ALU.mult, ['arT', 'bindFMS'], ['Qblk'])
            for s in range(nblk):
                lhs = Qblk[:, 0, s, :] if sample else arT[:, g, 0, :]
                mm(pb[6][:, g * 128:(g + 1) * 128], lhs, Sbd[:, g, s, :], s == 0, False, ['Qblk' if sample else 'arT', sbkey], [pk[6]])
            for hh in range(2):
                h = 2 * g + hh
                mm(pb[6][:, h * 64:(h + 1) * 64], Aak[:, h, 0:128], vb[:, h * 64:(h + 1) * 64], False, hh == 1, ['Aak', 'vb'], [pk[6]])
        cp('act', XT[:], pb[6][:, 0:256], [pk[6]], ['XT'])
        for h in range(4):
            mm(pb[6][:, 256 + h * 64:256 + (h + 1) * 64], Tm[:, h, :], XT[:, h * 64:(h + 1) * 64], True, True, [tkey, 'XT'], [pk[6]])
        cp('act', SAT[:], pb[6][:, 256:512], [pk[6]], ['SAT'])
        for g in range(2):
            if sample:
                tt('pool', Qblk[:, 0, 0:nblk, :], arT[:, g, 1, :].unsqueeze(1).to_broadcast([128, nblk, 128]), bindFM['S'][:], ALU.mult, ['arT', 'bindFMS', 'Qblk'], ['Qblk'])
            for s in range(nblk):
                lhs = Qblk[:, 0, s, :] if sample else arT[:, g, 1, :]
                mm(pb[6][:, g * 128:(g + 1) * 128], lhs, Sbd[:, g, s, :], s == 0, False, ['Qblk' if sample else 'arT', sbkey], [pk[6]])
            for hh in range(2):
                h = 2 * g + hh
                mm(pb[6][:, h * 64:(h + 1) * 64], Aab[:, h, 128:256], SAT[:, h * 64:(h + 1) * 64], False, False, ['Aab', 'SAT'], [pk[6]])
                mm(pb[6][:, h * 64:(h + 1) * 64], Aak[:, h, 128:256], vb[:, h * 64:(h + 1) * 64], False, hh == 1, ['Aak', 'vb'], [pk[6]])
        chunks = [(c0, min(nblk, c0 + 8)) for c0 in range(0, nblk, 8)]
        ubank = [7, 5]
        for g in range(2):
            if sample:
                bind = C('bindS')
                tt('pool', Vblk[:], vb[:, g * 128:(g + 1) * 128].rearrange("p (h d) -> p h d", d=64).unsqueeze(2).to_broadcast([128, 2, 16, 64]),
                   bind.unsqueeze(1).unsqueeze(3).to_broadcast([128, 2, 16, 64]), ALU.mult, ['vb', 'cst'], ['Vblk'])
                tt('pool', SAblk, SAT[:, g * 128:(g + 1) * 128].rearrange("p (h d) -> p h d", d=64).unsqueeze(2).to_broadcast([128, 2, 16, 64]),
                   bind.unsqueeze(1).unsqueeze(3).to_broadcast([128, 2, 16, 64]), ALU.mult, ['SAT', 'cst', 'Qblk'], ['Qblk'])
            for ci, (c0, c1) in enumerate(chunks):
                ncol = (c1 - c0) * 64
                bk = ubank[ci]
                for hh in range(2):
                    h = 2 * g + hh
                    po = hh * 64
                    rv = Vblk[:, hh, c0:c1, :].rearrange("p a b -> p (a b)") if sample else vb[:, h * 64:(h + 1) * 64]
                    rs_ = SAblk[:, hh, c0:c1, :].rearrange("p a b -> p (a b)") if sample else SAT[:, h * 64:(h + 1) * 64]
                    mm(pb[bk][po:po + 64, 0:ncol], kTM[:, g * 128 + po:g * 128 + po + 64], rv, True, False, ['kTM', 'vb', 'Vblk'], [pk[bk]])
                    mm(pb[bk][po:po + 64, 0:ncol], bTM[:, g * 128 + po:g * 128 + po + 64], rs_, False, True, ['bTM', 'SAT', 'Qblk'], [pk[bk]])
                if sample:
                    tt('dve', Sf[:, g, c0:c1, :], pb[bk][:, 0:ncol].rearrange("p (a b) -> p a b", b=64), Sf[:, g, c0:c1, :], ALU.add, [pk[bk], skey], [skey])
                    tt('pool', Sf[:, g, c0:c1, :], Sf[:, g, c0:c1, :], Eend[:, g, c0:c1].unsqueeze(2).to_broadcast([128, c1 - c0, 64]), ALU.mult, [skey, 'Eend'], [skey])
                else:
                    tt('dve', stmp[:], pb[bk][:, 0:64], Sf[:, g, 0, :], ALU.add, [pk[bk], skey], ['stmp'])
                    ts('dve', Sf[:, g, 0, :], stmp[:], Eend[:, g, 0:1], None, ALU.mult, ALU.bypass, ['stmp', 'Eend'], [skey])
                    cast_state(False, 2, 64, g, 0, 1, Sf, skey, sbkey)
        if sample:
            do = o_rwkv_s[l]
            for hp in range(2):
                for half in range(2):
                    b0 = half * 8
                    for hh in range(2):
                        cp('pool', Sbdn[hh * 64:(hh + 1) * 64, :, hh * 64:(hh + 1) * 64], SfS[hh * 64:(hh + 1) * 64, hp, b0:b0 + 8, :], ['SfS'], ['Sbdn'])
                    for bb in range(8):
                        mm(pb[7][:, bb * 64:(bb + 1) * 64], Sbdn[:, bb, :], IIf[:], True, True, ['Sbdn', 'IIf'], [pk[7]])
                    cp('act', stg[:], pb[7][:, 0:512].rearrange("p (a b) -> p a b", b=64), [pk[7]], ['stg'])
                    fns = []
                    for hh in range(2):
                        fns.append(lambda e, hp=hp, hh=hh, b0=b0: e.dma_start(out=do[b0:b0 + 8, 2 * hp + hh].rearrange("b v k -> v b k"),
                                                                             in_=stg[hh * 64:(hh + 1) * 64, :, :], allow_slow_non_contiguous=True))
                    S.dma(fns, ['stg'], [])
        o_evac()
        if PIPED[0]:
            tt('pool', bon[:].rearrange("p (h d) -> p h d", d=64), vTM32[:].rearrange("p (h d) -> p h d", d=64), bsum[:].unsqueeze(2).to_broadcast([128, 4, 64]), ALU.mult, ['vTM32', 'bsum'], ['bon'])

    def rwkv_be2(l, t):
        o3 = of32[:].rearrange("p (h d) -> p h d", d=64)
        red(st4[:, 0:4], o3, ['of32'], ['st4'])
        ts('dve', st4[:, 0:4], st4[:, 0:4], 1.0 / 64, None, ALU.mult, ALU.bypass, ['st4'], ['st4'])
        tt('dve', o3, o3, st4[:, 0:4].unsqueeze(2).to_broadcast([128, 4, 64]), ALU.subtract, ['of32', 'st4'], ['of32'])
        tt('pool', osq[:], of32[:], of32[:], ALU.mult, ['of32'], ['osq'])
        red(st4[:, 4:8], osq[:].rearrange("p (h d) -> p h d", d=64), ['osq'], ['st4'])
        rsq(st4[:, 4:8], st4[:, 4:8], ['st4'], 'st4', 1.0 / 64, 64e-5)
        tt('dve', o3, o3, st4[:, 4:8].unsqueeze(2).to_broadcast([128, 4, 64]), ALU.mult, ['of32', 'st4'], ['of32'])
        tt('pool', of32[:], of32[:], PBr('gng'), ALU.mult, ['of32', 'prmb'], ['of32'])
        tt('pool', of32[:], of32[:], PBr('gnb'), ALU.add, ['of32', 'prmb'], ['of32'])
        if PIPED[0]:
            tt('pool', of32[:], of32[:], bon[:], ALU.add, ['of32', 'bon'], ['of32'])
        else:
            tt('pool', osq[:].rearrange("p (h d) -> p h d", d=64), vTM32[:].rearrange("p (h d) -> p h d", d=64), bsum[:].unsqueeze(2).to_broadcast([128, 4, 64]), ALU.mult, ['vTM32', 'bsum'], ['osq'])
            tt('pool', of32[:], of32[:], osq[:], ALU.add, ['of32', 'osq'], ['of32'])
        tt('dve', ofin[:, 768:1024], of32[:], gsil[:], ALU.mult, ['of32', 'gsil'], ['ofin'])

    def store_rwkv_prompt_state(l):
        d = o_rwkv_p[l]
        fns = []
        for h in range(4):
            hp, hh = h // 2, h % 2
            fns.append(lambda e, h=h, hp=hp, hh=hh: e.dma_start(out=d[h].rearrange("v k -> k v"), in_=SfP['rwkv'][hh * 64:(hh + 1) * 64, hp, 0, :], allow_slow_non_contiguous=True))
        S.dma(fns, ['SfP'], [])

    def out_proj(t, mi):
        ak = 'xa0'
        for c in range(2):
            tr(pbb[4][:, c * 128:(c + 1) * 128], ofin[:, mi * 256 + c * 128:mi * 256 + (c + 1) * 128], identb[:], ['ofin', 'identb'], [pk[4]])
        cp('act', oT[:, 0:2, :].rearrange("p a b -> p (a b)"), pbb[4][:, 0:256], [pk[4]], ['oT'])
        dma(xa[:, 0, :], xacc_d[t * 128:(t + 1) * 128, :], ['xacc%d' % t], [ak])
        for half in range(2):
            for c in range(2):
                mm(pb[4][:, 0:512], oT[:, c, :], wout[:, 2 * mi + c, half * 512:(half + 1) * 512], c == 0, c == 1, ['oT', 'wout'], [pk[4]])
            tt('dve', xa[:, 0, half * 512:(half + 1) * 512], pb[4][:, 0:512], xa[:, 0, half * 512:(half + 1) * 512], ALU.add, [pk[4], ak], [ak])
        dma(xacc_d[t * 128:(t + 1) * 128, :], xa[:, 0, :], [ak], ['xacc%d' % t])

    def cross_fe(l, t):
        T = tile_type(t)
        xk = 'xt%d' % (t % 2)
        norm_T(xt[:, t % 2, :], xk, nrow[:, 0, :], 'nrow0')
        for hp in range(2):
            for kc in range(8):
                mm(pb[0][:, hp * 128:(hp + 1) * 128], wq[:, kc, hp * 128:(hp + 1) * 128], hT[:, HI[0], kc, :], kc == 0, kc == 7, ['wq', 'hT%d' % HI[0]], [pk[0]])
        for hp in range(2):
            for hh in range(2):
                po = hh * 64
                act(qxTm[po:po + 64, hp, hh, :], pb[0][po:po + 64, hp * 128:(hp + 1) * 128], AF.Copy, [pk[0]], ['qxTm'], scale=0.125)
        if T == 'P':
            for mc in range(2):
                for h in range(4):
                    hp, hh = h // 2, h % 2
                    mm(pb[1 + mc][:, h * 128:(h + 1) * 128], mkT[:, l, hp, mc * 128:(mc + 1) * 128], qxTm[:, hp, hh, :], True, True, ['mkT', 'qxTm'], [pk[1 + mc]])
        else:
            for b in range(16):
                load_cast(memb[:], cmk[l, b].rearrange("(mc p) f -> p mc f", p=128), 'memb', alt=b % 2)
                for mc in range(2):
                    for hp in range(2):
                        tr(pbb[4][:, (mc * 2 + hp) * 128:(mc * 2 + hp + 1) * 128], memb[:, mc, hp * 128:(hp + 1) * 128], identb[:], ['memb', 'identb'], [pk[4]])
                cp('act', mkTs[:].rearrange("p hp (mc m) -> p mc hp m", mc=2), pbb[4][:, 0:512].rearrange("p (mc hp m) -> p mc hp m", mc=2, hp=2), [pk[4]], ['mkTs'])
                for mc in range(2):
                    for h in range(4):
                        hp, hh = h // 2, h % 2
                        mm(pb[1 + mc][:, h * 128 + 8 * b:h * 128 + 8 * b + 8], mkTs[:, hp, mc * 128:(mc + 1) * 128], qxTm[:, hp, hh, 8 * b:8 * b + 8], True, True, ['mkTs', 'qxTm'], [pk[1 + mc]])
        for mc in range(2):
            act(pT[:, mc, :, :].rearrange("p a b -> p (a b)"), pb[1 + mc][:, 0:512], AF.Exp, [pk[1 + mc]], ['pT'])
        for mc in range(2):
            mm(pb[3][:, 0:512], onesb[:], pT[:, mc, :, :].rearrange("p a b -> p (a b)"), mc == 0, mc == 1, ['onesb', 'pT'], [pk[3]])
        act(rsb[:].rearrange("p a b -> p (a b)"), pb[3][:, 0:512], AF.Ln, [pk[3]], ['rsb'])
        act(rsb[:].rearrange("p a b -> p (a b)"), rsb[:].rearrange("p a b -> p (a b)"), AF.Exp, ['rsb'], ['rsb'], scale=-1.0)

    def cross_be(l, t):
        T = tile_type(t)
        xk = 'xt%d' % (t % 2)
        if T == 'P':
            for h in range(4):
                hp, po = h // 2, (h % 2) * 64
                for mc in range(2):
                    mm(pb[5][po:po + 64, hp * 128:(hp + 1) * 128], mvb[:, l, mc, h * 64:(h + 1) * 64], pT[:, mc, h, :], mc == 0, mc == 1, ['mvb', 'pT'], [pk[5]])
        else:
            for b in range(16):
                load_cast(mvs[:], cmv[l, b].rearrange("(mc p) f -> p mc f", p=128), 'mvs', alt=b % 2)
                for h in range(4):
                    hp, po = h // 2, (h % 2) * 64
                    for mc in range(2):
                        mm(pb[5][po:po + 64, hp * 128 + 8 * b:hp * 128 + 8 * b + 8], mvs[:, mc, h * 64:(h + 1) * 64], pT[:, mc, h, 8 * b:8 * b + 8], mc == 0, mc == 1, ['mvs', 'pT'], [pk[5]])
        for h in range(4):
            hp, po = h // 2, (h % 2) * 64
            tt('dve', oxT[po:po + 64, hp, :], pb[5][po:po + 64, hp * 128:(hp + 1) * 128], rsb[po:po + 64, h, :], ALU.mult, [pk[5], 'rsb'], ['oxT'])
        for half in range(2):
            for kc in range(2):
                mm(pb[6 + half][:, 0:512], oxT[:, kc, :], wo[:, kc, half * 512:(half + 1) * 512], kc == 0, kc == 1, ['oxT', 'wo'], [pk[6 + half]])
            tt('dve', xt[:, t % 2, half * 512:(half + 1) * 512], pb[6 + half][:, 0:512], xt[:, t % 2, half * 512:(half + 1) * 512], ALU.add, [pk[6 + half], xk], [xk])
        if l == 1:
            final_norm(t)
        else:
            dma(xs_d[t * 128:(t + 1) * 128, :], xt[:, t % 2, :], [xk], ['xs%d' % t])

    def final_norm(t):
        xk = 'xt%d' % (t % 2)
        src = xt[:, t % 2, :]
        S.op('dve', lambda e: e.scalar_tensor_tensor(out=ofin[:], in0=src, scalar=1.0, in1=src, op0=ALU.mult, op1=ALU.mult, accum_out=st4[:, 4:5]), [xk], ['ofin', 'st4'])
        rsq(st4[:, 6:7], st4[:, 4:5], ['st4'], 'st4', 1.0 / D, 1e-6)
        ystage = psf[:, 0:8, :].rearrange("p a b -> p (a b)")
        stt(ystage, src, st4[:, 6:7], nrow[:, 1, :], ALU.mult, ALU.mult, [xk, 'st4', 'nrow1', 'psf'], ['psf'])
        dst = y_s if t == 16 else y_p[t * 128:(t + 1) * 128, :]
        dma(dst, ystage, ['psf'], [])

    dma(nrow[:, 1, :], norm_f.partition_broadcast(128), [], ['nrow1'])
    MIX = [('ret', R0, 1024, ret_fe, ret_be, ret_be2), ('hgrn', H0, 1024, hgrn_fe, hgrn_be, hgrn_be2),
           ('gla', G0, 784, gla_fe, gla_be, gla_be2), ('rwkv', W0, 1088, rwkv_fe, rwkv_be, rwkv_be2)]
    PIPE = os.environ.get('NOPIPE', '0') != '1'

    def interleave(lists):
        lists = [x for x in lists if x]
        out = []
        idx = [0] * len(lists)
        tot = sum(len(x) for x in lists)
        while len(out) < tot:
            best = None
            for k, x in enumerate(lists):
                if idx[k] < len(x):
                    frac = idx[k] / float(len(x))
                    if best is None or frac < best[0]:
                        best = (frac, k)
            k = best[1]
            out.append(lists[k][idx[k]]); idx[k] += 1
        return out

    def load_x(l, t):
        if l == 0:
            src = x_s if t == 16 else x_p[t * 128:(t + 1) * 128, :]
            dma(xt[:, t % 2, :], src, [], ['xt%d' % (t % 2)])
        else:
            dma(xt[:, t % 2, :], xs_d[t * 128:(t + 1) * 128, :], ['xs%d' % t], ['xt%d' % (t % 2)])

    for l in range(NLAYERS):
        load_layer_weights(l)
        if l == 1:
            load_wout(1)
        load_params(l)
        for mi, (m, base, width, fn, fn_be, fn_be2) in enumerate(MIX):
            if MIXSEL is not None and m not in MIXSEL:
                continue
            S.barrier()
            WB[0] = base
            for kc in range(8):
                load_cast(win[:, kc, 0:width], w_in[l, kc * 128:(kc + 1) * 128, base:base + width], 'win', alt=kc % 2)
            if mi == 0:
                dma(nrow[:, 0, :], norm_mix[l].partition_broadcast(128), [], ['nrow0'])
            zero_prompt_states()
            def load_h(ti_):
                tt_ = TILES[ti_]
                dma(hT[:, ti_ % 2, :, :].rearrange("p a b -> p (a b)"), hts_d[tt_ * 128:(tt_ + 1) * 128, :], ['hts%d' % tt_], ['hT%d' % (ti_ % 2)])
            if mi == 0:
                load_x(l, TILES[0])
            else:
                load_h(0)
            pl = {'be1': None, 'be1_t': None, 'be2_prev': None, 'be2_old': None}

            def finish_tile(t_):
                if t_ == LASTP:
                    if m == 'rwkv':
                        store_rwkv_prompt_state(l)
                    else:
                        store_prompt_state(m, l)

            def drain():
                if pl['be1'] is None and pl['be2_old'] is None and pl['be2_prev'] is None:
                    return
                S.replay(interleave([pl['be2_old'], pl['be1']]))
                if pl['be1'] is not None:
                    finish_tile(pl['be1_t'])
                if pl['be2_prev'] is not None:
                    S.replay(pl['be2_prev'])
                pl['be1'] = pl['be2_prev'] = pl['be2_old'] = None
                S.barrier()
            for ti, t in enumerate(TILES):
                HI[0] = ti % 2
                if mi == 0:
                    norm_T(xt[:, t % 2, :], 'xt%d' % (t % 2), nrow[:, 0, :], 'nrow0')
                    dma(hts_d[t * 128:(t + 1) * 128, :], hT[:, HI[0], :, :].rearrange("p a b -> p (a b)"), ['hT%d' % HI[0]], ['hts%d' % t])
                    dma(xacc_d[t * 128:(t + 1) * 128, :], xt[:, t % 2, :], ['xt%d' % (t % 2)], ['xacc%d' % t])
                    if ti + 1 < len(TILES):
                        load_x(l, TILES[ti + 1])
                else:
                    if ti + 1 < len(TILES):
                        load_h(ti + 1)
                piped = PIPE and t != 16
                if not piped:
                    drain()
                    S.par[0] = 0; PO[0] = 0; PG[0] = 0; PIPED[0] = False
                    fn(l, t); fn_be(l, t); fn_be2(l, t); out_proj(t, mi)
                    finish_tile(t)
                else:
                    PIPED[0] = True
                    S.par[0] = (ti % 2) * (2 if mi == 0 else 1); PO[0] = ti % 2; PG[0] = ti % 3
                    S.begin_capture(); fn(l, t); fe_items = S.end_capture()
                    S.replay(interleave([pl['be2_old'], pl['be1'], fe_items]))
                    if pl['be1'] is not None:
                        finish_tile(pl['be1_t'])
                    pl['be2_old'] = pl['be2_prev']
                    S.begin_capture(); fn_be(l, t); pl['be1'] = S.end_capture(); pl['be1_t'] = t
                    S.begin_capture(); fn_be2(l, t); out_proj(t, mi); pl['be2_prev'] = S.end_capture()
            drain()
            S.par[0] = 0; PO[0] = 0; PG[0] = 0; PIPED[0] = False
        if MIXSEL is not None and 'cross' not in MIXSEL:
            continue
        S.barrier()
        dma(nrow[:, 0, :], norm_x[l].partition_broadcast(128), [], ['nrow0'])
        def load_xacc(t):
            dma(xt[:, t % 2, :], xacc_d[t * 128:(t + 1) * 128, :], ['xacc%d' % t], ['xt%d' % (t % 2)])
        HI[0] = 0
        load_xacc(TILES[0])
        pendx = None
        for ti, t in enumerate(TILES):
            piped = PIPE and t != 16
            if not piped:
                if pendx is not None:
                    S.replay(pendx); pendx = None
                    S.barrier()
                if ti + 1 < len(TILES):
                    load_xacc(TILES[ti + 1])
                PX[0] = 0
                cross_fe(l, t); cross_be(l, t)
            else:
                PX[0] = ti % 2
                S.begin_capture(); cross_fe(l, t); fe_items = S.end_capture()
                S.replay(interleave([pendx, fe_items]))
                if ti + 1 < len(TILES):
                    load_xacc(TILES[ti + 1])
                S.begin_capture(); cross_be(l, t); pendx = S.end_capture()
        if pendx is not None:
            S.replay(pendx); pendx = None
        PX[0] = 0
    print('recorded ops', S.nrec)
    S.final_wait_all('sp')
    S.emit()
    return nc


_NC_CACHE = {}


def kernel(**inputs):
    inp = {k: np.ascontiguousarray(np.asarray(v, dtype=np.float32)) for k, v in inputs.items()}
    if 'nc' not in _NC_CACHE:
        _NC_CACHE['nc'] = build_nc()
    nc = _NC_CACHE['nc']
    cst, bH, bS = make_consts()
    shared = {}
    for k in ['norm_mix', 'w_in', 'hgrn_lb_logits', 'hgrn_norm', 'gla_w_gate2', 'gla_b_gate', 'gla_norm', 'rwkv_mu', 'rwkv_w0',
              'rwkv_w2', 'rwkv_a0', 'rwkv_a2', 'rwkv_v0', 'rwkv_v1', 'rwkv_v2', 'rwkv_k_k', 'rwkv_k_a', 'rwkv_gn_g', 'rwkv_gn_b',
              'w_out', 'norm_x', 'wq_x', 'wo_x', 'norm_mem', 'wk_x', 'wv_x', 'norm_f']:
        shared[k] = inp[k]
    shared['rwkv_r_k'] = inp['rwkv_r_k'].reshape(2, 256)
    shared['cst'] = cst; shared['bindH_row'] = bH; shared['bindS_row'] = bS
    in_maps = []
    for c in range(8):
        m = dict(shared)
        m['x_p'] = inp['x_prompt'][c]
        m['x_s'] = inp['x_sample'][16 * c:16 * c + 16].reshape(128, D)
        m['mem_p'] = inp['mem_prompt'][c]
        sl = slice(16 * c, 16 * c + 16)
        m['st_ret'] = np.ascontiguousarray(inp['state_ret'][:, sl])
        m['st_hgrn'] = np.ascontiguousarray(inp['state_hgrn'][:, sl])
        m['st_gla'] = np.ascontiguousarray(inp['state_gla'][:, sl])
        m['st_rwkv'] = np.ascontiguousarray(inp['state_rwkv'][:, sl])
        m['st_shift'] = np.ascontiguousarray(inp['state_rwkv_shift'][:, sl])
        m['cmk'] = np.ascontiguousarray(inp['cache_mem_k'][:, sl].reshape(2, 16, 256, 256))
        m['cmv'] = np.ascontiguousarray(inp['cache_mem_v'][:, sl].reshape(2, 16, 256, 256))
        in_maps.append(m)
    res = run_bass_kernel_spmd(nc, in_maps, core_ids=list(range(8)))
    R = res.results

    def cat(name, axis):
        return np.concatenate([np.asarray(R[c][name], np.float32) for c in range(8)], axis=axis)

    def stackp(name):
        return np.stack([np.asarray(R[c][name], np.float32) for c in range(8)], axis=1)

    y_prompt = np.stack([np.asarray(R[c]['y_p'], np.float32) for c in range(8)], axis=0)
    y_sample = cat('y_s', 0).reshape(128, 8, D)
    outs = (y_prompt, y_sample,
            stackp('ret_p'), cat('ret_s', 1), stackp('hgrn_p'), cat('hgrn_s', 1), stackp('gla_p'), cat('gla_s', 1),
            stackp('rwkv_p'), cat('rwkv_s', 1), stackp('shift_p'), cat('shift_s', 1),
            stackp('mk_p').reshape(2, 8, 256, 4, 64), stackp('mv_p').reshape(2, 8, 256, 4, 64))
    return outs
```
